# Optimizing a Trainium2 kernel written in Bass

```python
import math
import jax, jax.numpy as jnp
from jax import lax
import numpy as np

D_MODEL = 1024
BATCH = 8
SEQ = 8192
DEPTH = 4
DEC_BATCH = 4
DEC_SEQ = 4096
PAST_LEN = 128

HEAD_DIM = 64
GRID_W = 64
Q_BLOCK = 128
EPS = 1e-6
A_HEADS = 8
A_KV_HEADS = 2
AXIAL_THETA = 10000.0
B_HEADS = 4
B_V_DIM = 2 * HEAD_DIM
C_HEADS = 16
C_KV_HEADS = 4
WINDOW = 128
ROPE_THETA = 500000.0
ROPE_DIM = HEAD_DIM // 4
FFN_HIDDEN = -(-8 * D_MODEL // (3 * 256)) * 256

A_Q = A_HEADS * HEAD_DIM
A_KV = A_KV_HEADS * HEAD_DIM
B_QK = B_HEADS * 2 * HEAD_DIM
B_V = B_HEADS * B_V_DIM
EVEN_IN = A_Q + 2 * A_KV + 2 * B_QK + B_V
EVEN_OUT = A_Q + B_V
C_Q = C_HEADS * HEAD_DIM
C_KV = C_KV_HEADS * HEAD_DIM
ODD_IN = C_Q + 2 * C_KV
ODD_OUT = C_Q
N_EVEN = (DEPTH + 1) // 2
N_ODD = DEPTH // 2

kernel_name = "hybrid_axial_diff_window_encoder"


def rms_norm(x, g):
    xf = x.astype(jnp.float32)
    y = xf * lax.rsqrt(jnp.mean(xf * xf, axis=-1, keepdims=True) + EPS)
    return (y * g.astype(jnp.float32)).astype(x.dtype)


def rms_norm_nogain(x):
    xf = x.astype(jnp.float32)
    return (xf * lax.rsqrt(jnp.mean(xf * xf, axis=-1, keepdims=True) + EPS)).astype(x.dtype)


def rope_table(pos, dim, theta):
    inv = theta ** (-jnp.arange(0, dim, 2, dtype=jnp.float32) / dim)
    ang = pos.astype(jnp.float32)[:, None] * inv[None, :]
    return jnp.cos(ang), jnp.sin(ang)


def rotate(x, cos, sin):
    xf = x.astype(jnp.float32)
    half = xf.shape[-1] // 2
    x1, x2 = xf[..., :half], xf[..., half:]
    c, s = cos[:, None, :], sin[:, None, :]
    return jnp.concatenate([x1 * c - x2 * s, x2 * c + x1 * s], axis=-1).astype(x.dtype)


def partial_rope(x, cos, sin):
    return jnp.concatenate([rotate(x[..., :ROPE_DIM], cos, sin), x[..., ROPE_DIM:]], axis=-1)


def axial_rope(x, row_cos, row_sin, col_cos, col_sin):
    half = HEAD_DIM // 2
    return jnp.concatenate([rotate(x[..., :half], row_cos, row_sin),
                            rotate(x[..., half:], col_cos, col_sin)], axis=-1)


def to_blocks(q):
    B, S = q.shape[0], q.shape[1]
    qb = q.reshape((B, S // Q_BLOCK, Q_BLOCK) + q.shape[2:])
    return jnp.moveaxis(qb, 1, 0)


def from_blocks(o):
    o = jnp.moveaxis(o, 0, 1)
    return o.reshape((o.shape[0], o.shape[1] * o.shape[2]) + o.shape[3:])


def dense_gqa_blocks(q, k, v):
    B, S, Hq, d = q.shape
    Hkv = k.shape[2]
    G = Hq // Hkv
    scale = d ** -0.5
    qb = to_blocks(q.reshape(B, S, Hkv, G, d))

    def one_block(qi):
        s = jnp.einsum('bqhgd,bkhd->bhgqk', qi, k).astype(jnp.float32) * scale
        p = jax.nn.softmax(s, axis=-1).astype(v.dtype)
        return jnp.einsum('bhgqk,bkhd->bqhgd', p, v)

    return from_blocks(lax.map(one_block, qb)).reshape(B, S, Hq, d)


def diff_attention_blocks(qs, ks, v, lam):
    d = qs.shape[-1]
    scale = d ** -0.5
    qb = to_blocks(qs)

    def one_block(qi):
        s = jnp.einsum('bqmhd,bkmhd->bmhqk', qi, ks).astype(jnp.float32) * scale
        p = jax.nn.softmax(s, axis=-1)
        a = (p[:, 0] - lam * p[:, 1]).astype(v.dtype)
        return jnp.einsum('bhqk,bkhe->bqhe', a, v)

    return from_blocks(lax.map(one_block, qb))


def window_gqa_sink(q, k, v, sink):
    B, S, Hq, d = q.shape
    Hkv = k.shape[2]
    G = Hq // Hkv
    nb = S // Q_BLOCK
    KB = Q_BLOCK + 2 * WINDOW
    scale = d ** -0.5
    kp = jnp.pad(k, ((0, 0), (WINDOW, WINDOW), (0, 0), (0, 0)))
    vp = jnp.pad(v, ((0, 0), (WINDOW, WINDOW), (0, 0), (0, 0)))
    qb = to_blocks(q.reshape(B, S, Hkv, G, d))
    rel = jnp.arange(KB)[None, :] - WINDOW - jnp.arange(Q_BLOCK)[:, None]
    band = jnp.abs(rel) <= WINDOW
    sink_g = sink.reshape(Hkv, G).astype(jnp.float32)

    def one_block(args):
        i, qi = args
        start = i * Q_BLOCK
        ki = lax.dynamic_slice_in_dim(kp, start, KB, axis=1)
        vi = lax.dynamic_slice_in_dim(vp, start, KB, axis=1)
        key_pos = start - WINDOW + jnp.arange(KB)
        valid = band & ((key_pos >= 0) & (key_pos < S))[None, :]
        s = jnp.einsum('bqhgd,bkhd->bhgqk', qi, ki).astype(jnp.float32) * scale
        s = jnp.where(valid, s, -jnp.inf)
        sink_col = jnp.broadcast_to(sink_g[None, :, :, None, None], s.shape[:-1] + (1,))
        p = jax.nn.softmax(jnp.concatenate([s, sink_col], axis=-1), axis=-1)[..., :KB]
        return jnp.einsum('bhgqk,bkhd->bqhgd', p.astype(v.dtype), vi)

    o = lax.map(one_block, (jnp.arange(nb), qb))
    return from_blocks(o).reshape(B, S, Hq, d)


def even_mixer(h, w_in, w_out, qk_norm_a, diff_lambda, lam_init, axial, rope):
    B, S, _ = h.shape
    proj = h @ w_in
    qa, ka, va, qbd, kbd, vbd = jnp.split(
        proj, [A_Q, A_Q + A_KV, A_Q + 2 * A_KV, A_Q + 2 * A_KV + B_QK, A_Q + 2 * A_KV + 2 * B_QK], axis=-1)
    qa = rms_norm(qa.reshape(B, S, A_HEADS, HEAD_DIM), qk_norm_a[0])
    ka = rms_norm(ka.reshape(B, S, A_KV_HEADS, HEAD_DIM), qk_norm_a[1])
    va = va.reshape(B, S, A_KV_HEADS, HEAD_DIM)
    qa = axial_rope(qa, *axial)
    ka = axial_rope(ka, *axial)
    oa = dense_gqa_blocks(qa, ka, va).reshape(B, S, A_Q)
    cos, sin = rope
    qbd = partial_rope(qbd.reshape(B, S, B_HEADS * 2, HEAD_DIM), cos, sin)
    kbd = partial_rope(kbd.reshape(B, S, B_HEADS * 2, HEAD_DIM), cos, sin)
    qs = qbd.reshape(B, S, B_HEADS, 2, HEAD_DIM).swapaxes(2, 3)
    ks = kbd.reshape(B, S, B_HEADS, 2, HEAD_DIM).swapaxes(2, 3)
    vbd = vbd.reshape(B, S, B_HEADS, B_V_DIM)
    lf = diff_lambda.astype(jnp.float32)
    lam = jnp.exp(jnp.sum(lf[0] * lf[1])) - jnp.exp(jnp.sum(lf[2] * lf[3])) + lam_init
    ob = diff_attention_blocks(qs, ks, vbd, lam)
    ob = (rms_norm_nogain(ob) * (1.0 - lam_init)).reshape(B, S, B_V)
    return jnp.concatenate([oa, ob], axis=-1) @ w_out


def odd_mixer(h, w_in, w_out, sink, rope):
    B, S, _ = h.shape
    cos, sin = rope
    q, k, v = jnp.split(h @ w_in, [C_Q, C_Q + C_KV], axis=-1)
    q = partial_rope(q.reshape(B, S, C_HEADS, HEAD_DIM), cos, sin)
    k = partial_rope(k.reshape(B, S, C_KV_HEADS, HEAD_DIM), cos, sin)
    v = v.reshape(B, S, C_KV_HEADS, HEAD_DIM)
    o = window_gqa_sink(q, k, v, sink).reshape(B, S, ODD_OUT)
    return o @ w_out


def swiglu(h, w_gate_up, w_down):
    g, u = jnp.split(h @ w_gate_up, 2, axis=-1)
    return (jax.nn.silu(g) * u) @ w_down


def trunk(x, w_in_even, w_out_even, qk_norm_a, diff_lambda, w_in_odd, w_out_odd, sink_c,
          w_gate_up, w_down, norm_mix_pre, norm_mix_post, norm_ffn_pre, norm_ffn_post):
    S = x.shape[1]
    rows = S // GRID_W
    t_row = jnp.broadcast_to(jnp.arange(rows)[:, None], (rows, GRID_W)).reshape(-1)
    t_col = jnp.broadcast_to(jnp.arange(GRID_W)[None, :], (rows, GRID_W)).reshape(-1)
    row_cos, row_sin = rope_table(t_row, HEAD_DIM // 2, AXIAL_THETA)
    col_cos, col_sin = rope_table(t_col, HEAD_DIM // 2, AXIAL_THETA)
    axial = (row_cos, row_sin, col_cos, col_sin)
    rope = rope_table(jnp.arange(S), ROPE_DIM, ROPE_THETA)
    for l in range(DEPTH):
        h = rms_norm(x, norm_mix_pre[l])
        if l % 2 == 0:
            lam_init = 0.8 - 0.6 * math.exp(-0.3 * l)
            m = even_mixer(h, w_in_even[l // 2], w_out_even[l // 2], qk_norm_a[l // 2],
                           diff_lambda[l // 2], lam_init, axial, rope)
        else:
            m = odd_mixer(h, w_in_odd[l // 2], w_out_odd[l // 2], sink_c[l // 2], rope)
        x = x + rms_norm(m, norm_mix_post[l])
        h = rms_norm(x, norm_ffn_pre[l])
        x = x + rms_norm(swiglu(h, w_gate_up[l], w_down[l]), norm_ffn_post[l])
    return x


def setup_inputs(seed: int = 0) -> dict:
    key = jax.random.key(seed)
    ks = jax.random.split(key, 16)
    f32 = jnp.float32

    def nrm(k, shape, scale):
        return jax.random.normal(k, shape, f32) * scale

    def gain(k, shape):
        return jnp.ones(shape, f32) + 0.05 * jax.random.normal(k, shape, f32)

    return {
        "x_prompt": nrm(ks[0], (BATCH, SEQ, D_MODEL), 1.0),
        "x_sample": nrm(ks[1], (DEC_BATCH, DEC_SEQ, D_MODEL), 1.0),
        "w_in_even": nrm(ks[2], (N_EVEN, D_MODEL, EVEN_IN), D_MODEL ** -0.5),
        "w_out_even": nrm(ks[3], (N_EVEN, EVEN_OUT, D_MODEL), EVEN_OUT ** -0.5),
        "qk_norm_a": gain(ks[4], (N_EVEN, 2, HEAD_DIM)),
        "diff_lambda": nrm(ks[5], (N_EVEN, 4, HEAD_DIM), 0.1),
        "w_in_odd": nrm(ks[6], (N_ODD, D_MODEL, ODD_IN), D_MODEL ** -0.5),
        "w_out_odd": nrm(ks[7], (N_ODD, ODD_OUT, D_MODEL), ODD_OUT ** -0.5),
        "sink_c": nrm(ks[8], (N_ODD, C_HEADS), 0.5),
        "w_gate_up": nrm(ks[9], (DEPTH, D_MODEL, 2 * FFN_HIDDEN), D_MODEL ** -0.5),
        "w_down": nrm(ks[10], (DEPTH, FFN_HIDDEN, D_MODEL), FFN_HIDDEN ** -0.5),
        "norm_mix_pre": gain(ks[11], (DEPTH, D_MODEL)),
        "norm_mix_post": gain(ks[12], (DEPTH, D_MODEL)),
        "norm_ffn_pre": gain(ks[13], (DEPTH, D_MODEL)),
        "norm_ffn_post": gain(ks[14], (DEPTH, D_MODEL)),
    }


def reference(x_prompt, x_sample, w_in_even, w_out_even, qk_norm_a, diff_lambda, w_in_odd,
              w_out_odd, sink_c, w_gate_up, w_down, norm_mix_pre, norm_mix_post, norm_ffn_pre,
              norm_ffn_post):
    y_prompt = trunk(x_prompt, w_in_even, w_out_even, qk_norm_a, diff_lambda, w_in_odd, w_out_odd,
                     sink_c, w_gate_up, w_down, norm_mix_pre, norm_mix_post, norm_ffn_pre, norm_ffn_post)
    y_sample = trunk(x_sample, w_in_even, w_out_even, qk_norm_a, diff_lambda, w_in_odd, w_out_odd,
                     sink_c, w_gate_up, w_down, norm_mix_pre, norm_mix_post, norm_ffn_pre, norm_ffn_post)
    return (y_prompt, y_sample)
```

```python
import math
from contextlib import ExitStack

import numpy as np
import concourse.bass as bass
import concourse.mybir as mybir
from concourse.bass_utils import run_bass_kernel_spmd

F32 = mybir.dt.float32
BF16 = mybir.dt.bfloat16
ALU = mybir.AluOpType
AF = mybir.ActivationFunctionType
AX = mybir.AxisListType

D = 1024
DEPTH = 4
HD = 64
EPS = 1e-6
FFN_H = 2816
NJH = FFN_H // 128
EVEN_IN = 2304
ODD_IN = 1536
SCALE = HD ** -0.5
NCORES = 8


def lam_init_of(l):
    return 0.8 - 0.6 * math.exp(-0.3 * l)


class Buf:
    __slots__ = ("t", "lw", "rd", "dsem")

    def __init__(self, t, dsem=None):
        self.t = t
        self.lw = None
        self.rd = {}
        self.dsem = dsem

    def __getitem__(self, k):
        return self.t[k]


def view(ap, dims):
    return bass.AP(ap.tensor, ap.offset, [list(ap.ap[0])] + [list(d) for d in dims])


class TR:
    def __init__(self, nc, es, n_dma_sems=48):
        self.nc = nc
        self.E = {"pe": nc.tensor, "act": nc.scalar, "dve": nc.vector, "pool": nc.gpsimd, "sp": nc.sync}
        self.sems = {}
        self.cnt = {}
        for e in ("pe", "act", "dve", "pool"):
            self.sems[e] = es.enter_context(nc.semaphore("s_" + e))
            self.cnt[e] = 0
        self.free_dsems = []
        for i in range(n_dma_sems):
            n = "d%d" % i
            self.sems[n] = es.enter_context(nc.semaphore("s_" + n))
            self.cnt[n] = 0
            self.free_dsems.append(n)
        self.waited = {e: {} for e in self.E}
        self.inflight = {}
        import os
        self.max_inflight = int(os.environ.get("KINFLIGHT", "3"))
        self.nopool = bool(os.environ.get("KNOPOOL"))

    def _wait(self, eng, toks):
        E = self.E[eng]
        w = self.waited[eng]
        for s, v in toks.items():
            if eng == "pe" and s == "pe":
                continue
            if w.get(s, 0) < v:
                E.wait_ge(self.sems[s], v)
                w[s] = v

    @staticmethod
    def _collect(reads, writes):
        toks = {}
        for b in reads:
            if b.lw is not None:
                s, v = b.lw
                if toks.get(s, 0) < v:
                    toks[s] = v
        for b in writes:
            if b.lw is not None:
                s, v = b.lw
                if toks.get(s, 0) < v:
                    toks[s] = v
            for s, v in b.rd.items():
                if toks.get(s, 0) < v:
                    toks[s] = v
        return toks

    def op(self, eng, fn, reads=(), writes=()):
        if eng == "pool" and self.nopool:
            eng = "dve"
        self._wait(eng, self._collect(reads, writes))
        inst = fn(self.E[eng])
        self.cnt[eng] += 1
        inst.then_inc(self.sems[eng], 1)
        tok = (eng, self.cnt[eng])
        for b in writes:
            b.lw = tok
            b.rd = {}
        for b in reads:
            if b.rd.get(eng, 0) < tok[1]:
                b.rd[eng] = tok[1]
        return tok

    def dma(self, out_ap, in_ap, buf, load, q="sp"):
        if load:
            toks = self._collect((), (buf,))
        else:
            toks = self._collect((buf,), ())
        fl = self.inflight.setdefault(q, [])
        while len(fl) >= self.max_inflight:
            s0, v0 = fl.pop(0)
            if toks.get(s0, 0) < v0:
                toks[s0] = v0
        self._wait(q, toks)
        s = buf.dsem
        inst = self.E[q].dma_start(out=out_ap, in_=in_ap)
        self.cnt[s] += 16
        inst.then_inc(self.sems[s], 16)
        tok = (s, self.cnt[s])
        fl.append(tok)
        if load:
            buf.lw = tok
            buf.rd = {}
        else:
            buf.rd[s] = tok[1]
        return tok

    def barrier(self):
        for e in self.E:
            self._wait(e, dict(self.cnt))


_UID = [0]


class Phase:
    def __init__(self, tr, nc):
        self.tr = tr
        self.nc = nc
        self.es = ExitStack()
        self.dsems = []
        self.k = 0

    def sb(self, shape, dt, dma=False, name=None):
        _UID[0] += 1
        t = self.es.enter_context(self.nc.sbuf_tensor("%s_%d" % (name or "sb", _UID[0]), list(shape), dt))
        ds = None
        if dma:
            ds = self.tr.free_dsems.pop()
            self.dsems.append(ds)
        return Buf(t, ds)

    def ps(self, shape, dt, name=None):
        _UID[0] += 1
        t = self.es.enter_context(self.nc.psum_tensor("%s_%d" % (name or "ps", _UID[0]), list(shape), dt))
        return Buf(t)

    def close(self):
        self.tr.barrier()
        self.es.close()
        self.tr.free_dsems.extend(self.dsems)
        self.tr.free_dsems.sort(key=lambda n: int(n[1:]))
        self.dsems = []


def build_program(Sp, Ss, depth=DEPTH):
    T = Sp + Ss
    seqs = [(0, Sp), (Sp, Ss)]
    nc = bass.Bass("TRN2", target_bir_lowering=False)

    def din(name, shape, dt=F32):
        return nc.dram_tensor(name, list(shape), dt, kind="ExternalInput").ap()

    xp = din("xp", [Sp, D])
    xs = din("xs", [Ss, D])
    w_in_even = din("w_in_even", [2, D, EVEN_IN])
    w_out_even = din("w_out_even", [2, D, D])
    w_in_odd = din("w_in_odd", [2, D, ODD_IN])
    w_out_odd = din("w_out_odd", [2, D, D])
    w_gate_up = din("w_gate_up", [4, D, 2 * FFN_H])
    w_down = din("w_down", [4, FFN_H, D])
    gpreT_mix = din("gpreT_mix", [4, 128, 8])
    gpreT_ffn = din("gpreT_ffn", [4, 128, 8])
    gpost_mix = din("gpost_mix", [4, 128, D])
    gpost_ffn = din("gpost_ffn", [4, 128, D])
    gqk = din("gqk", [2, 128, 640])
    dlam = din("dlam", [2, 128, 256])
    sinkc = din("sinkc", [2, 16])
    tab = din("tab", [max(Sp, Ss), 160])
    maskc = din("maskc", [128, 6 * 512])
    identc = din("identc", [128, 128])
    yp = nc.dram_tensor("yp", [Sp, D], F32, kind="ExternalOutput").ap()
    ys = nc.dram_tensor("ys", [Ss, D], F32, kind="ExternalOutput").ap()
    QTd = nc.dram_tensor("QTd", [8, 128, T], BF16).ap()
    KTd = nc.dram_tensor("KTd", [5, 128, T], BF16).ap()
    Vd = nc.dram_tensor("Vd", [T, 640], BF16).ap()
    OTd = nc.dram_tensor("OTd", [8, 128, T], BF16).ap()
    Y1d = nc.dram_tensor("Y1d", [T, D], F32).ap()

    def x_rows(src_is_input, t0, n):
        if t0 < Sp:
            base = xp if src_is_input else yp
            return base[t0:t0 + n, :]
        base = xs if src_is_input else ys
        return base[t0 - Sp:t0 - Sp + n, :]

    with ExitStack() as es:
        tr = TR(nc, es)
        gph = Phase(tr, nc)
        ident_f = gph.sb([128, 128], F32, dma=True)
        ident = gph.sb([128, 128], BF16)
        ones_f = gph.sb([128, 128], F32)
        ones_b = gph.sb([128, 128], BF16)
        mask_f = gph.sb([128, 6 * 512], F32, dma=True)
        mask_b = gph.sb([128, 6 * 512], BF16)
        tr.dma(ident_f[:], identc[:, :], ident_f, True)
        tr.op("dve", lambda e: e.tensor_copy(out=ident[:], in_=ident_f[:]), [ident_f], [ident])
        tr.op("pool", lambda e: e.memset(ones_f[:], 1.0), [], [ones_f])
        tr.op("pool", lambda e: e.memset(ones_b[:], 1.0), [], [ones_b])
        eps_t = gph.sb([128, 1], F32)
        tr.op("pool", lambda e: e.memset(eps_t[:], EPS), [], [eps_t])
        tr.dma(mask_f[:], maskc[:, :], mask_f, True)
        tr.op("dve", lambda e: e.tensor_copy(out=mask_b[:], in_=mask_f[:]), [mask_f], [mask_b])

        def load_weight(ph, Wb, Wd, C, N, gT=None, nch=1408, dstride=None, doff=0):
            if dstride is None:
                dstride = N
            sub = Phase(tr, nc)
            stg = [sub.sb([128, nch], F32, dma=True, name="wst") for _ in range(3)]
            k = 0
            for c in range(C):
                for n0 in range(0, N, nch):
                    w = min(nch, N - n0)
                    st = stg[k % 3]
                    tr.dma(st[:, 0:w], Wd[c * 128:(c + 1) * 128, n0:n0 + w], st, True)
                    dst = Wb[:, c * dstride + doff + n0:c * dstride + doff + n0 + w]
                    if gT is not None:
                        if k % 2 == 0:
                            tr.op("dve", lambda e: e.tensor_scalar(
                                out=dst, in0=st[:, 0:w], scalar1=gT[:, c:c + 1], scalar2=None, op0=ALU.mult),
                                [st, gT], [Wb])
                        else:
                            tr.op("act", lambda e: e.activation(out=dst, in_=st[:, 0:w], func=AF.Copy,
                                                                scale=gT[:, c:c + 1]), [st, gT], [Wb])
                    else:
                        eng = ("dve", "pool")[k % 2]
                        tr.op(eng, lambda e: e.tensor_copy(out=dst, in_=st[:, 0:w]), [st], [Wb])
                    k += 1
            sub.close()

        def rstd_from_ss(eng_ss_buf, ss_ap, out_buf, out_ap, inv_n):
            tr.op("act", lambda e: e.activation(out=out_ap, in_=ss_ap, func=AF.Ln, scale=inv_n, bias=eps_t[:, 0:1]),
                  [eng_ss_buf, eps_t], [out_buf])
            tr.op("act", lambda e: e.activation(out=out_ap, in_=out_ap, func=AF.Exp, scale=-0.5),
                  [out_buf], [out_buf])

        def norm_transpose(xt, j, ss, rstd, junk, hb, psT, hT, TT):
            xj = xt[:, j * D:(j + 1) * D]
            tr.op("act", lambda e: e.activation(out=junk[:], in_=xj, func=AF.Square, accum_out=ss[:, j:j + 1]),
                  [xt], [junk, ss])
            rstd_from_ss(ss, ss[:, j:j + 1], rstd, rstd[:, j:j + 1], 1.0 / D)
            tr.op("dve", lambda e: e.tensor_scalar(out=hb[:], in0=xj, scalar1=rstd[:, j:j + 1], scalar2=None,
                                                   op0=ALU.mult), [xt, rstd], [hb])

            def tps(e):
                r = None
                for c in range(8):
                    r = e.transpose(psT[:, c * 128:(c + 1) * 128], hb[:, c * 128:(c + 1) * 128], ident[:])
                return r
            tr.op("pe", tps, [hb, ident], [psT])
            tr.op("act", lambda e: e.activation(
                out=view(hT[:, j * 128:j * 128 + 1], [(TT, 8), (1, 128)]),
                in_=view(psT[:, 0:1], [(128, 8), (1, 128)]), func=AF.Copy), [psT], [hT])

        def post_residual(ps_y, xt, j, ss2, rstd2, junk, tmp, gpost):
            xj = xt[:, j * D:(j + 1) * D]
            tr.op("act", lambda e: e.activation(out=junk[:], in_=ps_y[:, 0:D], func=AF.Square,
                                                accum_out=ss2[:, j:j + 1]), [ps_y], [junk, ss2])
            rstd_from_ss(ss2, ss2[:, j:j + 1], rstd2, rstd2[:, j:j + 1], 1.0 / D)
            tr.op("dve", lambda e: e.scalar_tensor_tensor(out=tmp[:], in0=ps_y[:, 0:D], scalar=rstd2[:, j:j + 1],
                                                          in1=gpost[:], op0=ALU.mult, op1=ALU.mult),
                  [ps_y, rstd2, gpost], [tmp])
            tr.op("pool", lambda e: e.tensor_tensor(out=xj, in0=xj, in1=tmp[:], op=ALU.add), [xt, tmp], [xt])

        def rope_small(ps_src, nh, tb, j, ra, rb, dst, dst_off):
            import os as _os4
            RS = _os4.environ.get("KROPE", "")
            tbj = tb[:, j * 160:(j + 1) * 160]
            x_all = view(ps_src[:, 0:1], [(64, nh), (1, 16)])
            cc = view(tbj[:, 128:129], [(0, nh), (1, 16)])
            if "1" in RS:
                return
            tr.op("dve", lambda e: e.tensor_tensor(out=view(ra[:, 0:1], [(16, nh), (1, 16)]), in0=x_all, in1=cc,
                                                   op=ALU.mult), [ps_src, tb], [ra])
            if "2" in RS:
                return
            x_hi = view(ps_src[:, 8:9], [(64, nh), (1, 8)])
            x_lo = view(ps_src[:, 0:1], [(64, nh), (1, 8)])
            s_neg = view(tbj[:, 144:145], [(0, nh), (1, 8)])
            s_pos = view(tbj[:, 152:153], [(0, nh), (1, 8)])
            tr.op("dve", lambda e: e.tensor_tensor(out=view(rb[:, 0:1], [(16, nh), (1, 8)]), in0=x_hi, in1=s_neg,
                                                   op=ALU.mult), [ps_src, tb], [rb])
            tr.op("dve", lambda e: e.tensor_tensor(out=view(rb[:, 8:9], [(16, nh), (1, 8)]), in0=x_lo, in1=s_pos,
                                                   op=ALU.mult), [ps_src, tb], [rb])
            if "3" in RS:
                return
            tr.op("pool", lambda e: e.tensor_tensor(out=view(dst[:, dst_off:dst_off + 1], [(64, nh), (1, 16)]),
                                                    in0=view(ra[:, 0:1], [(16, nh), (1, 16)]),
                                                    in1=view(rb[:, 0:1], [(16, nh), (1, 16)]), op=ALU.add),
                  [ra, rb], [dst])

        def transposes_to_stage(src, src_off, nchunk, psT2, stage, st_chunk0, j, TT):
            def tps(e):
                r = None
                for c in range(nchunk):
                    r = e.transpose(psT2[:, c * 128:(c + 1) * 128],
                                    src[:, src_off + c * 128:src_off + (c + 1) * 128], ident[:])
                return r
            tr.op("pe", tps, [src, ident], [psT2])
            tr.op("act", lambda e: e.activation(
                out=view(stage[:, st_chunk0 * TT + j * 128:st_chunk0 * TT + j * 128 + 1], [(TT, nchunk), (1, 128)]),
                in_=view(psT2[:, 0:1], [(128, nchunk), (1, 128)]), func=AF.Copy), [psT2], [stage])

        def phase_p1(l):
            even = (l % 2 == 0)
            li = l // 2
            NIN = EVEN_IN if even else ODD_IN
            TT = 512
            ph = Phase(tr, nc)
            Wb = ph.sb([128, 8 * NIN], BF16, name="Win")
            import os as _os6
            if not even:
                dummy2 = ph.sb([128, 8], F32, dma=True)
            gT = ph.sb([128, 8], F32, dma=True)
            tr.dma(gT[:], gpreT_mix[l], gT, True)
            load_weight(ph, Wb, (w_in_even if even else w_in_odd)[li], 8, NIN, gT=gT, nch=(1152 if even else 768))
            NQC = 8
            NKC = 5 if even else 2
            NV = 640 if even else 256
            xts = [ph.sb([128, 4 * D], F32, dma=True, name="xt") for _ in range(2)]
            tbs = [ph.sb([128, 4 * 160], F32, dma=True, name="tb") for _ in range(2)]
            QTst = [ph.sb([128, NQC * TT], BF16, dma=True, name="QTst") for _ in range(2)]
            KTst = [ph.sb([128, NKC * TT], BF16, dma=True, name="KTst") for _ in range(2)]
            Vst = [ph.sb([128, 4 * NV], BF16, dma=True, name="Vst") for _ in range(2)]
            hT = ph.sb([128, 8 * TT], BF16, name="hT")
            hb = ph.sb([128, D], BF16, name="hb")
            junk = ph.sb([128, D], BF16, name="junk")
            ss = ph.sb([128, 4], F32)
            rstd = ph.sb([128, 4], F32)
            ra = ph.sb([128, 256], F32)
            rb = ph.sb([128, 256], F32)
            psT = ph.ps([128, 1024], BF16, name="psT")
            if even:
                gq = ph.sb([128, 640], F32, dma=True)
                tr.dma(gq[:], gqk[li], gq, True)
                sqs = ph.sb([128, 640], F32)
                ssh = ph.sb([128, 10], F32)
                rsh = ph.sb([128, 10], F32)
                tq = ph.sb([128, 640], F32)
                ta = ph.sb([128, 640], F32)
                tbb = ph.sb([128, 640], F32)
                qkb = ph.sb([128, 640], BF16)
                qbb = ph.sb([128, 512], BF16)
                kbb = ph.sb([128, 512], BF16)
                psA = ph.ps([128, 1024], F32, name="psA")
                psQB = ph.ps([128, 512], F32, name="psQB")
                psKB = ph.ps([128, 512], F32, name="psKB")
                psVB = ph.ps([128, 512], F32, name="psVB")
                psT2a = ph.ps([128, 1024], BF16, name="psT2a")
                psT2b = ph.ps([128, 1024], BF16, name="psT2b")
            else:
                import os as _os5
                if _os5.environ.get("KDUMMY"):
                    dummy = ph.sb([128, int(_os5.environ["KDUMMY"])], F32, dma=True)
                qb16 = ph.sb([128, 1024], BF16)
                kb16 = ph.sb([128, 256], BF16)
                psQ0 = ph.ps([128, 512], F32, name="psQ0")
                psQ1 = ph.ps([128, 512], F32, name="psQ1")
                psKV = ph.ps([128, 512], F32, name="psKV")
                psT2a = ph.ps([128, 1024], BF16, name="psT2a")
                psT2b = ph.ps([128, 1024], BF16, name="psT2b")

            qfs = [ph.sb([128, 512], F32, name="qf") for _ in range(2)]
            qfc = [0]

            def evac_rope(ps_buf, ps_off, ncols, nh, dst, dst_off, tb, j):
                qf = qfs[qfc[0] % 2]
                qfc[0] += 1
                tr.op("act", lambda e: e.activation(out=qf[:, 0:ncols], in_=ps_buf[:, ps_off:ps_off + ncols],
                                                    func=AF.Copy), [ps_buf], [qf])
                tr.op("pool", lambda e: e.tensor_copy(out=dst[:, dst_off:dst_off + ncols], in_=qf[:, 0:ncols]),
                      [qf], [dst])
                rope_small(qf, nh, tb, j, ra, rb, dst, dst_off)

            tiles = [(t0 + i * TT, t0, i * TT) for (t0, S) in seqs for i in range(S // TT)]

            def issue_loads(ti):
                g0, t0, off = tiles[ti]
                xt = xts[ti % 2]
                import os as _os3
                tr.dma(view(xt[:, 0:1], [(D, 4), (1, D)]),
                       x_rows(l == 0 or bool(_os3.environ.get("KXIN")), g0, TT).rearrange("(j p) d -> p j d", p=128), xt, True)
                tb = tbs[ti % 2]
                tr.dma(view(tb[:, 0:1], [(160, 4), (1, 160)]),
                       tab[off:off + TT, :].rearrange("(j p) d -> p j d", p=128), tb, True)

            def mm_group(ps_buf, ps_off, j, col0, ncols):
                def f(e):
                    r = None
                    for c in range(8):
                        r = e.matmul(ps_buf[:, ps_off:ps_off + ncols],
                                     lhsT=hT[:, c * TT + j * 128:c * TT + (j + 1) * 128],
                                     rhs=Wb[:, c * NIN + col0:c * NIN + col0 + ncols],
                                     start=(c == 0), stop=(c == 7))
                    return r
                return f

            issue_loads(0)
            for ti in range(len(tiles)):
                g0, t0, off = tiles[ti]
                if ti + 1 < len(tiles):
                    issue_loads(ti + 1)
                xt = xts[ti % 2]
                tb = tbs[ti % 2]
                qst, kst, vst = QTst[ti % 2], KTst[ti % 2], Vst[ti % 2]
                for j in range(4):
                    norm_transpose(xt, j, ss, rstd, junk, hb, psT, hT, TT)
                    tbj = tb[:, j * 160:(j + 1) * 160]
                    if even:
                        tr.op("pe", mm_group(psA, 0, j, 0, 512), [hT, Wb], [psA])
                        tr.op("pe", mm_group(psA, 512, j, 512, 256), [hT, Wb], [psA])
                        tr.op("pe", mm_group(psQB, 0, j, 768, 512), [hT, Wb], [psQB])
                        tr.op("pe", mm_group(psKB, 0, j, 1280, 512), [hT, Wb], [psKB])
                        tr.op("pe", mm_group(psVB, 0, j, 1792, 512), [hT, Wb], [psVB])
                        tr.op("act", lambda e: e.activation(out=sqs[:], in_=psA[:, 0:640], func=AF.Square),
                              [psA], [sqs])
                        tr.op("dve", lambda e: e.tensor_reduce(out=ssh[:], in_=view(sqs[:, 0:1], [(64, 10), (1, 64)]),
                                                               axis=AX.X, op=ALU.add), [sqs], [ssh])
                        rstd_from_ss(ssh, ssh[:], rsh, rsh[:], 1.0 / HD)
                        tr.op("dve", lambda e: e.tensor_tensor(
                            out=view(tq[:, 0:1], [(64, 10), (1, 64)]), in0=view(psA[:, 0:1], [(64, 10), (1, 64)]),
                            in1=view(rsh[:, 0:1], [(1, 10), (0, 64)]), op=ALU.mult), [psA, rsh], [tq])
                        tr.op("act", lambda e: e.activation(out=vst[:, j * 640:j * 640 + 128], in_=psA[:, 640:768],
                                                            func=AF.Copy), [psA], [vst])
                        tr.op("pool", lambda e: e.tensor_tensor(out=tq[:], in0=tq[:], in1=gq[:], op=ALU.mult),
                              [tq, gq], [tq])
                        tr.op("pool", lambda e: e.tensor_tensor(
                            out=view(ta[:, 0:1], [(64, 10), (1, 64)]), in0=view(tq[:, 0:1], [(64, 10), (1, 64)]),
                            in1=view(tbj[:, 0:1], [(0, 10), (1, 64)]), op=ALU.mult), [tq, tb], [ta])
                        tr.op("pool", lambda e: e.tensor_tensor(
                            out=view(tbb[:, 0:1], [(64, 10), (32, 2), (1, 16)]),
                            in0=view(tq[:, 16:17], [(64, 10), (32, 2), (1, 16)]),
                            in1=view(tbj[:, 64:65], [(0, 10), (32, 2), (1, 16)]), op=ALU.mult), [tq, tb], [tbb])
                        tr.op("pool", lambda e: e.tensor_tensor(
                            out=view(tbb[:, 16:17], [(64, 10), (32, 2), (1, 16)]),
                            in0=view(tq[:, 0:1], [(64, 10), (32, 2), (1, 16)]),
                            in1=view(tbj[:, 80:81], [(0, 10), (32, 2), (1, 16)]), op=ALU.mult), [tq, tb], [tbb])
                        tr.op("pool", lambda e: e.tensor_tensor(out=qkb[:], in0=ta[:], in1=tbb[:], op=ALU.add),
                              [ta, tbb], [qkb])
                        transposes_to_stage(qkb, 0, 4, psT2a, qst, 0, j, TT)
                        transposes_to_stage(qkb, 512, 1, psT2b, kst, 0, j, TT)
                        evac_rope(psQB, 0, 512, 8, qbb, 0, tb, j)
                        transposes_to_stage(qbb, 0, 4, psT2a, qst, 4, j, TT)
                        evac_rope(psKB, 0, 512, 8, kbb, 0, tb, j)
                        transposes_to_stage(kbb, 0, 4, psT2b, kst, 1, j, TT)
                        tr.op("act", lambda e: e.activation(out=vst[:, j * 640 + 128:(j + 1) * 640],
                                                            in_=psVB[:, 0:512], func=AF.Copy), [psVB], [vst])
                    else:
                        import os as _os
                        SK = _os.environ.get("KSKIP", "")
                        tr.op("pe", mm_group(psQ0, 0, j, 0, 512), [hT, Wb], [psQ0])
                        tr.op("pe", mm_group(psQ1, 0, j, 512, 512), [hT, Wb], [psQ1])
                        tr.op("pe", mm_group(psKV, 0, j, 1024, 512), [hT, Wb], [psKV])
                        evac_rope(psQ0, 0, 512, 8, qb16, 0, tb, j)
                        evac_rope(psQ1, 0, 512, 8, qb16, 512, tb, j)
                        if "c" not in SK:
                            transposes_to_stage(qb16, 0, 8, psT2a, qst, 0, j, TT)
                        evac_rope(psKV, 0, 256, 4, kb16, 0, tb, j)
                        if "e" not in SK:
                            transposes_to_stage(kb16, 0, 2, psT2b, kst, 0, j, TT)
                        tr.op("act", lambda e: e.activation(out=vst[:, j * 256:(j + 1) * 256], in_=psKV[:, 256:512],
                                                            func=AF.Copy), [psKV], [vst])
                import os as _os2
                SK2 = _os2.environ.get("KSKIP", "")
                if "q" not in SK2:
                    tr.dma(QTd[0:NQC, :, g0:g0 + TT].rearrange("c p s -> p c s"),
                           view(qst[:, 0:1], [(TT, NQC), (1, TT)]), qst, False)
                if "k" not in SK2:
                    tr.dma(KTd[0:NKC, :, g0:g0 + TT].rearrange("c p s -> p c s"),
                           view(kst[:, 0:1], [(TT, NKC), (1, TT)]), kst, False)
                if "v" not in SK2:
                    tr.dma(Vd[g0:g0 + TT, 0:NV].rearrange("(j p) d -> p j d", p=128),
                           view(vst[:, 0:1], [(NV, 4), (1, NV)]), vst, False)
            ph.close()

        def attn_ac(ph, seq, K2, Vg, qchunk, window, esink, es_cols, bufs):
            t0, S = seq
            (QTs, ps_s, pts, ps_o, ps_bc, recs, bcs, OTst, cnt, pend) = bufs
            nk = S // 128
            nq = S // 512
            for qt in range(nq):
                QT = QTs[cnt[0] % len(QTs)]
                cnt[0] += 1
                tr.dma(QT[:], QTd[qchunk, :, t0 + qt * 512:t0 + (qt + 1) * 512], QT, True)
                if window:
                    kcs = [kc for kc in range(qt * 4 - 1, qt * 4 + 5) if 0 <= kc < nk]
                else:
                    kcs = list(range(nk))

                def qk(i, kc):
                    pss = ps_s[i % 2]
                    def f(e):
                        e.matmul(pss[:, 0:512], lhsT=K2[0:64, kc * 128:(kc + 1) * 128], rhs=QT[0:64, :],
                                 start=True, stop=True)
                        return e.matmul(pss[:, 512:1024], lhsT=K2[64:128, kc * 128:(kc + 1) * 128],
                                        rhs=QT[64:128, :], start=True, stop=True)
                    tr.op("pe", f, [K2, QT], [pss])

                def expo(i, kc):
                    pss = ps_s[i % 2]
                    pt = pts[i % 3]
                    tr.op("act", lambda e: e.activation(out=pt[:], in_=pss[:], func=AF.Exp, scale=SCALE),
                          [pss], [pt])
                    if window:
                        jm = kc - qt * 4 + 1
                        tr.op("dve", lambda e: e.tensor_tensor(
                            out=view(pt[:, 0:1], [(512, 2), (1, 512)]), in0=view(pt[:, 0:1], [(512, 2), (1, 512)]),
                            in1=view(mask_b[:, jm * 512:jm * 512 + 1], [(0, 2), (1, 512)]), op=ALU.mult),
                            [pt, mask_b], [pt])

                def pv(i, kc):
                    pt = pts[i % 3]
                    def f(e):
                        e.matmul(ps_o[0][0:65, :], lhsT=Vg[:, kc * 65:(kc + 1) * 65], rhs=pt[:, 0:512],
                                 start=(i == 0), stop=(i == len(kcs) - 1))
                        return e.matmul(ps_o[1][0:65, :], lhsT=Vg[:, kc * 65:(kc + 1) * 65], rhs=pt[:, 512:1024],
                                        start=(i == 0), stop=(i == len(kcs) - 1))
                    tr.op("pe", f, [Vg, pt], [ps_o[0], ps_o[1]])

                for i, kc in enumerate(kcs):
                    qk(i, kc)
                    expo(i, kc)
                    if i == 1 and pend:
                        pend.pop(0)()
                    if i > 0:
                        pv(i - 1, kcs[i - 1])
                pv(len(kcs) - 1, kcs[-1])
                for hh in range(2):
                    po = ps_o[hh]
                    rec = recs[hh]
                    if esink is not None:
                        col = es_cols[hh]
                        tr.op("dve", lambda e: e.tensor_scalar(out=rec[64:65, :], in0=po[64:65, :],
                                                               scalar1=esink[64:65, col:col + 1], scalar2=None,
                                                               op0=ALU.add), [po, esink], [rec])
                        tr.op("dve", lambda e: e.reciprocal(out=rec[64:65, :], in_=rec[64:65, :]), [rec], [rec])
                    else:
                        tr.op("dve", lambda e: e.reciprocal(out=rec[64:65, :], in_=po[64:65, :]), [po], [rec])

                def fin(qt=qt):
                    for hh in range(2):
                        po = ps_o[hh]
                        rec = recs[hh]
                        tr.op("pe", lambda e: e.matmul(ps_bc[0:64, :], lhsT=ones_f[64:65, 0:64], rhs=rec[64:65, :],
                                                       start=True, stop=True), [ones_f, rec], [ps_bc])
                        tr.op("act", lambda e: e.activation(out=bcs[0:64, :], in_=ps_bc[0:64, :], func=AF.Copy),
                              [ps_bc], [bcs])
                        ot = OTst[cnt[1] % len(OTst)]
                        cnt[1] += 1
                        tr.op("dve", lambda e: e.tensor_tensor(out=ot[0:64, :], in0=po[0:64, :], in1=bcs[0:64, :],
                                                               op=ALU.mult), [po, bcs], [ot])
                        tr.dma(OTd[qchunk, hh * 64:(hh + 1) * 64, t0 + qt * 512:t0 + (qt + 1) * 512], ot[0:64, :],
                               ot, False)
                pend.append(fin)

        def alloc_ac(ph):
            QTs = [ph.sb([128, 512], BF16, dma=True, name="QT") for _ in range(3)]
            ps_s = [ph.ps([128, 1024], F32, name="ps_s") for _ in range(2)]
            pts = [ph.sb([128, 1024], BF16, name="pt") for _ in range(3)]
            ps_o = [ph.ps([128, 512], F32, name="ps_o") for _ in range(2)]
            ps_bc = ph.ps([128, 512], F32, name="ps_bc")
            recs = [ph.sb([128, 512], F32, name="rec") for _ in range(2)]
            bcs = ph.sb([128, 512], F32, name="bcs")
            OTst = [ph.sb([128, 512], BF16, dma=True, name="OTst") for _ in range(4)]
            return (QTs, ps_s, pts, ps_o, ps_bc, recs, bcs, OTst, [0, 0], [])

        def load_kv_ac(seq, K2, Vg, kchunk, khalf, vcol):
            t0, S = seq
            nk = S // 128
            for half in range(2):
                tr.dma(K2[half * 64:(half + 1) * 64, 0:S], KTd[kchunk, khalf * 64:(khalf + 1) * 64, t0:t0 + S], K2, True)
            tr.dma(view(Vg[:, 0:1], [(65, nk), (1, 64)]),
                   Vd[t0:t0 + S, vcol:vcol + 64].rearrange("(k p) d -> p k d", p=128), Vg, True)
            tr.op("pool", lambda e: e.memset(view(Vg[:, 64:65], [(65, nk), (1, 1)]), 1.0), [], [Vg])

        def phase_p2_even_a(l):
            ph = Phase(tr, nc)
            Smax = max(Sp, Ss)
            K2 = ph.sb([128, Smax], BF16, dma=True, name="K2")
            Vg = ph.sb([128, (Smax // 128) * 65], BF16, dma=True, name="Vg")
            bufs = alloc_ac(ph)
            for seq in seqs:
                for g in range(2):
                    load_kv_ac(seq, K2, Vg, 0, g, g * 64)
                    for pair in range(2):
                        attn_ac(ph, seq, K2, Vg, g * 2 + pair, False, None, None, bufs)
            while bufs[9]:
                bufs[9].pop(0)()
            ph.close()

        def phase_p2_odd(l):
            li = l // 2
            ph = Phase(tr, nc)
            Smax = max(Sp, Ss)
            K2 = ph.sb([128, Smax], BF16, dma=True, name="K2")
            Vg = ph.sb([128, (Smax // 128) * 65], BF16, dma=True, name="Vg")
            esink = ph.sb([128, 16], F32, dma=True, name="esink")
            tr.dma(esink[64:65, :], sinkc[li:li + 1, :], esink, True)
            tr.op("act", lambda e: e.activation(out=esink[64:65, :], in_=esink[64:65, :], func=AF.Exp),
                  [esink], [esink])
            bufs = alloc_ac(ph)
            for seq in seqs:
                for hk in range(4):
                    load_kv_ac(seq, K2, Vg, hk // 2, hk % 2, hk * 64)
                    for pair in range(2):
                        qc = hk * 2 + pair
                        attn_ac(ph, seq, K2, Vg, qc, True, esink, (2 * qc, 2 * qc + 1), bufs)
            while bufs[9]:
                bufs[9].pop(0)()
            ph.close()

        def phase_p2_even_b(l):
            li = l // 2
            c_out = 1.0 - lam_init_of(l)
            ph = Phase(tr, nc)
            Smax = max(Sp, Ss)
            K2 = ph.sb([128, Smax], BF16, dma=True, name="KB")
            VB = ph.sb([128, Smax], BF16, dma=True, name="VB")
            QTs = [ph.sb([128, 512], BF16, dma=True, name="QT") for _ in range(3)]
            ps_s = [ph.ps([128, 1024], F32, name="ps_s") for _ in range(2)]
            pts = [ph.sb([128, 1024], BF16, name="pt") for _ in range(3)]
            ps_o = [ph.ps([128, 512], F32, name="ps_o") for _ in range(2)]
            ps_d = [ph.ps([128, 512], F32, name="ps_d") for _ in range(2)]
            r0 = ph.sb([128, 512], F32)
            r1 = ph.sb([128, 512], F32)
            fa = ph.sb([128, 512], F32)
            fb = ph.sb([128, 512], F32)
            fo = ph.sb([128, 512], F32)
            fsq = ph.sb([128, 512], F32)
            frs = ph.sb([128, 512], F32)
            OTst = [ph.sb([128, 512], BF16, dma=True, name="OTst") for _ in range(2)]
            dl = ph.sb([128, 256], F32, dma=True)
            tr.dma(dl[:], dlam[li], dl, True)
            pr = ph.sb([128, 128], F32)
            sm = ph.sb([128, 2], F32)
            ex = ph.sb([128, 2], F32)
            nlam = ph.sb([128, 1], F32)
            tr.op("dve", lambda e: e.tensor_tensor(out=view(pr[:, 0:1], [(64, 2), (1, 64)]),
                                                   in0=view(dl[:, 0:1], [(128, 2), (1, 64)]),
                                                   in1=view(dl[:, 64:65], [(128, 2), (1, 64)]), op=ALU.mult), [dl], [pr])
            tr.op("dve", lambda e: e.tensor_reduce(out=sm[:], in_=view(pr[:, 0:1], [(64, 2), (1, 64)]), axis=AX.X,
                                                   op=ALU.add), [pr], [sm])
            tr.op("act", lambda e: e.activation(out=ex[:], in_=sm[:], func=AF.Exp), [sm], [ex])
            tr.op("dve", lambda e: e.tensor_tensor(out=nlam[:], in0=ex[:, 1:2], in1=ex[:, 0:1], op=ALU.subtract),
                  [ex], [nlam])
            tr.op("dve", lambda e: e.tensor_scalar(out=nlam[:], in0=nlam[:], scalar1=-lam_init_of(l), scalar2=None,
                                                   op0=ALU.add), [nlam], [nlam])
            qcnt = 0
            ocntB = [0]
            pendB = []
            accA = ph.sb([128, 512], F32, name="accA")
            accB = ph.sb([128, 512], F32, name="accB")
            for (t0, S) in seqs:
                nk = S // 128
                nq = S // 512
                for h in range(4):
                    tr.dma(K2[:, 0:S], KTd[1 + h, :, t0:t0 + S], K2, True)
                    tr.dma(view(VB[:, 0:1], [(128, nk), (1, 128)]),
                           Vd[t0:t0 + S, 128 + h * 128:128 + (h + 1) * 128].rearrange("(k p) d -> p k d", p=128),
                           VB, True)
                    for qt in range(nq):
                        QT = QTs[qcnt % 3]
                        qcnt += 1
                        tr.dma(QT[:], QTd[4 + h, :, t0 + qt * 512:t0 + (qt + 1) * 512], QT, True)

                        def qk(i):
                            pss = ps_s[i % 2]
                            def f(e):
                                e.matmul(pss[:, 0:512], lhsT=K2[0:64, i * 128:(i + 1) * 128], rhs=QT[0:64, :],
                                         start=True, stop=True)
                                return e.matmul(pss[:, 512:1024], lhsT=K2[64:128, i * 128:(i + 1) * 128],
                                                rhs=QT[64:128, :], start=True, stop=True)
                            tr.op("pe", f, [K2, QT], [pss])

                        def expo(i):
                            pss = ps_s[i % 2]
                            pt = pts[i % 3]
                            tr.op("act", lambda e: e.activation(out=pt[:], in_=pss[:], func=AF.Exp, scale=SCALE),
                                  [pss], [pt])

                        def pv(i):
                            pt = pts[i % 3]
                            st, sp_ = (i == 0), (i == nk - 1)
                            def f(e):
                                e.matmul(ps_o[0][:, :], lhsT=VB[:, i * 128:(i + 1) * 128], rhs=pt[:, 0:512],
                                         start=st, stop=sp_)
                                return e.matmul(ps_o[1][:, :], lhsT=VB[:, i * 128:(i + 1) * 128],
                                                rhs=pt[:, 512:1024], start=st, stop=sp_)
                            tr.op("pe", f, [VB, pt], [ps_o[0], ps_o[1]])

                        def dacc(i):
                            pt = pts[i % 3]
                            if i == 0:
                                tr.op("dve", lambda e: e.tensor_copy(out=accA[:], in_=pt[:, 0:512]), [pt], [accA])
                                tr.op("pool", lambda e: e.tensor_copy(out=accB[:], in_=pt[:, 512:1024]), [pt], [accB])
                            else:
                                tr.op("dve", lambda e: e.tensor_tensor(out=accA[:], in0=accA[:], in1=pt[:, 0:512],
                                                                       op=ALU.add), [pt, accA], [accA])
                                tr.op("pool", lambda e: e.tensor_tensor(out=accB[:], in0=accB[:], in1=pt[:, 512:1024],
                                                                        op=ALU.add), [pt, accB], [accB])

                        for i in range(nk):
                            qk(i)
                            expo(i)
                            dacc(i)
                            if pendB and i == pendB[0][0]:
                                pendB.pop(0)[1]()
                            if i > 0:
                                pv(i - 1)
                        pv(nk - 1)

                        tr.op("pe", lambda e: e.matmul(ps_d[0][:, :], lhsT=ones_f[:, :], rhs=accA[:], start=True,
                                                       stop=True), [ones_f, accA], [ps_d[0]])
                        tr.op("pe", lambda e: e.matmul(ps_d[1][:, :], lhsT=ones_f[:, :], rhs=accB[:], start=True,
                                                       stop=True), [ones_f, accB], [ps_d[1]])

                        def fin1():
                            tr.op("dve", lambda e: e.reciprocal(out=r0[:], in_=ps_d[0][:, :]), [ps_d[0]], [r0])
                            tr.op("dve", lambda e: e.reciprocal(out=r1[:], in_=ps_d[1][:, :]), [ps_d[1]], [r1])
                            tr.op("dve", lambda e: e.tensor_tensor(out=fa[:], in0=ps_o[0][:, :], in1=r0[:],
                                                                   op=ALU.mult), [ps_o[0], r0], [fa])
                            tr.op("dve", lambda e: e.tensor_tensor(out=fb[:], in0=ps_o[1][:, :], in1=r1[:],
                                                                   op=ALU.mult), [ps_o[1], r1], [fb])
                            tr.op("dve", lambda e: e.scalar_tensor_tensor(out=fo[:], in0=fb[:], scalar=nlam[:, 0:1],
                                                                          in1=fa[:], op0=ALU.mult, op1=ALU.add),
                                  [fb, fa, nlam], [fo])
                            tr.op("pool", lambda e: e.tensor_tensor(out=fsq[:], in0=fo[:], in1=fo[:], op=ALU.mult),
                                  [fo], [fsq])

                        def fin2(h=h, t0=t0, qt=qt):
                            tr.op("pe", lambda e: e.matmul(ps_d[0][:, :], lhsT=ones_f[:, :], rhs=fsq[:], start=True,
                                                           stop=True), [ones_f, fsq], [ps_d[0]])
                            rstd_from_ss(ps_d[0], ps_d[0][:, :], frs, frs[:], 1.0 / 128)
                            ot = OTst[ocntB[0] % 2]
                            ocntB[0] += 1
                            tr.op("dve", lambda e: e.scalar_tensor_tensor(out=ot[:], in0=fo[:], scalar=c_out,
                                                                          in1=frs[:], op0=ALU.mult, op1=ALU.mult),
                                  [fo, frs], [ot])
                            tr.dma(OTd[4 + h, :, t0 + qt * 512:t0 + (qt + 1) * 512], ot[:], ot, False)
                        pendB.append((1, fin1))
                        pendB.append((3, fin2))
            while pendB:
                pendB.pop(0)[1]()
            ph.close()

        def phase_p3a(l):
            even = (l % 2 == 0)
            li = l // 2
            TT = 512
            ph = Phase(tr, nc)
            Wb = ph.sb([128, 8 * D], BF16, name="Wout")
            load_weight(ph, Wb, (w_out_even if even else w_out_odd)[li], 8, D, nch=1024)
            gpost = ph.sb([128, D], F32, dma=True)
            tr.dma(gpost[:], gpost_mix[l], gpost, True)
            xts = [ph.sb([128, 4 * D], F32, dma=True, name="xt") for _ in range(2)]
            OTs = [ph.sb([128, 8 * TT], BF16, dma=True, name="OT") for _ in range(2)]
            junk = ph.sb([128, D], BF16)
            tmp = ph.sb([128, D], F32)
            ss2 = ph.sb([128, 4], F32)
            rstd2 = ph.sb([128, 4], F32)
            ps_m = [ph.ps([128, 1024], F32, name="ps_m") for _ in range(2)]
            tiles = [t0 + i * TT for (t0, S) in seqs for i in range(S // TT)]

            def issue_loads(ti):
                g0 = tiles[ti]
                xt = xts[ti % 2]
                tr.dma(view(xt[:, 0:1], [(D, 4), (1, D)]),
                       x_rows(l == 0, g0, TT).rearrange("(j p) d -> p j d", p=128), xt, True)
                ot = OTs[ti % 2]
                tr.dma(view(ot[:, 0:1], [(TT, 8), (1, TT)]), OTd[:, :, g0:g0 + TT].rearrange("c p s -> p c s"),
                       ot, True)

            issue_loads(0)
            k = 0
            for ti in range(len(tiles)):
                g0 = tiles[ti]
                if ti + 1 < len(tiles):
                    issue_loads(ti + 1)
                xt = xts[ti % 2]
                ot = OTs[ti % 2]
                for j in range(4):
                    pm = ps_m[k % 2]
                    k += 1
                    def f(e):
                        r = None
                        for half in range(2):
                            for c in range(8):
                                r = e.matmul(pm[:, half * 512:(half + 1) * 512],
                                             lhsT=ot[:, c * TT + j * 128:c * TT + (j + 1) * 128],
                                             rhs=Wb[:, c * D + half * 512:c * D + (half + 1) * 512],
                                             start=(c == 0), stop=(c == 7))
                        return r
                    tr.op("pe", f, [ot, Wb], [pm])
                    post_residual(pm, xt, j, ss2, rstd2, junk, tmp, gpost)
                tr.dma(x_rows(False, g0, TT).rearrange("(j p) d -> p j d", p=128),
                       view(xt[:, 0:1], [(D, 4), (1, D)]), xt, False)
            ph.close()

        def phase_p3b(l):
            TT = 512
            NJ = 4
            NH2 = NJH // 2
            HW = NH2 * 128
            ph = Phase(tr, nc)
            Wgu = ph.sb([128, 8 * 2 * HW], BF16, name="Wgu")
            Wd = ph.sb([128, NH2 * D], BF16, name="Wd")
            gT = ph.sb([128, 8], F32, dma=True)
            tr.dma(gT[:], gpreT_ffn[l], gT, True)
            gpost = ph.sb([128, D], F32, dma=True)
            tr.dma(gpost[:], gpost_ffn[l], gpost, True)
            xts = [ph.sb([128, NJ * D], F32, dma=True, name="xt") for _ in range(2)]
            y1b = [ph.sb([128, 2 * D], F32, dma=True, name="y1b") for _ in range(2)]
            hT = ph.sb([128, 8 * TT], BF16, name="hT")
            aT = ph.sb([128, NH2 * TT], BF16, name="aT")
            hb = ph.sb([128, D], BF16)
            junk = ph.sb([128, D], BF16)
            tmp = ph.sb([128, D], F32)
            tmp2 = ph.sb([128, D], F32)
            sg = [ph.sb([128, TT], F32, name="sg") for _ in range(2)]
            ss = ph.sb([128, 4], F32)
            rstd = ph.sb([128, 4], F32)
            ss2 = ph.sb([128, 4], F32)
            rstd2 = ph.sb([128, 4], F32)
            psT = ph.ps([128, 1024], BF16, name="psT")
            ps_gu = [ph.ps([128, 1024], F32, name="ps_gu") for _ in range(2)]
            ps_y = ph.ps([128, 1024], F32, name="ps_y")
            tiles = [t0 + i * TT for (t0, S) in seqs for i in range(S // TT)]
            kg = 0
            for half in range(2):
                load_weight(ph, Wgu, w_gate_up[l][:, half * HW:(half + 1) * HW], 8, HW, gT=gT, nch=704,
                            dstride=2 * HW, doff=0)
                load_weight(ph, Wgu, w_gate_up[l][:, FFN_H + half * HW:FFN_H + (half + 1) * HW], 8, HW, gT=gT,
                            nch=704, dstride=2 * HW, doff=HW)
                load_weight(ph, Wd, w_down[l][half * HW:(half + 1) * HW, :], NH2, D, nch=512)

                def issue_loads(ti):
                    g0 = tiles[ti]
                    xt = xts[ti % 2]
                    tr.dma(view(xt[:, 0:1], [(D, NJ), (1, D)]),
                           x_rows(False, g0, TT).rearrange("(j p) d -> p j d", p=128), xt, True)

                def y1_rows(g0, jj):
                    return Y1d[g0 + jj * 256:g0 + (jj + 1) * 256, :].rearrange("(j p) d -> p j d", p=128)

                issue_loads(0)
                yk = 0
                for ti in range(len(tiles)):
                    g0 = tiles[ti]
                    if ti + 1 < len(tiles):
                        issue_loads(ti + 1)
                    xt = xts[ti % 2]
                    if half == 1:
                        for jj in range(2):
                            ybp = y1b[(yk + jj) % 2]
                            tr.dma(view(ybp[:, 0:1], [(D, 2), (1, D)]), y1_rows(g0, jj), ybp, True)
                    for j in range(NJ):
                        norm_transpose(xt, j, ss, rstd, junk, hb, psT, hT, TT)
                    for jh in range(NH2):
                        pg = ps_gu[kg % 2]
                        sgb = sg[kg % 2]
                        kg += 1

                        def f(e):
                            r = None
                            for gu in range(2):
                                col = gu * HW + jh * 128
                                for c in range(8):
                                    r = e.matmul(pg[:, gu * TT:(gu + 1) * TT],
                                                 lhsT=Wgu[:, c * 2 * HW + col:c * 2 * HW + col + 128],
                                                 rhs=hT[:, c * TT:(c + 1) * TT], start=(c == 0), stop=(c == 7))
                            return r
                        tr.op("pe", f, [Wgu, hT], [pg])
                        tr.op("act", lambda e: e.activation(out=sgb[:], in_=pg[:, 0:TT], func=AF.Silu), [pg], [sgb])
                        tr.op("dve", lambda e: e.tensor_tensor(out=aT[:, jh * TT:(jh + 1) * TT], in0=pg[:, TT:2 * TT],
                                                               in1=sgb[:], op=ALU.mult), [pg, sgb], [aT])
                    for j in range(NJ):
                        jj, jl = j // 2, j % 2
                        if jl == 0:
                            yb = y1b[yk % 2]
                            yk += 1

                        def f2(e):
                            r = None
                            for dh in range(2):
                                for jh in range(NH2):
                                    r = e.matmul(ps_y[:, dh * 512:(dh + 1) * 512],
                                                 lhsT=aT[:, jh * TT + j * 128:jh * TT + (j + 1) * 128],
                                                 rhs=Wd[:, jh * D + dh * 512:jh * D + (dh + 1) * 512],
                                                 start=(jh == 0), stop=(jh == NH2 - 1))
                            return r
                        tr.op("pe", f2, [aT, Wd], [ps_y])
                        if half == 0:
                            tr.op("dve", lambda e: e.tensor_copy(out=yb[:, jl * D:(jl + 1) * D], in_=ps_y[:, 0:D]),
                                  [ps_y], [yb])
                            if jl == 1:
                                tr.dma(y1_rows(g0, jj), view(yb[:, 0:1], [(D, 2), (1, D)]), yb, False)
                        else:
                            tr.op("dve", lambda e: e.tensor_tensor(out=tmp2[:], in0=ps_y[:, 0:D],
                                                                   in1=yb[:, jl * D:(jl + 1) * D], op=ALU.add),
                                  [ps_y, yb], [tmp2])
                            post_residual(tmp2, xt, j, ss2, rstd2, junk, tmp, gpost)
                    if half == 1:
                        tr.dma(x_rows(False, g0, TT).rearrange("(j p) d -> p j d", p=128),
                               view(xt[:, 0:1], [(D, NJ), (1, D)]), xt, False)
            ph.close()

        tr.barrier()
        import os
        dbg = os.environ.get("KPHASES")
        for l in range(depth):
            if os.environ.get("KLAYERS") and str(l) not in os.environ["KLAYERS"]:
                continue
            dbg = os.environ.get("KPH%d" % l, os.environ.get("KPHASES"))
            if dbg is None or "1" in dbg:
                phase_p1(l)
            if l % 2 == 0:
                if dbg is None or "a" in dbg:
                    phase_p2_even_a(l)
                if dbg is None or "b" in dbg:
                    phase_p2_even_b(l)
            else:
                if dbg is None or "c" in dbg:
                    phase_p2_odd(l)
            if dbg is None or "3" in dbg:
                phase_p3a(l)
            if dbg is None or "4" in dbg:
                phase_p3b(l)
        gph.close()
    return nc


def _rope_table(S):
    t = np.arange(S)
    inv_ax = (10000.0 ** (-np.arange(0, 32, 2, dtype=np.float32) / 32)).astype(np.float32)
    row = (t // 64).astype(np.float32)[:, None] * inv_ax[None, :]
    col = (t % 64).astype(np.float32)[:, None] * inv_ax[None, :]
    inv_p = (500000.0 ** (-np.arange(0, 16, 2, dtype=np.float32) / 16)).astype(np.float32)
    ang = t.astype(np.float32)[:, None] * inv_p[None, :]
    tab = np.zeros((S, 160), np.float32)
    rc, rs, cc, cs = np.cos(row), np.sin(row), np.cos(col), np.sin(col)
    tab[:, 0:16] = rc; tab[:, 16:32] = rc; tab[:, 32:48] = cc; tab[:, 48:64] = cc
    tab[:, 64:80] = -rs; tab[:, 80:96] = rs; tab[:, 96:112] = -cs; tab[:, 112:128] = cs
    pc, ps = np.cos(ang), np.sin(ang)
    tab[:, 128:136] = pc; tab[:, 136:144] = pc
    tab[:, 144:152] = -ps; tab[:, 152:160] = ps
    return tab


def _mask_const():
    k = np.arange(128)[:, None]
    q = np.arange(512)[None, :]
    m = np.zeros((128, 6 * 512), np.float32)
    for jm in range(6):
        j = jm - 1
        m[:, jm * 512:(jm + 1) * 512] = (np.abs(q - (j * 128 + k)) <= 128).astype(np.float32)
    return m


_CACHE = {}


def run(inputs, Sp, Ss, n_cores, depth=DEPTH, trace=False):
    key = (Sp, Ss, depth)
    if key not in _CACHE:
        _CACHE[key] = build_program(Sp, Ss, depth)
    nc = _CACHE[key]
    f = lambda a: np.ascontiguousarray(np.asarray(a, dtype=np.float32))
    xpf, xsf = f(inputs["x_prompt"]), f(inputs["x_sample"])
    nbs = xsf.shape[0]
    qk = f(inputs["qk_norm_a"])
    gqk = np.stack([np.broadcast_to(np.concatenate([np.tile(qk[i, 0], 8), np.tile(qk[i, 1], 2)])[None, :], (128, 640))
                    for i in range(2)])
    dl = f(inputs["diff_lambda"]).reshape(2, 1, 256)
    shared = {
        "w_in_even": f(inputs["w_in_even"]), "w_out_even": f(inputs["w_out_even"]),
        "w_in_odd": f(inputs["w_in_odd"]), "w_out_odd": f(inputs["w_out_odd"]),
        "w_gate_up": f(inputs["w_gate_up"]), "w_down": f(inputs["w_down"]),
        "gpreT_mix": f(f(inputs["norm_mix_pre"]).reshape(4, 8, 128).transpose(0, 2, 1)),
        "gpreT_ffn": f(f(inputs["norm_ffn_pre"]).reshape(4, 8, 128).transpose(0, 2, 1)),
        "gpost_mix": f(np.broadcast_to(f(inputs["norm_mix_post"])[:, None, :], (4, 128, D))),
        "gpost_ffn": f(np.broadcast_to(f(inputs["norm_ffn_post"])[:, None, :], (4, 128, D))),
        "gqk": f(gqk), "dlam": f(np.broadcast_to(dl, (2, 128, 256))),
        "sinkc": f(inputs["sink_c"]),
        "tab": _rope_table(max(Sp, Ss)), "maskc": _mask_const(), "identc": np.eye(128, dtype=np.float32),
    }
    in_maps = []
    for i in range(n_cores):
        m = dict(shared)
        m["xp"] = xpf[i]
        m["xs"] = xsf[i % nbs]
        in_maps.append(m)
    res = run_bass_kernel_spmd(nc, in_maps, core_ids=list(range(n_cores)), trace=trace)
    yp = np.stack([res.results[i]["yp"] for i in range(n_cores)]).astype(np.float32)
    ysm = np.stack([res.results[i]["ys"] for i in range(nbs)]).astype(np.float32)
    return (yp, ysm), res


def kernel(**inputs):
    (yp, ysm), _ = run(inputs, 8192, 4096, NCORES)
    return (yp, ysm)
```

```python
import math
from contextlib import ExitStack

import numpy as np
import concourse.bass as bass
import concourse.mybir as mybir
from concourse.bass_utils import run_bass_kernel_spmd

F32 = mybir.dt.float32
BF16 = mybir.dt.bfloat16
ALU = mybir.AluOpType
AF = mybir.ActivationFunctionType
AX = mybir.AxisListType

D = 1024
DEPTH = 4
HD = 64
EPS = 1e-6
FFN_H = 2816
NJH = FFN_H // 128
EVEN_IN = 2304
ODD_IN = 1536
SCALE = HD ** -0.5
NCORES = 8


def lam_init_of(l):
    return 0.8 - 0.6 * math.exp(-0.3 * l)


class Buf:
    __slots__ = ("t", "lw", "rd", "dsem")

    def __init__(self, t, dsem=None):
        self.t = t
        self.lw = None
        self.rd = {}
        self.dsem = dsem

    def __getitem__(self, k):
        return self.t[k]


def view(ap, dims):
    return bass.AP(ap.tensor, ap.offset, [list(ap.ap[0])] + [list(d) for d in dims])


class TR:
    def __init__(self, nc, es, n_dma_sems=48):
        self.nc = nc
        self.E = {"pe": nc.tensor, "act": nc.scalar, "dve": nc.vector, "pool": nc.gpsimd, "sp": nc.sync}
        self.sems = {}
        self.cnt = {}
        for e in ("pe", "act", "dve", "pool"):
            self.sems[e] = es.enter_context(nc.semaphore("s_" + e))
            self.cnt[e] = 0
        self.free_dsems = []
        for i in range(n_dma_sems):
            n = "d%d" % i
            self.sems[n] = es.enter_context(nc.semaphore("s_" + n))
            self.cnt[n] = 0
            self.free_dsems.append(n)
        self.waited = {e: {} for e in self.E}
        self.inflight = {}
        import os
        self.max_inflight = int(os.environ.get("KINFLIGHT", "3"))
        self.nopool = bool(os.environ.get("KNOPOOL"))

    def _wait(self, eng, toks):
        E = self.E[eng]
        w = self.waited[eng]
        for s, v in toks.items():
            if eng == "pe" and s == "pe":
                continue
            if w.get(s, 0) < v:
                E.wait_ge(self.sems[s], v)
                w[s] = v

    @staticmethod
    def _collect(reads, writes):
        toks = {}
        for b in reads:
            if b.lw is not None:
                s, v = b.lw
                if toks.get(s, 0) < v:
                    toks[s] = v
        for b in writes:
            if b.lw is not None:
                s, v = b.lw
                if toks.get(s, 0) < v:
                    toks[s] = v
            for s, v in b.rd.items():
                if toks.get(s, 0) < v:
                    toks[s] = v
        return toks

    def op(self, eng, fn, reads=(), writes=()):
        if eng == "pool" and self.nopool:
            eng = "dve"
        self._wait(eng, self._collect(reads, writes))
        inst = fn(self.E[eng])
        self.cnt[eng] += 1
        inst.then_inc(self.sems[eng], 1)
        tok = (eng, self.cnt[eng])
        for b in writes:
            b.lw = tok
            b.rd = {}
        for b in reads:
            if b.rd.get(eng, 0) < tok[1]:
                b.rd[eng] = tok[1]
        return tok

    def dma(self, out_ap, in_ap, buf, load, q="sp"):
        if load:
            toks = self._collect((), (buf,))
        else:
            toks = self._collect((buf,), ())
        fl = self.inflight.setdefault(q, [])
        while len(fl) >= self.max_inflight:
            s0, v0 = fl.pop(0)
            if toks.get(s0, 0) < v0:
                toks[s0] = v0
        self._wait(q, toks)
        s = buf.dsem
        inst = self.E[q].dma_start(out=out_ap, in_=in_ap)
        self.cnt[s] += 16
        inst.then_inc(self.sems[s], 16)
        tok = (s, self.cnt[s])
        fl.append(tok)
        if load:
            buf.lw = tok
            buf.rd = {}
        else:
            buf.rd[s] = tok[1]
        return tok

    def barrier(self):
        for e in self.E:
            self._wait(e, dict(self.cnt))


_UID = [0]


class Phase:
    def __init__(self, tr, nc):
        self.tr = tr
        self.nc = nc
        self.es = ExitStack()
        self.dsems = []
        self.k = 0

    def sb(self, shape, dt, dma=False, name=None):
        _UID[0] += 1
        t = self.es.enter_context(self.nc.sbuf_tensor("%s_%d" % (name or "sb", _UID[0]), list(shape), dt))
        ds = None
        if dma:
            ds = self.tr.free_dsems.pop()
            self.dsems.append(ds)
        return Buf(t, ds)

    def ps(self, shape, dt, name=None):
        _UID[0] += 1
        t = self.es.enter_context(self.nc.psum_tensor("%s_%d" % (name or "ps", _UID[0]), list(shape), dt))
        return Buf(t)

    def close(self):
        self.tr.barrier()
        self.es.close()
        self.tr.free_dsems.extend(self.dsems)
        self.tr.free_dsems.sort(key=lambda n: int(n[1:]))
        self.dsems = []


def build_program(Sp, Ss, depth=DEPTH):
    T = Sp + Ss
    seqs = [(0, Sp), (Sp, Ss)]
    nc = bass.Bass("TRN2", target_bir_lowering=False)

    def din(name, shape, dt=F32):
        return nc.dram_tensor(name, list(shape), dt, kind="ExternalInput").ap()

    xp = din("xp", [Sp, D])
    xs = din("xs", [Ss, D])
    w_in_even = din("w_in_even", [2, D, EVEN_IN])
    w_out_even = din("w_out_even", [2, D, D])
    w_in_odd = din("w_in_odd", [2, D, ODD_IN])
    w_out_odd = din("w_out_odd", [2, D, D])
    w_gate_up = din("w_gate_up", [4, D, 2 * FFN_H])
    w_down = din("w_down", [4, FFN_H, D])
    gpreT_mix = din("gpreT_mix", [4, 128, 8])
    gpreT_ffn = din("gpreT_ffn", [4, 128, 8])
    gpost_mix = din("gpost_mix", [4, 128, D])
    gpost_ffn = din("gpost_ffn", [4, 128, D])
    gqk = din("gqk", [2, 128, 640])
    dlam = din("dlam", [2, 128, 256])
    sinkc = din("sinkc", [2, 16])
    tab = din("tab", [max(Sp, Ss), 160])
    maskc = din("maskc", [128, 6 * 512])
    identc = din("identc", [128, 128])
    yp = nc.dram_tensor("yp", [Sp, D], F32, kind="ExternalOutput").ap()
    ys = nc.dram_tensor("ys", [Ss, D], F32, kind="ExternalOutput").ap()
    QTd = nc.dram_tensor("QTd", [8, 128, T], BF16).ap()
    KTd = nc.dram_tensor("KTd", [5, 128, T], BF16).ap()
    Vd = nc.dram_tensor("Vd", [T, 640], BF16).ap()
    OTd = nc.dram_tensor("OTd", [8, 128, T], BF16).ap()
    Y1d = nc.dram_tensor("Y1d", [T, D], F32).ap()

    def x_rows(src_is_input, t0, n):
        if t0 < Sp:
            base = xp if src_is_input else yp
            return base[t0:t0 + n, :]
        base = xs if src_is_input else ys
        return base[t0 - Sp:t0 - Sp + n, :]

    with ExitStack() as es:
        tr = TR(nc, es)
        gph = Phase(tr, nc)
        ident_f = gph.sb([128, 128], F32, dma=True)
        ident = gph.sb([128, 128], BF16)
        ones_f = gph.sb([128, 128], F32)
        ones_b = gph.sb([128, 128], BF16)
        mask_f = gph.sb([128, 6 * 512], F32, dma=True)
        mask_b = gph.sb([128, 6 * 512], BF16)
        tr.dma(ident_f[:], identc[:, :], ident_f, True)
        tr.op("dve", lambda e: e.tensor_copy(out=ident[:], in_=ident_f[:]), [ident_f], [ident])
        tr.op("pool", lambda e: e.memset(ones_f[:], 1.0), [], [ones_f])
        tr.op("pool", lambda e: e.memset(ones_b[:], 1.0), [], [ones_b])
        eps_t = gph.sb([128, 1], F32)
        tr.op("pool", lambda e: e.memset(eps_t[:], EPS), [], [eps_t])
        tr.dma(mask_f[:], maskc[:, :], mask_f, True)
        tr.op("dve", lambda e: e.tensor_copy(out=mask_b[:], in_=mask_f[:]), [mask_f], [mask_b])

        def load_weight(ph, Wb, Wd, C, N, gT=None, nch=1408, dstride=None, doff=0):
            if dstride is None:
                dstride = N
            sub = Phase(tr, nc)
            stg = [sub.sb([128, nch], F32, dma=True, name="wst") for _ in range(3)]
            k = 0
            for c in range(C):
                for n0 in range(0, N, nch):
                    w = min(nch, N - n0)
                    st = stg[k % 3]
                    tr.dma(st[:, 0:w], Wd[c * 128:(c + 1) * 128, n0:n0 + w], st, True)
                    dst = Wb[:, c * dstride + doff + n0:c * dstride + doff + n0 + w]
                    if gT is not None:
                        if k % 2 == 0:
                            tr.op("dve", lambda e: e.tensor_scalar(
                                out=dst, in0=st[:, 0:w], scalar1=gT[:, c:c + 1], scalar2=None, op0=ALU.mult),
                                [st, gT], [Wb])
                        else:
                            tr.op("act", lambda e: e.activation(out=dst, in_=st[:, 0:w], func=AF.Copy,
                                                                scale=gT[:, c:c + 1]), [st, gT], [Wb])
                    else:
                        eng = ("dve", "pool")[k % 2]
                        tr.op(eng, lambda e: e.tensor_copy(out=dst, in_=st[:, 0:w]), [st], [Wb])
                    k += 1
            sub.close()

        def rstd_from_ss(eng_ss_buf, ss_ap, out_buf, out_ap, inv_n):
            tr.op("act", lambda e: e.activation(out=out_ap, in_=ss_ap, func=AF.Ln, scale=inv_n, bias=eps_t[:, 0:1]),
                  [eng_ss_buf, eps_t], [out_buf])
            tr.op("act", lambda e: e.activation(out=out_ap, in_=out_ap, func=AF.Exp, scale=-0.5),
                  [out_buf], [out_buf])

        def norm_transpose(xt, j, ss, rstd, junk, hb, psT, hT, TT):
            xj = xt[:, j * D:(j + 1) * D]
            tr.op("act", lambda e: e.activation(out=junk[:], in_=xj, func=AF.Square, accum_out=ss[:, j:j + 1]),
                  [xt], [junk, ss])
            rstd_from_ss(ss, ss[:, j:j + 1], rstd, rstd[:, j:j + 1], 1.0 / D)
            tr.op("dve", lambda e: e.tensor_scalar(out=hb[:], in0=xj, scalar1=rstd[:, j:j + 1], scalar2=None,
                                                   op0=ALU.mult), [xt, rstd], [hb])

            def tps(e):
                r = None
                for c in range(8):
                    r = e.transpose(psT[:, c * 128:(c + 1) * 128], hb[:, c * 128:(c + 1) * 128], ident[:])
                return r
            tr.op("pe", tps, [hb, ident], [psT])
            tr.op("act", lambda e: e.activation(
                out=view(hT[:, j * 128:j * 128 + 1], [(TT, 8), (1, 128)]),
                in_=view(psT[:, 0:1], [(128, 8), (1, 128)]), func=AF.Copy), [psT], [hT])

        def post_residual(ps_y, xt, j, ss2, rstd2, junk, tmp, gpost):
            xj = xt[:, j * D:(j + 1) * D]
            tr.op("act", lambda e: e.activation(out=junk[:], in_=ps_y[:, 0:D], func=AF.Square,
                                                accum_out=ss2[:, j:j + 1]), [ps_y], [junk, ss2])
            rstd_from_ss(ss2, ss2[:, j:j + 1], rstd2, rstd2[:, j:j + 1], 1.0 / D)
            tr.op("dve", lambda e: e.scalar_tensor_tensor(out=tmp[:], in0=ps_y[:, 0:D], scalar=rstd2[:, j:j + 1],
                                                          in1=gpost[:], op0=ALU.mult, op1=ALU.mult),
                  [ps_y, rstd2, gpost], [tmp])
            tr.op("pool", lambda e: e.tensor_tensor(out=xj, in0=xj, in1=tmp[:], op=ALU.add), [xt, tmp], [xt])

        def rope_small(ps_src, nh, tb, j, ra, rb, dst, dst_off):
            import os as _os4
            RS = _os4.environ.get("KROPE", "")
            tbj = tb[:, j * 160:(j + 1) * 160]
            x_all = view(ps_src[:, 0:1], [(64, nh), (1, 16)])
            cc = view(tbj[:, 128:129], [(0, nh), (1, 16)])
            if "1" in RS:
                return
            tr.op("dve", lambda e: e.tensor_tensor(out=view(ra[:, 0:1], [(16, nh), (1, 16)]), in0=x_all, in1=cc,
                                                   op=ALU.mult), [ps_src, tb], [ra])
            if "2" in RS:
                return
            x_hi = view(ps_src[:, 8:9], [(64, nh), (1, 8)])
            x_lo = view(ps_src[:, 0:1], [(64, nh), (1, 8)])
            s_neg = view(tbj[:, 144:145], [(0, nh), (1, 8)])
            s_pos = view(tbj[:, 152:153], [(0, nh), (1, 8)])
            tr.op("dve", lambda e: e.tensor_tensor(out=view(rb[:, 0:1], [(16, nh), (1, 8)]), in0=x_hi, in1=s_neg,
                                                   op=ALU.mult), [ps_src, tb], [rb])
            tr.op("dve", lambda e: e.tensor_tensor(out=view(rb[:, 8:9], [(16, nh), (1, 8)]), in0=x_lo, in1=s_pos,
                                                   op=ALU.mult), [ps_src, tb], [rb])
            if "3" in RS:
                return
            tr.op("pool", lambda e: e.tensor_tensor(out=view(dst[:, dst_off:dst_off + 1], [(64, nh), (1, 16)]),
                                                    in0=view(ra[:, 0:1], [(16, nh), (1, 16)]),
                                                    in1=view(rb[:, 0:1], [(16, nh), (1, 16)]), op=ALU.add),
                  [ra, rb], [dst])

        def transposes_to_stage(src, src_off, nchunk, psT2, stage, st_chunk0, j, TT):
            def tps(e):
                r = None
                for c in range(nchunk):
                    r = e.transpose(psT2[:, c * 128:(c + 1) * 128],
                                    src[:, src_off + c * 128:src_off + (c + 1) * 128], ident[:])
                return r
            tr.op("pe", tps, [src, ident], [psT2])
            tr.op("act", lambda e: e.activation(
                out=view(stage[:, st_chunk0 * TT + j * 128:st_chunk0 * TT + j * 128 + 1], [(TT, nchunk), (1, 128)]),
                in_=view(psT2[:, 0:1], [(128, nchunk), (1, 128)]), func=AF.Copy), [psT2], [stage])

        def phase_p1(l):
            even = (l % 2 == 0)
            li = l // 2
            NIN = EVEN_IN if even else ODD_IN
            TT = 512
            ph = Phase(tr, nc)
            Wb = ph.sb([128, 8 * NIN], BF16, name="Win")
            import os as _os6
            if not even:
                dummy2 = ph.sb([128, 8], F32, dma=True)
            gT = ph.sb([128, 8], F32, dma=True)
            tr.dma(gT[:], gpreT_mix[l], gT, True)
            load_weight(ph, Wb, (w_in_even if even else w_in_odd)[li], 8, NIN, gT=gT, nch=(1152 if even else 768))
            NQC = 8
            NKC = 5 if even else 2
            NV = 640 if even else 256
            xts = [ph.sb([128, 4 * D], F32, dma=True, name="xt") for _ in range(2)]
            tbs = [ph.sb([128, 4 * 160], F32, dma=True, name="tb") for _ in range(2)]
            QTst = [ph.sb([128, NQC * TT], BF16, dma=True, name="QTst") for _ in range(2)]
            KTst = [ph.sb([128, NKC * TT], BF16, dma=True, name="KTst") for _ in range(2)]
            Vst = [ph.sb([128, 4 * NV], BF16, dma=True, name="Vst") for _ in range(2)]
            hT = ph.sb([128, 8 * TT], BF16, name="hT")
            hb = ph.sb([128, D], BF16, name="hb")
            junk = ph.sb([128, D], BF16, name="junk")
            ss = ph.sb([128, 4], F32)
            rstd = ph.sb([128, 4], F32)
            ra = ph.sb([128, 256], F32)
            rb = ph.sb([128, 256], F32)
            psT = ph.ps([128, 1024], BF16, name="psT")
            if even:
                gq = ph.sb([128, 640], F32, dma=True)
                tr.dma(gq[:], gqk[li], gq, True)
                sqs = ph.sb([128, 640], F32)
                ssh = ph.sb([128, 10], F32)
                rsh = ph.sb([128, 10], F32)
                tq = ph.sb([128, 640], F32)
                ta = ph.sb([128, 640], F32)
                tbb = ph.sb([128, 640], F32)
                qkb = ph.sb([128, 640], BF16)
                qbb = ph.sb([128, 512], BF16)
                kbb = ph.sb([128, 512], BF16)
                psA = ph.ps([128, 1024], F32, name="psA")
                psQB = ph.ps([128, 512], F32, name="psQB")
                psKB = ph.ps([128, 512], F32, name="psKB")
                psVB = ph.ps([128, 512], F32, name="psVB")
                psT2a = ph.ps([128, 1024], BF16, name="psT2a")
                psT2b = ph.ps([128, 1024], BF16, name="psT2b")
            else:
                import os as _os5
                if _os5.environ.get("KDUMMY"):
                    dummy = ph.sb([128, int(_os5.environ["KDUMMY"])], F32, dma=True)
                qb16 = ph.sb([128, 1024], BF16)
                kb16 = ph.sb([128, 256], BF16)
                psQ0 = ph.ps([128, 512], F32, name="psQ0")
                psQ1 = ph.ps([128, 512], F32, name="psQ1")
                psKV = ph.ps([128, 512], F32, name="psKV")
                psT2a = ph.ps([128, 1024], BF16, name="psT2a")
                psT2b = ph.ps([128, 1024], BF16, name="psT2b")

            qfs = [ph.sb([128, 512], F32, name="qf") for _ in range(2)]
            qfc = [0]

            def evac_rope(ps_buf, ps_off, ncols, nh, dst, dst_off, tb, j):
                qf = qfs[qfc[0] % 2]
                qfc[0] += 1
                tr.op("act", lambda e: e.activation(out=qf[:, 0:ncols], in_=ps_buf[:, ps_off:ps_off + ncols],
                                                    func=AF.Copy), [ps_buf], [qf])
                tr.op("pool", lambda e: e.tensor_copy(out=dst[:, dst_off:dst_off + ncols], in_=qf[:, 0:ncols]),
                      [qf], [dst])
                rope_small(qf, nh, tb, j, ra, rb, dst, dst_off)

            tiles = [(t0 + i * TT, t0, i * TT) for (t0, S) in seqs for i in range(S // TT)]

            def issue_loads(ti):
                g0, t0, off = tiles[ti]
                xt = xts[ti % 2]
                import os as _os3
                tr.dma(view(xt[:, 0:1], [(D, 4), (1, D)]),
                       x_rows(l == 0 or bool(_os3.environ.get("KXIN")), g0, TT).rearrange("(j p) d -> p j d", p=128), xt, True)
                tb = tbs[ti % 2]
                tr.dma(view(tb[:, 0:1], [(160, 4), (1, 160)]),
                       tab[off:off + TT, :].rearrange("(j p) d -> p j d", p=128), tb, True)

            def mm_group(ps_buf, ps_off, j, col0, ncols):
                def f(e):
                    r = None
                    for c in range(8):
                        r = e.matmul(ps_buf[:, ps_off:ps_off + ncols],
                                     lhsT=hT[:, c * TT + j * 128:c * TT + (j + 1) * 128],
                                     rhs=Wb[:, c * NIN + col0:c * NIN + col0 + ncols],
                                     start=(c == 0), stop=(c == 7))
                    return r
                return f

            issue_loads(0)
            for ti in range(len(tiles)):
                g0, t0, off = tiles[ti]
                if ti + 1 < len(tiles):
                    issue_loads(ti + 1)
                xt = xts[ti % 2]
                tb = tbs[ti % 2]
                qst, kst, vst = QTst[ti % 2], KTst[ti % 2], Vst[ti % 2]
                for j in range(4):
                    norm_transpose(xt, j, ss, rstd, junk, hb, psT, hT, TT)
                    tbj = tb[:, j * 160:(j + 1) * 160]
                    if even:
                        tr.op("pe", mm_group(psA, 0, j, 0, 512), [hT, Wb], [psA])
                        tr.op("pe", mm_group(psA, 512, j, 512, 256), [hT, Wb], [psA])
                        tr.op("pe", mm_group(psQB, 0, j, 768, 512), [hT, Wb], [psQB])
                        tr.op("pe", mm_group(psKB, 0, j, 1280, 512), [hT, Wb], [psKB])
                        tr.op("pe", mm_group(psVB, 0, j, 1792, 512), [hT, Wb], [psVB])
                        tr.op("act", lambda e: e.activation(out=sqs[:], in_=psA[:, 0:640], func=AF.Square),
                              [psA], [sqs])
                        tr.op("dve", lambda e: e.tensor_reduce(out=ssh[:], in_=view(sqs[:, 0:1], [(64, 10), (1, 64)]),
                                                               axis=AX.X, op=ALU.add), [sqs], [ssh])
                        rstd_from_ss(ssh, ssh[:], rsh, rsh[:], 1.0 / HD)
                        tr.op("dve", lambda e: e.tensor_tensor(
                            out=view(tq[:, 0:1], [(64, 10), (1, 64)]), in0=view(psA[:, 0:1], [(64, 10), (1, 64)]),
                            in1=view(rsh[:, 0:1], [(1, 10), (0, 64)]), op=ALU.mult), [psA, rsh], [tq])
                        tr.op("act", lambda e: e.activation(out=vst[:, j * 640:j * 640 + 128], in_=psA[:, 640:768],
                                                            func=AF.Copy), [psA], [vst])
                        tr.op("pool", lambda e: e.tensor_tensor(out=tq[:], in0=tq[:], in1=gq[:], op=ALU.mult),
                              [tq, gq], [tq])
                        tr.op("pool", lambda e: e.tensor_tensor(
                            out=view(ta[:, 0:1], [(64, 10), (1, 64)]), in0=view(tq[:, 0:1], [(64, 10), (1, 64)]),
                            in1=view(tbj[:, 0:1], [(0, 10), (1, 64)]), op=ALU.mult), [tq, tb], [ta])
                        tr.op("pool", lambda e: e.tensor_tensor(
                            out=view(tbb[:, 0:1], [(64, 10), (32, 2), (1, 16)]),
                            in0=view(tq[:, 16:17], [(64, 10), (32, 2), (1, 16)]),
                            in1=view(tbj[:, 64:65], [(0, 10), (32, 2), (1, 16)]), op=ALU.mult), [tq, tb], [tbb])
                        tr.op("pool", lambda e: e.tensor_tensor(
                            out=view(tbb[:, 16:17], [(64, 10), (32, 2), (1, 16)]),
                            in0=view(tq[:, 0:1], [(64, 10), (32, 2), (1, 16)]),
                            in1=view(tbj[:, 80:81], [(0, 10), (32, 2), (1, 16)]), op=ALU.mult), [tq, tb], [tbb])
                        tr.op("pool", lambda e: e.tensor_tensor(out=qkb[:], in0=ta[:], in1=tbb[:], op=ALU.add),
                              [ta, tbb], [qkb])
                        transposes_to_stage(qkb, 0, 4, psT2a, qst, 0, j, TT)
                        transposes_to_stage(qkb, 512, 1, psT2b, kst, 0, j, TT)
                        evac_rope(psQB, 0, 512, 8, qbb, 0, tb, j)
                        transposes_to_stage(qbb, 0, 4, psT2a, qst, 4, j, TT)
                        evac_rope(psKB, 0, 512, 8, kbb, 0, tb, j)
                        transposes_to_stage(kbb, 0, 4, psT2b, kst, 1, j, TT)
                        tr.op("act", lambda e: e.activation(out=vst[:, j * 640 + 128:(j + 1) * 640],
                                                            in_=psVB[:, 0:512], func=AF.Copy), [psVB], [vst])
                    else:
                        import os as _os
                        SK = _os.environ.get("KSKIP", "")
                        tr.op("pe", mm_group(psQ0, 0, j, 0, 512), [hT, Wb], [psQ0])
                        tr.op("pe", mm_group(psQ1, 0, j, 512, 512), [hT, Wb], [psQ1])
                        tr.op("pe", mm_group(psKV, 0, j, 1024, 512), [hT, Wb], [psKV])
                        evac_rope(psQ0, 0, 512, 8, qb16, 0, tb, j)
                        evac_rope(psQ1, 0, 512, 8, qb16, 512, tb, j)
                        if "c" not in SK:
                            transposes_to_stage(qb16, 0, 8, psT2a, qst, 0, j, TT)
                        evac_rope(psKV, 0, 256, 4, kb16, 0, tb, j)
                        if "e" not in SK:
                            transposes_to_stage(kb16, 0, 2, psT2b, kst, 0, j, TT)
                        tr.op("act", lambda e: e.activation(out=vst[:, j * 256:(j + 1) * 256], in_=psKV[:, 256:512],
                                                            func=AF.Copy), [psKV], [vst])
                import os as _os2
                SK2 = _os2.environ.get("KSKIP", "")
                if "q" not in SK2:
                    tr.dma(QTd[0:NQC, :, g0:g0 + TT].rearrange("c p s -> p c s"),
                           view(qst[:, 0:1], [(TT, NQC), (1, TT)]), qst, False)
                if "k" not in SK2:
                    tr.dma(KTd[0:NKC, :, g0:g0 + TT].rearrange("c p s -> p c s"),
                           view(kst[:, 0:1], [(TT, NKC), (1, TT)]), kst, False)
                if "v" not in SK2:
                    tr.dma(Vd[g0:g0 + TT, 0:NV].rearrange("(j p) d -> p j d", p=128),
                           view(vst[:, 0:1], [(NV, 4), (1, NV)]), vst, False)
            ph.close()

        def attn_ac(ph, seq, K2, Vg, qchunk, window, esink, es_cols, bufs):
            t0, S = seq
            (QTs, ps_s, pts, ps_o, ps_bc, recs, bcs, OTst, cnt, pend) = bufs
            nk = S // 128
            nq = S // 512
            for qt in range(nq):
                QT = QTs[cnt[0] % len(QTs)]
                cnt[0] += 1
                tr.dma(QT[:], QTd[qchunk, :, t0 + qt * 512:t0 + (qt + 1) * 512], QT, True)
                if window:
                    kcs = [kc for kc in range(qt * 4 - 1, qt * 4 + 5) if 0 <= kc < nk]
                else:
                    kcs = list(range(nk))

                def qk(i, kc):
                    pss = ps_s[i % 2]
                    def f(e):
                        e.matmul(pss[:, 0:512], lhsT=K2[0][:, kc * 128:(kc + 1) * 128], rhs=QT[:, :],
                                 start=True, stop=True)
                        return e.matmul(pss[:, 512:1024], lhsT=K2[1][:, kc * 128:(kc + 1) * 128],
                                        rhs=QT[:, :], start=True, stop=True)
                    tr.op("pe", f, [K2[0], K2[1], QT], [pss])

                def expo(i, kc):
                    pss = ps_s[i % 2]
                    pt = pts[i % 3]
                    tr.op("act", lambda e: e.activation(out=pt[:], in_=pss[:], func=AF.Exp, scale=SCALE),
                          [pss], [pt])
                    if window:
                        jm = kc - qt * 4 + 1
                        tr.op("dve", lambda e: e.tensor_tensor(
                            out=view(pt[:, 0:1], [(512, 2), (1, 512)]), in0=view(pt[:, 0:1], [(512, 2), (1, 512)]),
                            in1=view(mask_b[:, jm * 512:jm * 512 + 1], [(0, 2), (1, 512)]), op=ALU.mult),
                            [pt, mask_b], [pt])

                def pv(i, kc):
                    pt = pts[i % 3]
                    def f(e):
                        e.matmul(ps_o[0][0:65, :], lhsT=Vg[:, kc * 65:(kc + 1) * 65], rhs=pt[:, 0:512],
                                 start=(i == 0), stop=(i == len(kcs) - 1))
                        return e.matmul(ps_o[1][0:65, :], lhsT=Vg[:, kc * 65:(kc + 1) * 65], rhs=pt[:, 512:1024],
                                        start=(i == 0), stop=(i == len(kcs) - 1))
                    tr.op("pe", f, [Vg, pt], [ps_o[0], ps_o[1]])

                for i, kc in enumerate(kcs):
                    qk(i, kc)
                    expo(i, kc)
                    if i == 1 and pend:
                        pend.pop(0)()
                    if i > 0:
                        pv(i - 1, kcs[i - 1])
                pv(len(kcs) - 1, kcs[-1])
                for hh in range(2):
                    po = ps_o[hh]
                    rec = recs[hh]
                    if esink is not None:
                        col = es_cols[hh]
                        tr.op("dve", lambda e: e.tensor_scalar(out=rec[64:65, :], in0=po[64:65, :],
                                                               scalar1=esink[64:65, col:col + 1], scalar2=None,
                                                               op0=ALU.add), [po, esink], [rec])
                        tr.op("dve", lambda e: e.reciprocal(out=rec[64:65, :], in_=rec[64:65, :]), [rec], [rec])
                    else:
                        tr.op("dve", lambda e: e.reciprocal(out=rec[64:65, :], in_=po[64:65, :]), [po], [rec])

                def fin(qt=qt):
                    for hh in range(2):
                        po = ps_o[hh]
                        rec = recs[hh]
                        tr.op("pe", lambda e: e.matmul(ps_bc[0:64, :], lhsT=ones_f[64:65, 0:64], rhs=rec[64:65, :],
                                                       start=True, stop=True), [ones_f, rec], [ps_bc])
                        tr.op("act", lambda e: e.activation(out=bcs[0:64, :], in_=ps_bc[0:64, :], func=AF.Copy),
                              [ps_bc], [bcs])
                        ot = OTst[cnt[1] % len(OTst)]
                        cnt[1] += 1
                        tr.op("dve", lambda e: e.tensor_tensor(out=ot[0:64, :], in0=po[0:64, :], in1=bcs[0:64, :],
                                                               op=ALU.mult), [po, bcs], [ot])
                        tr.dma(OTd[qchunk, hh * 64:(hh + 1) * 64, t0 + qt * 512:t0 + (qt + 1) * 512], ot[0:64, :],
                               ot, False)
                pend.append(fin)

        def alloc_k2(ph, Smax):
            k2 = [ph.sb([128, Smax], BF16, dma=True, name="K2") for _ in range(2)]
            tr.op("pool", lambda e: e.memset(k2[0][64:128, :], 0.0), [], [k2[0]])
            tr.op("pool", lambda e: e.memset(k2[1][0:64, :], 0.0), [], [k2[1]])
            return k2

        def alloc_ac(ph):
            QTs = [ph.sb([128, 512], BF16, dma=True, name="QT") for _ in range(3)]
            ps_s = [ph.ps([128, 1024], F32, name="ps_s") for _ in range(2)]
            pts = [ph.sb([128, 1024], BF16, name="pt") for _ in range(3)]
            ps_o = [ph.ps([128, 512], F32, name="ps_o") for _ in range(2)]
            ps_bc = ph.ps([128, 512], F32, name="ps_bc")
            recs = [ph.sb([128, 512], F32, name="rec") for _ in range(2)]
            bcs = ph.sb([128, 512], F32, name="bcs")
            OTst = [ph.sb([128, 512], BF16, dma=True, name="OTst") for _ in range(4)]
            return (QTs, ps_s, pts, ps_o, ps_bc, recs, bcs, OTst, [0, 0], [])

        def load_kv_ac(seq, K2, Vg, kchunk, khalf, vcol):
            t0, S = seq
            nk = S // 128
            for half in range(2):
                tr.dma(K2[half][half * 64:(half + 1) * 64, 0:S], KTd[kchunk, khalf * 64:(khalf + 1) * 64, t0:t0 + S],
                       K2[half], True)
            tr.dma(view(Vg[:, 0:1], [(65, nk), (1, 64)]),
                   Vd[t0:t0 + S, vcol:vcol + 64].rearrange("(k p) d -> p k d", p=128), Vg, True)
            tr.op("pool", lambda e: e.memset(view(Vg[:, 64:65], [(65, nk), (1, 1)]), 1.0), [], [Vg])

        def phase_p2_even_a(l):
            ph = Phase(tr, nc)
            Smax = max(Sp, Ss)
            K2 = alloc_k2(ph, Smax)
            Vg = ph.sb([128, (Smax // 128) * 65], BF16, dma=True, name="Vg")
            bufs = alloc_ac(ph)
            for seq in seqs:
                for g in range(2):
                    load_kv_ac(seq, K2, Vg, 0, g, g * 64)
                    for pair in range(2):
                        attn_ac(ph, seq, K2, Vg, g * 2 + pair, False, None, None, bufs)
            while bufs[9]:
                bufs[9].pop(0)()
            ph.close()

        def phase_p2_odd(l):
            li = l // 2
            ph = Phase(tr, nc)
            Smax = max(Sp, Ss)
            K2 = alloc_k2(ph, Smax)
            Vg = ph.sb([128, (Smax // 128) * 65], BF16, dma=True, name="Vg")
            esink = ph.sb([128, 16], F32, dma=True, name="esink")
            tr.dma(esink[64:65, :], sinkc[li:li + 1, :], esink, True)
            tr.op("act", lambda e: e.activation(out=esink[64:65, :], in_=esink[64:65, :], func=AF.Exp),
                  [esink], [esink])
            bufs = alloc_ac(ph)
            for seq in seqs:
                for hk in range(4):
                    load_kv_ac(seq, K2, Vg, hk // 2, hk % 2, hk * 64)
                    for pair in range(2):
                        qc = hk * 2 + pair
                        attn_ac(ph, seq, K2, Vg, qc, True, esink, (2 * qc, 2 * qc + 1), bufs)
            while bufs[9]:
                bufs[9].pop(0)()
            ph.close()

        def phase_p2_even_b(l):
            li = l // 2
            c_out = 1.0 - lam_init_of(l)
            ph = Phase(tr, nc)
            Smax = max(Sp, Ss)
            K2 = alloc_k2(ph, Smax)
            VB = ph.sb([128, Smax], BF16, dma=True, name="VB")
            QTs = [ph.sb([128, 512], BF16, dma=True, name="QT") for _ in range(3)]
            ps_s = [ph.ps([128, 1024], F32, name="ps_s") for _ in range(2)]
            pts = [ph.sb([128, 1024], BF16, name="pt") for _ in range(3)]
            ps_o = [ph.ps([128, 512], F32, name="ps_o") for _ in range(2)]
            ps_d = [ph.ps([128, 512], F32, name="ps_d") for _ in range(2)]
            r0 = ph.sb([128, 512], F32)
            r1 = ph.sb([128, 512], F32)
            fa = ph.sb([128, 512], F32)
            fb = ph.sb([128, 512], F32)
            fo = ph.sb([128, 512], F32)
            fsq = ph.sb([128, 512], F32)
            frs = ph.sb([128, 512], F32)
            OTst = [ph.sb([128, 512], BF16, dma=True, name="OTst") for _ in range(2)]
            dl = ph.sb([128, 256], F32, dma=True)
            tr.dma(dl[:], dlam[li], dl, True)
            pr = ph.sb([128, 128], F32)
            sm = ph.sb([128, 2], F32)
            ex = ph.sb([128, 2], F32)
            nlam = ph.sb([128, 1], F32)
            tr.op("dve", lambda e: e.tensor_tensor(out=view(pr[:, 0:1], [(64, 2), (1, 64)]),
                                                   in0=view(dl[:, 0:1], [(128, 2), (1, 64)]),
                                                   in1=view(dl[:, 64:65], [(128, 2), (1, 64)]), op=ALU.mult), [dl], [pr])
            tr.op("dve", lambda e: e.tensor_reduce(out=sm[:], in_=view(pr[:, 0:1], [(64, 2), (1, 64)]), axis=AX.X,
                                                   op=ALU.add), [pr], [sm])
            tr.op("act", lambda e: e.activation(out=ex[:], in_=sm[:], func=AF.Exp), [sm], [ex])
            tr.op("dve", lambda e: e.tensor_tensor(out=nlam[:], in0=ex[:, 1:2], in1=ex[:, 0:1], op=ALU.subtract),
                  [ex], [nlam])
            tr.op("dve", lambda e: e.tensor_scalar(out=nlam[:], in0=nlam[:], scalar1=-lam_init_of(l), scalar2=None,
                                                   op0=ALU.add), [nlam], [nlam])
            qcnt = 0
            ocntB = [0]
            pendB = []
            accA = ph.sb([128, 512], F32, name="accA")
            accB = ph.sb([128, 512], F32, name="accB")
            for (t0, S) in seqs:
                nk = S // 128
                nq = S // 512
                for h in range(4):
                    for half in range(2):
                        tr.dma(K2[half][half * 64:(half + 1) * 64, 0:S], KTd[1 + h, half * 64:(half + 1) * 64, t0:t0 + S],
                               K2[half], True)
                    tr.dma(view(VB[:, 0:1], [(128, nk), (1, 128)]),
                           Vd[t0:t0 + S, 128 + h * 128:128 + (h + 1) * 128].rearrange("(k p) d -> p k d", p=128),
                           VB, True)
                    for qt in range(nq):
                        QT = QTs[qcnt % 3]
                        qcnt += 1
                        tr.dma(QT[:], QTd[4 + h, :, t0 + qt * 512:t0 + (qt + 1) * 512], QT, True)

                        def qk(i):
                            pss = ps_s[i % 2]
                            def f(e):
                                e.matmul(pss[:, 0:512], lhsT=K2[0][:, i * 128:(i + 1) * 128], rhs=QT[:, :],
                                         start=True, stop=True)
                                return e.matmul(pss[:, 512:1024], lhsT=K2[1][:, i * 128:(i + 1) * 128],
                                                rhs=QT[:, :], start=True, stop=True)
                            tr.op("pe", f, [K2[0], K2[1], QT], [pss])

                        def expo(i):
                            pss = ps_s[i % 2]
                            pt = pts[i % 3]
                            tr.op("act", lambda e: e.activation(out=pt[:], in_=pss[:], func=AF.Exp, scale=SCALE),
                                  [pss], [pt])

                        def pv(i):
                            pt = pts[i % 3]
                            st, sp_ = (i == 0), (i == nk - 1)
                            def f(e):
                                e.matmul(ps_o[0][:, :], lhsT=VB[:, i * 128:(i + 1) * 128], rhs=pt[:, 0:512],
                                         start=st, stop=sp_)
                                return e.matmul(ps_o[1][:, :], lhsT=VB[:, i * 128:(i + 1) * 128],
                                                rhs=pt[:, 512:1024], start=st, stop=sp_)
                            tr.op("pe", f, [VB, pt], [ps_o[0], ps_o[1]])

                        def dacc(i):
                            pt = pts[i % 3]
                            if i == 0:
                                tr.op("dve", lambda e: e.tensor_copy(out=accA[:], in_=pt[:, 0:512]), [pt], [accA])
                                tr.op("pool", lambda e: e.tensor_copy(out=accB[:], in_=pt[:, 512:1024]), [pt], [accB])
                            else:
                                tr.op("dve", lambda e: e.tensor_tensor(out=accA[:], in0=accA[:], in1=pt[:, 0:512],
                                                                       op=ALU.add), [pt, accA], [accA])
                                tr.op("pool", lambda e: e.tensor_tensor(out=accB[:], in0=accB[:], in1=pt[:, 512:1024],
                                                                        op=ALU.add), [pt, accB], [accB])

                        for i in range(nk):
                            qk(i)
                            expo(i)
                            dacc(i)
                            if pendB and i == pendB[0][0]:
                                pendB.pop(0)[1]()
                            if i > 0:
                                pv(i - 1)
                        pv(nk - 1)

                        tr.op("pe", lambda e: e.matmul(ps_d[0][:, :], lhsT=ones_f[:, :], rhs=accA[:], start=True,
                                                       stop=True), [ones_f, accA], [ps_d[0]])
                        tr.op("pe", lambda e: e.matmul(ps_d[1][:, :], lhsT=ones_f[:, :], rhs=accB[:], start=True,
                                                       stop=True), [ones_f, accB], [ps_d[1]])

                        def fin1():
                            tr.op("dve", lambda e: e.reciprocal(out=r0[:], in_=ps_d[0][:, :]), [ps_d[0]], [r0])
                            tr.op("dve", lambda e: e.reciprocal(out=r1[:], in_=ps_d[1][:, :]), [ps_d[1]], [r1])
                            tr.op("dve", lambda e: e.tensor_tensor(out=fa[:], in0=ps_o[0][:, :], in1=r0[:],
                                                                   op=ALU.mult), [ps_o[0], r0], [fa])
                            tr.op("dve", lambda e: e.tensor_tensor(out=fb[:], in0=ps_o[1][:, :], in1=r1[:],
                                                                   op=ALU.mult), [ps_o[1], r1], [fb])
                            tr.op("dve", lambda e: e.scalar_tensor_tensor(out=fo[:], in0=fb[:], scalar=nlam[:, 0:1],
                                                                          in1=fa[:], op0=ALU.mult, op1=ALU.add),
                                  [fb, fa, nlam], [fo])
                            tr.op("pool", lambda e: e.tensor_tensor(out=fsq[:], in0=fo[:], in1=fo[:], op=ALU.mult),
                                  [fo], [fsq])

                        def fin2(h=h, t0=t0, qt=qt):
                            tr.op("pe", lambda e: e.matmul(ps_d[0][:, :], lhsT=ones_f[:, :], rhs=fsq[:], start=True,
                                                           stop=True), [ones_f, fsq], [ps_d[0]])
                            rstd_from_ss(ps_d[0], ps_d[0][:, :], frs, frs[:], 1.0 / 128)
                            ot = OTst[ocntB[0] % 2]
                            ocntB[0] += 1
                            tr.op("dve", lambda e: e.scalar_tensor_tensor(out=ot[:], in0=fo[:], scalar=c_out,
                                                                          in1=frs[:], op0=ALU.mult, op1=ALU.mult),
                                  [fo, frs], [ot])
                            tr.dma(OTd[4 + h, :, t0 + qt * 512:t0 + (qt + 1) * 512], ot[:], ot, False)
                        pendB.append((1, fin1))
                        pendB.append((3, fin2))
            while pendB:
                pendB.pop(0)[1]()
            ph.close()

        def phase_p3a(l):
            even = (l % 2 == 0)
            li = l // 2
            TT = 512
            ph = Phase(tr, nc)
            Wb = ph.sb([128, 8 * D], BF16, name="Wout")
            load_weight(ph, Wb, (w_out_even if even else w_out_odd)[li], 8, D, nch=1024)
            gpost = ph.sb([128, D], F32, dma=True)
            tr.dma(gpost[:], gpost_mix[l], gpost, True)
            xts = [ph.sb([128, 4 * D], F32, dma=True, name="xt") for _ in range(2)]
            OTs = [ph.sb([128, 8 * TT], BF16, dma=True, name="OT") for _ in range(2)]
            junk = ph.sb([128, D], BF16)
            tmp = ph.sb([128, D], F32)
            ss2 = ph.sb([128, 4], F32)
            rstd2 = ph.sb([128, 4], F32)
            ps_m = [ph.ps([128, 1024], F32, name="ps_m") for _ in range(2)]
            tiles = [t0 + i * TT for (t0, S) in seqs for i in range(S // TT)]

            def issue_loads(ti):
                g0 = tiles[ti]
                xt = xts[ti % 2]
                tr.dma(view(xt[:, 0:1], [(D, 4), (1, D)]),
                       x_rows(l == 0, g0, TT).rearrange("(j p) d -> p j d", p=128), xt, True)
                ot = OTs[ti % 2]
                tr.dma(view(ot[:, 0:1], [(TT, 8), (1, TT)]), OTd[:, :, g0:g0 + TT].rearrange("c p s -> p c s"),
                       ot, True)

            issue_loads(0)
            k = 0
            for ti in range(len(tiles)):
                g0 = tiles[ti]
                if ti + 1 < len(tiles):
                    issue_loads(ti + 1)
                xt = xts[ti % 2]
                ot = OTs[ti % 2]
                for j in range(4):
                    pm = ps_m[k % 2]
                    k += 1
                    def f(e):
                        r = None
                        for half in range(2):
                            for c in range(8):
                                r = e.matmul(pm[:, half * 512:(half + 1) * 512],
                                             lhsT=ot[:, c * TT + j * 128:c * TT + (j + 1) * 128],
                                             rhs=Wb[:, c * D + half * 512:c * D + (half + 1) * 512],
                                             start=(c == 0), stop=(c == 7))
                        return r
                    tr.op("pe", f, [ot, Wb], [pm])
                    post_residual(pm, xt, j, ss2, rstd2, junk, tmp, gpost)
                tr.dma(x_rows(False, g0, TT).rearrange("(j p) d -> p j d", p=128),
                       view(xt[:, 0:1], [(D, 4), (1, D)]), xt, False)
            ph.close()

        def phase_p3b(l):
            TT = 256
            NJ = TT // 128
            ph = Phase(tr, nc)
            Wgu = ph.sb([128, 8 * 2 * FFN_H], BF16, name="Wgu")
            Wd = ph.sb([128, NJH * D], BF16, name="Wd")
            gT = ph.sb([128, 8], F32, dma=True)
            tr.dma(gT[:], gpreT_ffn[l], gT, True)
            load_weight(ph, Wgu, w_gate_up[l], 8, 2 * FFN_H, gT=gT)
            load_weight(ph, Wd, w_down[l], NJH, D, nch=1024)
            gpost = ph.sb([128, D], F32, dma=True)
            tr.dma(gpost[:], gpost_ffn[l], gpost, True)
            xts = [ph.sb([128, NJ * D], F32, dma=True, name="xt") for _ in range(2)]
            hT = ph.sb([128, 8 * TT], BF16, name="hT")
            aT = ph.sb([128, NJH * TT], BF16, name="aT")
            hb = ph.sb([128, D], BF16)
            junk = ph.sb([128, D], BF16)
            tmp = ph.sb([128, D], F32)
            sg = [ph.sb([128, TT], F32, name="sg") for _ in range(2)]
            ss = ph.sb([128, 4], F32)
            rstd = ph.sb([128, 4], F32)
            ss2 = ph.sb([128, 4], F32)
            rstd2 = ph.sb([128, 4], F32)
            psT = ph.ps([128, 1024], BF16, name="psT")
            ps_gu = [ph.ps([128, 512], F32, name="ps_gu") for _ in range(3)]
            ps_y = [ph.ps([128, 1024], F32, name="ps_y") for _ in range(2)]
            tiles = [t0 + i * TT for (t0, S) in seqs for i in range(S // TT)]

            def issue_loads(ti):
                g0 = tiles[ti]
                xt = xts[ti % 2]
                tr.dma(view(xt[:, 0:1], [(D, NJ), (1, D)]),
                       x_rows(False, g0, TT).rearrange("(j p) d -> p j d", p=128), xt, True)

            issue_loads(0)
            kg = 0
            ky = 0
            for ti in range(len(tiles)):
                g0 = tiles[ti]
                if ti + 1 < len(tiles):
                    issue_loads(ti + 1)
                xt = xts[ti % 2]
                for j in range(NJ):
                    norm_transpose(xt, j, ss, rstd, junk, hb, psT, hT, TT)
                for jh in range(NJH):
                    pg = ps_gu[kg % 3]
                    sgb = sg[kg % 2]
                    kg += 1
                    def f(e):
                        r = None
                        for gu in range(2):
                            col = gu * FFN_H + jh * 128
                            for c in range(8):
                                r = e.matmul(pg[:, gu * TT:(gu + 1) * TT],
                                             lhsT=Wgu[:, c * 2 * FFN_H + col:c * 2 * FFN_H + col + 128],
                                             rhs=hT[:, c * TT:(c + 1) * TT], start=(c == 0), stop=(c == 7))
                        return r
                    tr.op("pe", f, [Wgu, hT], [pg])
                    tr.op("act", lambda e: e.activation(out=sgb[:], in_=pg[:, 0:TT], func=AF.Silu), [pg], [sgb])
                    tr.op("dve", lambda e: e.tensor_tensor(out=aT[:, jh * TT:(jh + 1) * TT], in0=pg[:, TT:2 * TT],
                                                           in1=sgb[:], op=ALU.mult), [pg, sgb], [aT])
                for j in range(NJ):
                    py = ps_y[ky % 2]
                    ky += 1
                    def f2(e):
                        r = None
                        for half in range(2):
                            for jh in range(NJH):
                                r = e.matmul(py[:, half * 512:(half + 1) * 512],
                                             lhsT=aT[:, jh * TT + j * 128:jh * TT + (j + 1) * 128],
                                             rhs=Wd[:, jh * D + half * 512:jh * D + (half + 1) * 512],
                                             start=(jh == 0), stop=(jh == NJH - 1))
                        return r
                    tr.op("pe", f2, [aT, Wd], [py])
                    post_residual(py, xt, j, ss2, rstd2, junk, tmp, gpost)
                tr.dma(x_rows(False, g0, TT).rearrange("(j p) d -> p j d", p=128),
                       view(xt[:, 0:1], [(D, NJ), (1, D)]), xt, False)
            ph.close()

        tr.barrier()
        import os
        dbg = os.environ.get("KPHASES")
        for l in range(depth):
            if os.environ.get("KLAYERS") and str(l) not in os.environ["KLAYERS"]:
                continue
            dbg = os.environ.get("KPH%d" % l, os.environ.get("KPHASES"))
            if dbg is None or "1" in dbg:
                phase_p1(l)
            if l % 2 == 0:
                if dbg is None or "a" in dbg:
                    phase_p2_even_a(l)
                if dbg is None or "b" in dbg:
                    phase_p2_even_b(l)
            else:
                if dbg is None or "c" in dbg:
                    phase_p2_odd(l)
            if dbg is None or "3" in dbg:
                phase_p3a(l)
            if dbg is None or "4" in dbg:
                phase_p3b(l)
        gph.close()
    return nc


def _rope_table(S):
    t = np.arange(S)
    inv_ax = (10000.0 ** (-np.arange(0, 32, 2, dtype=np.float32) / 32)).astype(np.float32)
    row = (t // 64).astype(np.float32)[:, None] * inv_ax[None, :]
    col = (t % 64).astype(np.float32)[:, None] * inv_ax[None, :]
    inv_p = (500000.0 ** (-np.arange(0, 16, 2, dtype=np.float32) / 16)).astype(np.float32)
    ang = t.astype(np.float32)[:, None] * inv_p[None, :]
    tab = np.zeros((S, 160), np.float32)
    rc, rs, cc, cs = np.cos(row), np.sin(row), np.cos(col), np.sin(col)
    tab[:, 0:16] = rc; tab[:, 16:32] = rc; tab[:, 32:48] = cc; tab[:, 48:64] = cc
    tab[:, 64:80] = -rs; tab[:, 80:96] = rs; tab[:, 96:112] = -cs; tab[:, 112:128] = cs
    pc, ps = np.cos(ang), np.sin(ang)
    tab[:, 128:136] = pc; tab[:, 136:144] = pc
    tab[:, 144:152] = -ps; tab[:, 152:160] = ps
    return tab


def _mask_const():
    k = np.arange(128)[:, None]
    q = np.arange(512)[None, :]
    m = np.zeros((128, 6 * 512), np.float32)
    for jm in range(6):
        j = jm - 1
        m[:, jm * 512:(jm + 1) * 512] = (np.abs(q - (j * 128 + k)) <= 128).astype(np.float32)
    return m


_CACHE = {}


def run(inputs, Sp, Ss, n_cores, depth=DEPTH, trace=False):
    key = (Sp, Ss, depth)
    if key not in _CACHE:
        _CACHE[key] = build_program(Sp, Ss, depth)
    nc = _CACHE[key]
    f = lambda a: np.ascontiguousarray(np.asarray(a, dtype=np.float32))
    xpf, xsf = f(inputs["x_prompt"]), f(inputs["x_sample"])
    nbs = xsf.shape[0]
    qk = f(inputs["qk_norm_a"])
    gqk = np.stack([np.broadcast_to(np.concatenate([np.tile(qk[i, 0], 8), np.tile(qk[i, 1], 2)])[None, :], (128, 640))
                    for i in range(2)])
    dl = f(inputs["diff_lambda"]).reshape(2, 1, 256)
    shared = {
        "w_in_even": f(inputs["w_in_even"]), "w_out_even": f(inputs["w_out_even"]),
        "w_in_odd": f(inputs["w_in_odd"]), "w_out_odd": f(inputs["w_out_odd"]),
        "w_gate_up": f(inputs["w_gate_up"]), "w_down": f(inputs["w_down"]),
        "gpreT_mix": f(f(inputs["norm_mix_pre"]).reshape(4, 8, 128).transpose(0, 2, 1)),
        "gpreT_ffn": f(f(inputs["norm_ffn_pre"]).reshape(4, 8, 128).transpose(0, 2, 1)),
        "gpost_mix": f(np.broadcast_to(f(inputs["norm_mix_post"])[:, None, :], (4, 128, D))),
        "gpost_ffn": f(np.broadcast_to(f(inputs["norm_ffn_post"])[:, None, :], (4, 128, D))),
        "gqk": f(gqk), "dlam": f(np.broadcast_to(dl, (2, 128, 256))),
        "sinkc": f(inputs["sink_c"]),
        "tab": _rope_table(max(Sp, Ss)), "maskc": _mask_const(), "identc": np.eye(128, dtype=np.float32),
    }
    in_maps = []
    for i in range(n_cores):
        m = dict(shared)
        m["xp"] = xpf[i]
        m["xs"] = xsf[i % nbs]
        in_maps.append(m)
    res = run_bass_kernel_spmd(nc, in_maps, core_ids=list(range(n_cores)), trace=trace)
    yp = np.stack([res.results[i]["yp"] for i in range(n_cores)]).astype(np.float32)
    ysm = np.stack([res.results[i]["ys"] for i in range(nbs)]).astype(np.float32)
    return (yp, ysm), res


def kernel(**inputs):
    (yp, ysm), _ = run(inputs, 8192, 4096, NCORES)
    return (yp, ysm)
```

```python
import math
from contextlib import ExitStack

import numpy as np
import concourse.bass as bass
import concourse.mybir as mybir
from concourse.bass_utils import run_bass_kernel_spmd

F32 = mybir.dt.float32
BF16 = mybir.dt.bfloat16
ALU = mybir.AluOpType
AF = mybir.ActivationFunctionType
AX = mybir.AxisListType

D = 1024
DEPTH = 4
HD = 64
EPS = 1e-6
FFN_H = 2816
NJH = FFN_H // 128
EVEN_IN = 2304
ODD_IN = 1536
SCALE = HD ** -0.5
NCORES = 8


def lam_init_of(l):
    return 0.8 - 0.6 * math.exp(-0.3 * l)


class Buf:
    __slots__ = ("t", "lw", "rd", "dsem")

    def __init__(self, t, dsem=None):
        self.t = t
        self.lw = None
        self.rd = {}
        self.dsem = dsem

    def __getitem__(self, k):
        return self.t[k]


def view(ap, dims):
    return bass.AP(ap.tensor, ap.offset, [list(ap.ap[0])] + [list(d) for d in dims])


class TR:
    def __init__(self, nc, es, n_dma_sems=48):
        self.nc = nc
        self.E = {"pe": nc.tensor, "act": nc.scalar, "dve": nc.vector, "pool": nc.gpsimd, "sp": nc.sync}
        self.sems = {}
        self.cnt = {}
        for e in ("pe", "act", "dve", "pool"):
            self.sems[e] = es.enter_context(nc.semaphore("s_" + e))
            self.cnt[e] = 0
        self.free_dsems = []
        for i in range(n_dma_sems):
            n = "d%d" % i
            self.sems[n] = es.enter_context(nc.semaphore("s_" + n))
            self.cnt[n] = 0
            self.free_dsems.append(n)
        self.waited = {e: {} for e in self.E}
        self.inflight = {}
        import os
        self.max_inflight = int(os.environ.get("KINFLIGHT", "3"))
        self.nopool = bool(os.environ.get("KNOPOOL"))

    def _wait(self, eng, toks):
        E = self.E[eng]
        w = self.waited[eng]
        for s, v in toks.items():
            if eng == "pe" and s == "pe":
                continue
            if w.get(s, 0) < v:
                E.wait_ge(self.sems[s], v)
                w[s] = v

    @staticmethod
    def _collect(reads, writes):
        toks = {}
        for b in reads:
            if b.lw is not None:
                s, v = b.lw
                if toks.get(s, 0) < v:
                    toks[s] = v
        for b in writes:
            if b.lw is not None:
                s, v = b.lw
                if toks.get(s, 0) < v:
                    toks[s] = v
            for s, v in b.rd.items():
                if toks.get(s, 0) < v:
                    toks[s] = v
        return toks

    def op(self, eng, fn, reads=(), writes=()):
        if eng == "pool" and self.nopool:
            eng = "dve"
        self._wait(eng, self._collect(reads, writes))
        inst = fn(self.E[eng])
        self.cnt[eng] += 1
        inst.then_inc(self.sems[eng], 1)
        tok = (eng, self.cnt[eng])
        for b in writes:
            b.lw = tok
            b.rd = {}
        for b in reads:
            if b.rd.get(eng, 0) < tok[1]:
                b.rd[eng] = tok[1]
        return tok

    def dma(self, out_ap, in_ap, buf, load, q="sp"):
        if load:
            toks = self._collect((), (buf,))
        else:
            toks = self._collect((buf,), ())
        fl = self.inflight.setdefault(q, [])
        while len(fl) >= self.max_inflight:
            s0, v0 = fl.pop(0)
            if toks.get(s0, 0) < v0:
                toks[s0] = v0
        self._wait(q, toks)
        s = buf.dsem
        inst = self.E[q].dma_start(out=out_ap, in_=in_ap)
        self.cnt[s] += 16
        inst.then_inc(self.sems[s], 16)
        tok = (s, self.cnt[s])
        fl.append(tok)
        if load:
            buf.lw = tok
            buf.rd = {}
        else:
            buf.rd[s] = tok[1]
        return tok

    def barrier(self):
        for e in self.E:
            self._wait(e, dict(self.cnt))


_UID = [0]


class Phase:
    def __init__(self, tr, nc):
        self.tr = tr
        self.nc = nc
        self.es = ExitStack()
        self.dsems = []
        self.k = 0

    def sb(self, shape, dt, dma=False, name=None):
        _UID[0] += 1
        t = self.es.enter_context(self.nc.sbuf_tensor("%s_%d" % (name or "sb", _UID[0]), list(shape), dt))
        ds = None
        if dma:
            ds = self.tr.free_dsems.pop()
            self.dsems.append(ds)
        return Buf(t, ds)

    def ps(self, shape, dt, name=None):
        _UID[0] += 1
        t = self.es.enter_context(self.nc.psum_tensor("%s_%d" % (name or "ps", _UID[0]), list(shape), dt))
        return Buf(t)

    def close(self):
        self.tr.barrier()
        self.es.close()
        self.tr.free_dsems.extend(self.dsems)
        self.dsems = []


def build_program(Sp, Ss, depth=DEPTH):
    T = Sp + Ss
    seqs = [(0, Sp), (Sp, Ss)]
    nc = bass.Bass("TRN2", target_bir_lowering=False)

    def din(name, shape, dt=F32):
        return nc.dram_tensor(name, list(shape), dt, kind="ExternalInput").ap()

    xp = din("xp", [Sp, D])
    xs = din("xs", [Ss, D])
    w_in_even = din("w_in_even", [2, D, EVEN_IN])
    w_out_even = din("w_out_even", [2, D, D])
    w_in_odd = din("w_in_odd", [2, D, ODD_IN])
    w_out_odd = din("w_out_odd", [2, D, D])
    w_gate_up = din("w_gate_up", [4, D, 2 * FFN_H])
    w_down = din("w_down", [4, FFN_H, D])
    gpreT_mix = din("gpreT_mix", [4, 128, 8])
    gpreT_ffn = din("gpreT_ffn", [4, 128, 8])
    gpost_mix = din("gpost_mix", [4, 128, D])
    gpost_ffn = din("gpost_ffn", [4, 128, D])
    gqk = din("gqk", [2, 128, 640])
    dlam = din("dlam", [2, 128, 256])
    sinkc = din("sinkc", [2, 16])
    tab = din("tab", [max(Sp, Ss), 160])
    maskc = din("maskc", [128, 6 * 512])
    identc = din("identc", [128, 128])
    yp = nc.dram_tensor("yp", [Sp, D], F32, kind="ExternalOutput").ap()
    ys = nc.dram_tensor("ys", [Ss, D], F32, kind="ExternalOutput").ap()
    QTd = nc.dram_tensor("QTd", [8, 128, T], BF16).ap()
    KTd = nc.dram_tensor("KTd", [5, 128, T], BF16).ap()
    Vd = nc.dram_tensor("Vd", [T, 640], BF16).ap()
    OTd = nc.dram_tensor("OTd", [8, 128, T], BF16).ap()

    def x_rows(src_is_input, t0, n):
        if t0 < Sp:
            base = xp if src_is_input else yp
            return base[t0:t0 + n, :]
        base = xs if src_is_input else ys
        return base[t0 - Sp:t0 - Sp + n, :]

    with ExitStack() as es:
        tr = TR(nc, es)
        gph = Phase(tr, nc)
        ident_f = gph.sb([128, 128], F32, dma=True)
        ident = gph.sb([128, 128], BF16)
        ones_f = gph.sb([128, 128], F32)
        ones_b = gph.sb([128, 128], BF16)
        mask_f = gph.sb([128, 6 * 512], F32, dma=True)
        mask_b = gph.sb([128, 6 * 512], BF16)
        tr.dma(ident_f[:], identc[:, :], ident_f, True)
        tr.op("dve", lambda e: e.tensor_copy(out=ident[:], in_=ident_f[:]), [ident_f], [ident])
        tr.op("pool", lambda e: e.memset(ones_f[:], 1.0), [], [ones_f])
        tr.op("pool", lambda e: e.memset(ones_b[:], 1.0), [], [ones_b])
        eps_t = gph.sb([128, 1], F32)
        tr.op("pool", lambda e: e.memset(eps_t[:], EPS), [], [eps_t])
        tr.dma(mask_f[:], maskc[:, :], mask_f, True)
        tr.op("dve", lambda e: e.tensor_copy(out=mask_b[:], in_=mask_f[:]), [mask_f], [mask_b])

        def load_weight(ph, Wb, Wd, C, N, gT=None, nch=1408):
            sub = Phase(tr, nc)
            stg = [sub.sb([128, nch], F32, dma=True, name="wst") for _ in range(3)]
            k = 0
            for c in range(C):
                for n0 in range(0, N, nch):
                    w = min(nch, N - n0)
                    st = stg[k % 3]
                    tr.dma(st[:, 0:w], Wd[c * 128:(c + 1) * 128, n0:n0 + w], st, True)
                    dst = Wb[:, c * N + n0:c * N + n0 + w]
                    if gT is not None:
                        if k % 2 == 0:
                            tr.op("dve", lambda e: e.tensor_scalar(
                                out=dst, in0=st[:, 0:w], scalar1=gT[:, c:c + 1], scalar2=None, op0=ALU.mult),
                                [st, gT], [Wb])
                        else:
                            tr.op("act", lambda e: e.activation(out=dst, in_=st[:, 0:w], func=AF.Copy,
                                                                scale=gT[:, c:c + 1]), [st, gT], [Wb])
                    else:
                        eng = ("dve", "pool")[k % 2]
                        tr.op(eng, lambda e: e.tensor_copy(out=dst, in_=st[:, 0:w]), [st], [Wb])
                    k += 1
            sub.close()

        def rstd_from_ss(eng_ss_buf, ss_ap, out_buf, out_ap, inv_n):
            tr.op("act", lambda e: e.activation(out=out_ap, in_=ss_ap, func=AF.Ln, scale=inv_n, bias=eps_t[:, 0:1]),
                  [eng_ss_buf, eps_t], [out_buf])
            tr.op("act", lambda e: e.activation(out=out_ap, in_=out_ap, func=AF.Exp, scale=-0.5),
                  [out_buf], [out_buf])

        def norm_transpose(xt, j, ss, rstd, junk, hb, psT, hT, TT, jo=None):
            if jo is None:
                jo = j
            xj = xt[:, j * D:(j + 1) * D]
            tr.op("act", lambda e: e.activation(out=junk[:], in_=xj, func=AF.Square, accum_out=ss[:, j:j + 1]),
                  [xt], [junk, ss])
            rstd_from_ss(ss, ss[:, j:j + 1], rstd, rstd[:, j:j + 1], 1.0 / D)
            tr.op("dve", lambda e: e.tensor_scalar(out=hb[:], in0=xj, scalar1=rstd[:, j:j + 1], scalar2=None,
                                                   op0=ALU.mult), [xt, rstd], [hb])

            def tps(e):
                r = None
                for c in range(8):
                    r = e.transpose(psT[:, c * 128:(c + 1) * 128], hb[:, c * 128:(c + 1) * 128], ident[:])
                return r
            tr.op("pe", tps, [hb, ident], [psT])
            tr.op("act", lambda e: e.activation(
                out=view(hT[:, jo * 128:jo * 128 + 1], [(TT, 8), (1, 128)]),
                in_=view(psT[:, 0:1], [(128, 8), (1, 128)]), func=AF.Copy), [psT], [hT])

        def post_residual(ps_y, xt, j, ss2, rstd2, junk, tmp, gpost):
            xj = xt[:, j * D:(j + 1) * D]
            tr.op("act", lambda e: e.activation(out=junk[:], in_=ps_y[:, 0:D], func=AF.Square,
                                                accum_out=ss2[:, j:j + 1]), [ps_y], [junk, ss2])
            rstd_from_ss(ss2, ss2[:, j:j + 1], rstd2, rstd2[:, j:j + 1], 1.0 / D)
            tr.op("dve", lambda e: e.scalar_tensor_tensor(out=tmp[:], in0=ps_y[:, 0:D], scalar=rstd2[:, j:j + 1],
                                                          in1=gpost[:], op0=ALU.mult, op1=ALU.mult),
                  [ps_y, rstd2, gpost], [tmp])
            tr.op("pool", lambda e: e.tensor_tensor(out=xj, in0=xj, in1=tmp[:], op=ALU.add), [xt, tmp], [xt])

        def rope_small(ps_src, nh, tb, j, ra, rb, dst, dst_off):
            import os as _os4
            RS = _os4.environ.get("KROPE", "")
            tbj = tb[:, j * 160:(j + 1) * 160]
            x_all = view(ps_src[:, 0:1], [(64, nh), (1, 16)])
            cc = view(tbj[:, 128:129], [(0, nh), (1, 16)])
            if "1" in RS:
                return
            tr.op("dve", lambda e: e.tensor_tensor(out=view(ra[:, 0:1], [(16, nh), (1, 16)]), in0=x_all, in1=cc,
                                                   op=ALU.mult), [ps_src, tb], [ra])
            if "2" in RS:
                return
            x_hi = view(ps_src[:, 8:9], [(64, nh), (1, 8)])
            x_lo = view(ps_src[:, 0:1], [(64, nh), (1, 8)])
            s_neg = view(tbj[:, 144:145], [(0, nh), (1, 8)])
            s_pos = view(tbj[:, 152:153], [(0, nh), (1, 8)])
            tr.op("dve", lambda e: e.tensor_tensor(out=view(rb[:, 0:1], [(16, nh), (1, 8)]), in0=x_hi, in1=s_neg,
                                                   op=ALU.mult), [ps_src, tb], [rb])
            tr.op("dve", lambda e: e.tensor_tensor(out=view(rb[:, 8:9], [(16, nh), (1, 8)]), in0=x_lo, in1=s_pos,
                                                   op=ALU.mult), [ps_src, tb], [rb])
            if "3" in RS:
                return
            tr.op("pool", lambda e: e.tensor_tensor(out=view(dst[:, dst_off:dst_off + 1], [(64, nh), (1, 16)]),
                                                    in0=view(ra[:, 0:1], [(16, nh), (1, 16)]),
                                                    in1=view(rb[:, 0:1], [(16, nh), (1, 16)]), op=ALU.add),
                  [ra, rb], [dst])

        def transposes_to_stage(src, src_off, nchunk, psT2, stage, st_chunk0, j, TT):
            def tps(e):
                r = None
                for c in range(nchunk):
                    r = e.transpose(psT2[:, c * 128:(c + 1) * 128],
                                    src[:, src_off + c * 128:src_off + (c + 1) * 128], ident[:])
                return r
            tr.op("pe", tps, [src, ident], [psT2])
            tr.op("act", lambda e: e.activation(
                out=view(stage[:, st_chunk0 * TT + j * 128:st_chunk0 * TT + j * 128 + 1], [(TT, nchunk), (1, 128)]),
                in_=view(psT2[:, 0:1], [(128, nchunk), (1, 128)]), func=AF.Copy), [psT2], [stage])

        def phase_p1(l):
            even = (l % 2 == 0)
            li = l // 2
            NIN = EVEN_IN if even else ODD_IN
            TT = 512
            ph = Phase(tr, nc)
            Wb = ph.sb([128, 8 * NIN], BF16, name="Win")
            import os as _os6
            if not even:
                dummy2 = ph.sb([128, 8], F32, dma=True)
            gT = ph.sb([128, 8], F32, dma=True)
            tr.dma(gT[:], gpreT_mix[l], gT, True)
            load_weight(ph, Wb, (w_in_even if even else w_in_odd)[li], 8, NIN, gT=gT, nch=(1152 if even else 768))
            NQC = 8
            NKC = 5 if even else 2
            NV = 640 if even else 256
            xts = [ph.sb([128, 4 * D], F32, dma=True, name="xt") for _ in range(2)]
            tbs = [ph.sb([128, 4 * 160], F32, dma=True, name="tb") for _ in range(2)]
            QTst = [ph.sb([128, NQC * TT], BF16, dma=True, name="QTst") for _ in range(2)]
            KTst = [ph.sb([128, NKC * TT], BF16, dma=True, name="KTst") for _ in range(2)]
            Vst = [ph.sb([128, 4 * NV], BF16, dma=True, name="Vst") for _ in range(2)]
            hTs = [ph.sb([128, 8 * 128], BF16, name="hT") for _ in range(2)]
            hb = ph.sb([128, D], BF16, name="hb")
            junk = ph.sb([128, D], BF16, name="junk")
            ss = ph.sb([128, 4], F32)
            rstd = ph.sb([128, 4], F32)
            ra = ph.sb([128, 256], F32)
            rb = ph.sb([128, 256], F32)
            psT = ph.ps([128, 1024], BF16, name="psT")
            if even:
                gq = ph.sb([128, 640], F32, dma=True)
                tr.dma(gq[:], gqk[li], gq, True)
                sqs = ph.sb([128, 640], F32)
                ssh = ph.sb([128, 10], F32)
                rsh = ph.sb([128, 10], F32)
                tq = ph.sb([128, 640], F32)
                ta = ph.sb([128, 640], F32)
                tbb = ph.sb([128, 640], F32)
                qkb = ph.sb([128, 640], BF16)
                qbb = ph.sb([128, 512], BF16)
                kbb = ph.sb([128, 512], BF16)
                psA = ph.ps([128, 1024], F32, name="psA")
                psQB = ph.ps([128, 512], F32, name="psQB")
                psKB = ph.ps([128, 512], F32, name="psKB")
                psVB = ph.ps([128, 512], F32, name="psVB")
                psT2a = ph.ps([128, 1024], BF16, name="psT2a")
                psT2b = ph.ps([128, 1024], BF16, name="psT2b")
            else:
                import os as _os5
                if _os5.environ.get("KDUMMY"):
                    dummy = ph.sb([128, int(_os5.environ["KDUMMY"])], F32, dma=True)
                qb16 = ph.sb([128, 1024], BF16)
                kb16 = ph.sb([128, 256], BF16)
                psQ0 = ph.ps([128, 512], F32, name="psQ0")
                psQ1 = ph.ps([128, 512], F32, name="psQ1")
                psKV = ph.ps([128, 512], F32, name="psKV")
                psT2a = ph.ps([128, 1024], BF16, name="psT2a")
                psT2b = ph.ps([128, 1024], BF16, name="psT2b")

            qfs = [ph.sb([128, 512], F32, name="qf") for _ in range(2)]
            qfc = [0]

            def evac_rope(ps_buf, ps_off, ncols, nh, dst, dst_off, tb, j):
                qf = qfs[qfc[0] % 2]
                qfc[0] += 1
                tr.op("act", lambda e: e.activation(out=qf[:, 0:ncols], in_=ps_buf[:, ps_off:ps_off + ncols],
                                                    func=AF.Copy), [ps_buf], [qf])
                tr.op("pool", lambda e: e.tensor_copy(out=dst[:, dst_off:dst_off + ncols], in_=qf[:, 0:ncols]),
                      [qf], [dst])
                rope_small(qf, nh, tb, j, ra, rb, dst, dst_off)

            tiles = [(t0 + i * TT, t0, i * TT) for (t0, S) in seqs for i in range(S // TT)]

            def issue_loads(ti):
                g0, t0, off = tiles[ti]
                xt = xts[ti % 2]
                import os as _os3
                tr.dma(view(xt[:, 0:1], [(D, 4), (1, D)]),
                       x_rows(l == 0 or bool(_os3.environ.get("KXIN")), g0, TT).rearrange("(j p) d -> p j d", p=128), xt, True)
                tb = tbs[ti % 2]
                tr.dma(view(tb[:, 0:1], [(160, 4), (1, 160)]),
                       tab[off:off + TT, :].rearrange("(j p) d -> p j d", p=128), tb, True)

            hcur = [None]

            def mm_group(ps_buf, ps_off, j, col0, ncols):
                hT = hcur[0]
                def f(e):
                    r = None
                    for c in range(8):
                        r = e.matmul(ps_buf[:, ps_off:ps_off + ncols],
                                     lhsT=hT[:, c * 128:(c + 1) * 128],
                                     rhs=Wb[:, c * NIN + col0:c * NIN + col0 + ncols],
                                     start=(c == 0), stop=(c == 7))
                    return r
                return f

            def chain(n):
                ti_, j_ = n // 4, n % 4
                norm_transpose(xts[ti_ % 2], j_, ss, rstd, junk, hb, psT, hTs[n % 2], 128, jo=0)

            issue_loads(0)
            if len(tiles) > 1:
                issue_loads(1)
            chain(0)
            for ti in range(len(tiles)):
                g0, t0, off = tiles[ti]
                xt = xts[ti % 2]
                tb = tbs[ti % 2]
                qst, kst, vst = QTst[ti % 2], KTst[ti % 2], Vst[ti % 2]
                for j in range(4):
                    n = ti * 4 + j
                    if n + 1 < 4 * len(tiles):
                        chain(n + 1)
                    hT = hTs[n % 2]
                    hcur[0] = hT
                    tbj = tb[:, j * 160:(j + 1) * 160]
                    if even:
                        tr.op("pe", mm_group(psA, 0, j, 0, 512), [hT, Wb], [psA])
                        tr.op("pe", mm_group(psA, 512, j, 512, 256), [hT, Wb], [psA])
                        tr.op("pe", mm_group(psQB, 0, j, 768, 512), [hT, Wb], [psQB])
                        tr.op("pe", mm_group(psKB, 0, j, 1280, 512), [hT, Wb], [psKB])
                        tr.op("pe", mm_group(psVB, 0, j, 1792, 512), [hT, Wb], [psVB])
                        tr.op("act", lambda e: e.activation(out=sqs[:], in_=psA[:, 0:640], func=AF.Square),
                              [psA], [sqs])
                        tr.op("dve", lambda e: e.tensor_reduce(out=ssh[:], in_=view(sqs[:, 0:1], [(64, 10), (1, 64)]),
                                                               axis=AX.X, op=ALU.add), [sqs], [ssh])
                        rstd_from_ss(ssh, ssh[:], rsh, rsh[:], 1.0 / HD)
                        tr.op("dve", lambda e: e.tensor_tensor(
                            out=view(tq[:, 0:1], [(64, 10), (1, 64)]), in0=view(psA[:, 0:1], [(64, 10), (1, 64)]),
                            in1=view(rsh[:, 0:1], [(1, 10), (0, 64)]), op=ALU.mult), [psA, rsh], [tq])
                        tr.op("act", lambda e: e.activation(out=vst[:, j * 640:j * 640 + 128], in_=psA[:, 640:768],
                                                            func=AF.Copy), [psA], [vst])
                        tr.op("pool", lambda e: e.tensor_tensor(out=tq[:], in0=tq[:], in1=gq[:], op=ALU.mult),
                              [tq, gq], [tq])
                        tr.op("pool", lambda e: e.tensor_tensor(
                            out=view(ta[:, 0:1], [(64, 10), (1, 64)]), in0=view(tq[:, 0:1], [(64, 10), (1, 64)]),
                            in1=view(tbj[:, 0:1], [(0, 10), (1, 64)]), op=ALU.mult), [tq, tb], [ta])
                        tr.op("pool", lambda e: e.tensor_tensor(
                            out=view(tbb[:, 0:1], [(64, 10), (32, 2), (1, 16)]),
                            in0=view(tq[:, 16:17], [(64, 10), (32, 2), (1, 16)]),
                            in1=view(tbj[:, 64:65], [(0, 10), (32, 2), (1, 16)]), op=ALU.mult), [tq, tb], [tbb])
                        tr.op("pool", lambda e: e.tensor_tensor(
                            out=view(tbb[:, 16:17], [(64, 10), (32, 2), (1, 16)]),
                            in0=view(tq[:, 0:1], [(64, 10), (32, 2), (1, 16)]),
                            in1=view(tbj[:, 80:81], [(0, 10), (32, 2), (1, 16)]), op=ALU.mult), [tq, tb], [tbb])
                        tr.op("pool", lambda e: e.tensor_tensor(out=qkb[:], in0=ta[:], in1=tbb[:], op=ALU.add),
                              [ta, tbb], [qkb])
                        transposes_to_stage(qkb, 0, 4, psT2a, qst, 0, j, TT)
                        transposes_to_stage(qkb, 512, 1, psT2b, kst, 0, j, TT)
                        evac_rope(psQB, 0, 512, 8, qbb, 0, tb, j)
                        transposes_to_stage(qbb, 0, 4, psT2a, qst, 4, j, TT)
                        evac_rope(psKB, 0, 512, 8, kbb, 0, tb, j)
                        transposes_to_stage(kbb, 0, 4, psT2b, kst, 1, j, TT)
                        tr.op("act", lambda e: e.activation(out=vst[:, j * 640 + 128:(j + 1) * 640],
                                                            in_=psVB[:, 0:512], func=AF.Copy), [psVB], [vst])
                    else:
                        import os as _os
                        SK = _os.environ.get("KSKIP", "")
                        tr.op("pe", mm_group(psQ0, 0, j, 0, 512), [hT, Wb], [psQ0])
                        tr.op("pe", mm_group(psQ1, 0, j, 512, 512), [hT, Wb], [psQ1])
                        tr.op("pe", mm_group(psKV, 0, j, 1024, 512), [hT, Wb], [psKV])
                        evac_rope(psQ0, 0, 512, 8, qb16, 0, tb, j)
                        evac_rope(psQ1, 0, 512, 8, qb16, 512, tb, j)
                        if "c" not in SK:
                            transposes_to_stage(qb16, 0, 8, psT2a, qst, 0, j, TT)
                        evac_rope(psKV, 0, 256, 4, kb16, 0, tb, j)
                        if "e" not in SK:
                            transposes_to_stage(kb16, 0, 2, psT2b, kst, 0, j, TT)
                        tr.op("act", lambda e: e.activation(out=vst[:, j * 256:(j + 1) * 256], in_=psKV[:, 256:512],
                                                            func=AF.Copy), [psKV], [vst])
                if ti + 2 < len(tiles):
                    issue_loads(ti + 2)
                import os as _os2
                SK2 = _os2.environ.get("KSKIP", "")
                if "q" not in SK2:
                    tr.dma(QTd[0:NQC, :, g0:g0 + TT].rearrange("c p s -> p c s"),
                           view(qst[:, 0:1], [(TT, NQC), (1, TT)]), qst, False)
                if "k" not in SK2:
                    tr.dma(KTd[0:NKC, :, g0:g0 + TT].rearrange("c p s -> p c s"),
                           view(kst[:, 0:1], [(TT, NKC), (1, TT)]), kst, False)
                if "v" not in SK2:
                    tr.dma(Vd[g0:g0 + TT, 0:NV].rearrange("(j p) d -> p j d", p=128),
                           view(vst[:, 0:1], [(NV, 4), (1, NV)]), vst, False)
            ph.close()

        def attn_ac(ph, seq, K2, Vg, qchunk, window, esink, es_cols, bufs):
            t0, S = seq
            (QTs, ps_s, pts, ps_o, ps_bc, recs, bcs, OTst, cnt, pend) = bufs
            nk = S // 128
            nq = S // 512
            for qt in range(nq):
                QT = QTs[cnt[0] % len(QTs)]
                cnt[0] += 1
                tr.dma(QT[:], QTd[qchunk, :, t0 + qt * 512:t0 + (qt + 1) * 512], QT, True)
                if window:
                    kcs = [kc for kc in range(qt * 4 - 1, qt * 4 + 5) if 0 <= kc < nk]
                else:
                    kcs = list(range(nk))

                def qk(i, kc):
                    pss = ps_s[i % 2]
                    def f(e):
                        e.matmul(pss[:, 0:512], lhsT=K2[0:64, kc * 128:(kc + 1) * 128], rhs=QT[0:64, :],
                                 start=True, stop=True)
                        return e.matmul(pss[:, 512:1024], lhsT=K2[64:128, kc * 128:(kc + 1) * 128],
                                        rhs=QT[64:128, :], start=True, stop=True)
                    tr.op("pe", f, [K2, QT], [pss])

                def expo(i, kc):
                    pss = ps_s[i % 2]
                    pt = pts[i % 3]
                    tr.op("act", lambda e: e.activation(out=pt[:], in_=pss[:], func=AF.Exp, scale=SCALE),
                          [pss], [pt])
                    if window:
                        jm = kc - qt * 4 + 1
                        tr.op("dve", lambda e: e.tensor_tensor(
                            out=view(pt[:, 0:1], [(512, 2), (1, 512)]), in0=view(pt[:, 0:1], [(512, 2), (1, 512)]),
                            in1=view(mask_b[:, jm * 512:jm * 512 + 1], [(0, 2), (1, 512)]), op=ALU.mult),
                            [pt, mask_b], [pt])

                def pv(i, kc):
                    pt = pts[i % 3]
                    def f(e):
                        e.matmul(ps_o[0][0:65, :], lhsT=Vg[:, kc * 65:(kc + 1) * 65], rhs=pt[:, 0:512],
                                 start=(i == 0), stop=(i == len(kcs) - 1))
                        return e.matmul(ps_o[1][0:65, :], lhsT=Vg[:, kc * 65:(kc + 1) * 65], rhs=pt[:, 512:1024],
                                        start=(i == 0), stop=(i == len(kcs) - 1))
                    tr.op("pe", f, [Vg, pt], [ps_o[0], ps_o[1]])

                for i, kc in enumerate(kcs):
                    qk(i, kc)
                    expo(i, kc)
                    if i == 1 and pend:
                        pend.pop(0)()
                    if i > 0:
                        pv(i - 1, kcs[i - 1])
                pv(len(kcs) - 1, kcs[-1])
                for hh in range(2):
                    po = ps_o[hh]
                    rec = recs[hh]
                    if esink is not None:
                        col = es_cols[hh]
                        tr.op("dve", lambda e: e.tensor_scalar(out=rec[64:65, :], in0=po[64:65, :],
                                                               scalar1=esink[64:65, col:col + 1], scalar2=None,
                                                               op0=ALU.add), [po, esink], [rec])
                        tr.op("dve", lambda e: e.reciprocal(out=rec[64:65, :], in_=rec[64:65, :]), [rec], [rec])
                    else:
                        tr.op("dve", lambda e: e.reciprocal(out=rec[64:65, :], in_=po[64:65, :]), [po], [rec])

                def fin(qt=qt):
                    for hh in range(2):
                        po = ps_o[hh]
                        rec = recs[hh]
                        tr.op("pe", lambda e: e.matmul(ps_bc[0:64, :], lhsT=ones_f[64:65, 0:64], rhs=rec[64:65, :],
                                                       start=True, stop=True), [ones_f, rec], [ps_bc])
                        tr.op("act", lambda e: e.activation(out=bcs[0:64, :], in_=ps_bc[0:64, :], func=AF.Copy),
                              [ps_bc], [bcs])
                        ot = OTst[cnt[1] % len(OTst)]
                        cnt[1] += 1
                        tr.op("dve", lambda e: e.tensor_tensor(out=ot[0:64, :], in0=po[0:64, :], in1=bcs[0:64, :],
                                                               op=ALU.mult), [po, bcs], [ot])
                        tr.dma(OTd[qchunk, hh * 64:(hh + 1) * 64, t0 + qt * 512:t0 + (qt + 1) * 512], ot[0:64, :],
                               ot, False)
                pend.append(fin)

        def alloc_ac(ph):
            QTs = [ph.sb([128, 512], BF16, dma=True, name="QT") for _ in range(3)]
            ps_s = [ph.ps([128, 1024], F32, name="ps_s") for _ in range(2)]
            pts = [ph.sb([128, 1024], BF16, name="pt") for _ in range(3)]
            ps_o = [ph.ps([128, 512], F32, name="ps_o") for _ in range(2)]
            ps_bc = ph.ps([128, 512], F32, name="ps_bc")
            recs = [ph.sb([128, 512], F32, name="rec") for _ in range(2)]
            bcs = ph.sb([128, 512], F32, name="bcs")
            OTst = [ph.sb([128, 512], BF16, dma=True, name="OTst") for _ in range(4)]
            return (QTs, ps_s, pts, ps_o, ps_bc, recs, bcs, OTst, [0, 0], [])

        def load_kv_ac(seq, K2, Vg, kchunk, khalf, vcol):
            t0, S = seq
            nk = S // 128
            for half in range(2):
                tr.dma(K2[half * 64:(half + 1) * 64, 0:S], KTd[kchunk, khalf * 64:(khalf + 1) * 64, t0:t0 + S], K2, True)
            tr.dma(view(Vg[:, 0:1], [(65, nk), (1, 64)]),
                   Vd[t0:t0 + S, vcol:vcol + 64].rearrange("(k p) d -> p k d", p=128), Vg, True)
            tr.op("pool", lambda e: e.memset(view(Vg[:, 64:65], [(65, nk), (1, 1)]), 1.0), [], [Vg])

        def phase_p2_even_a(l):
            ph = Phase(tr, nc)
            Smax = max(Sp, Ss)
            K2 = ph.sb([128, Smax], BF16, dma=True, name="K2")
            Vg = ph.sb([128, (Smax // 128) * 65], BF16, dma=True, name="Vg")
            bufs = alloc_ac(ph)
            for seq in seqs:
                for g in range(2):
                    load_kv_ac(seq, K2, Vg, 0, g, g * 64)
                    for pair in range(2):
                        attn_ac(ph, seq, K2, Vg, g * 2 + pair, False, None, None, bufs)
            while bufs[9]:
                bufs[9].pop(0)()
            ph.close()

        def phase_p2_odd(l):
            li = l // 2
            ph = Phase(tr, nc)
            Smax = max(Sp, Ss)
            K2 = ph.sb([128, Smax], BF16, dma=True, name="K2")
            Vg = ph.sb([128, (Smax // 128) * 65], BF16, dma=True, name="Vg")
            esink = ph.sb([128, 16], F32, dma=True, name="esink")
            tr.dma(esink[64:65, :], sinkc[li:li + 1, :], esink, True)
            tr.op("act", lambda e: e.activation(out=esink[64:65, :], in_=esink[64:65, :], func=AF.Exp),
                  [esink], [esink])
            bufs = alloc_ac(ph)
            for seq in seqs:
                for hk in range(4):
                    load_kv_ac(seq, K2, Vg, hk // 2, hk % 2, hk * 64)
                    for pair in range(2):
                        qc = hk * 2 + pair
                        attn_ac(ph, seq, K2, Vg, qc, True, esink, (2 * qc, 2 * qc + 1), bufs)
            while bufs[9]:
                bufs[9].pop(0)()
            ph.close()

        def phase_p2_even_b(l):
            li = l // 2
            c_out = 1.0 - lam_init_of(l)
            ph = Phase(tr, nc)
            Smax = max(Sp, Ss)
            K2 = ph.sb([128, Smax], BF16, dma=True, name="KB")
            VB = ph.sb([128, Smax], BF16, dma=True, name="VB")
            QTs = [ph.sb([128, 512], BF16, dma=True, name="QT") for _ in range(3)]
            ps_s = [ph.ps([128, 1024], F32, name="ps_s") for _ in range(2)]
            pts = [ph.sb([128, 1024], BF16, name="pt") for _ in range(3)]
            ps_o = [ph.ps([128, 512], F32, name="ps_o") for _ in range(2)]
            ps_d = [ph.ps([128, 512], F32, name="ps_d") for _ in range(2)]
            r0 = ph.sb([128, 512], F32)
            r1 = ph.sb([128, 512], F32)
            fa = ph.sb([128, 512], F32)
            fb = ph.sb([128, 512], F32)
            fo = ph.sb([128, 512], F32)
            fsq = ph.sb([128, 512], F32)
            frs = ph.sb([128, 512], F32)
            OTst = [ph.sb([128, 512], BF16, dma=True, name="OTst") for _ in range(2)]
            dl = ph.sb([128, 256], F32, dma=True)
            tr.dma(dl[:], dlam[li], dl, True)
            pr = ph.sb([128, 128], F32)
            sm = ph.sb([128, 2], F32)
            ex = ph.sb([128, 2], F32)
            nlam = ph.sb([128, 1], F32)
            tr.op("dve", lambda e: e.tensor_tensor(out=view(pr[:, 0:1], [(64, 2), (1, 64)]),
                                                   in0=view(dl[:, 0:1], [(128, 2), (1, 64)]),
                                                   in1=view(dl[:, 64:65], [(128, 2), (1, 64)]), op=ALU.mult), [dl], [pr])
            tr.op("dve", lambda e: e.tensor_reduce(out=sm[:], in_=view(pr[:, 0:1], [(64, 2), (1, 64)]), axis=AX.X,
                                                   op=ALU.add), [pr], [sm])
            tr.op("act", lambda e: e.activation(out=ex[:], in_=sm[:], func=AF.Exp), [sm], [ex])
            tr.op("dve", lambda e: e.tensor_tensor(out=nlam[:], in0=ex[:, 1:2], in1=ex[:, 0:1], op=ALU.subtract),
                  [ex], [nlam])
            tr.op("dve", lambda e: e.tensor_scalar(out=nlam[:], in0=nlam[:], scalar1=-lam_init_of(l), scalar2=None,
                                                   op0=ALU.add), [nlam], [nlam])
            qcnt = 0
            ocntB = [0]
            pendB = []
            accA = ph.sb([128, 512], F32, name="accA")
            accB = ph.sb([128, 512], F32, name="accB")
            for (t0, S) in seqs:
                nk = S // 128
                nq = S // 512
                for h in range(4):
                    tr.dma(K2[:, 0:S], KTd[1 + h, :, t0:t0 + S], K2, True)
                    tr.dma(view(VB[:, 0:1], [(128, nk), (1, 128)]),
                           Vd[t0:t0 + S, 128 + h * 128:128 + (h + 1) * 128].rearrange("(k p) d -> p k d", p=128),
                           VB, True)
                    for qt in range(nq):
                        QT = QTs[qcnt % 3]
                        qcnt += 1
                        tr.dma(QT[:], QTd[4 + h, :, t0 + qt * 512:t0 + (qt + 1) * 512], QT, True)

                        def qk(i):
                            pss = ps_s[i % 2]
                            def f(e):
                                e.matmul(pss[:, 0:512], lhsT=K2[0:64, i * 128:(i + 1) * 128], rhs=QT[0:64, :],
                                         start=True, stop=True)
                                return e.matmul(pss[:, 512:1024], lhsT=K2[64:128, i * 128:(i + 1) * 128],
                                                rhs=QT[64:128, :], start=True, stop=True)
                            tr.op("pe", f, [K2, QT], [pss])

                        def expo(i):
                            pss = ps_s[i % 2]
                            pt = pts[i % 3]
                            tr.op("act", lambda e: e.activation(out=pt[:], in_=pss[:], func=AF.Exp, scale=SCALE),
                                  [pss], [pt])

                        def pv(i):
                            pt = pts[i % 3]
                            st, sp_ = (i == 0), (i == nk - 1)
                            def f(e):
                                e.matmul(ps_o[0][:, :], lhsT=VB[:, i * 128:(i + 1) * 128], rhs=pt[:, 0:512],
                                         start=st, stop=sp_)
                                return e.matmul(ps_o[1][:, :], lhsT=VB[:, i * 128:(i + 1) * 128],
                                                rhs=pt[:, 512:1024], start=st, stop=sp_)
                            tr.op("pe", f, [VB, pt], [ps_o[0], ps_o[1]])

                        def dacc(i):
                            pt = pts[i % 3]
                            if i == 0:
                                tr.op("dve", lambda e: e.tensor_copy(out=accA[:], in_=pt[:, 0:512]), [pt], [accA])
                                tr.op("pool", lambda e: e.tensor_copy(out=accB[:], in_=pt[:, 512:1024]), [pt], [accB])
                            else:
                                tr.op("dve", lambda e: e.tensor_tensor(out=accA[:], in0=accA[:], in1=pt[:, 0:512],
                                                                       op=ALU.add), [pt, accA], [accA])
                                tr.op("pool", lambda e: e.tensor_tensor(out=accB[:], in0=accB[:], in1=pt[:, 512:1024],
                                                                        op=ALU.add), [pt, accB], [accB])

                        for i in range(nk):
                            qk(i)
                            expo(i)
                            dacc(i)
                            if pendB and i == pendB[0][0]:
                                pendB.pop(0)[1]()
                            if i > 0:
                                pv(i - 1)
                        pv(nk - 1)

                        tr.op("pe", lambda e: e.matmul(ps_d[0][:, :], lhsT=ones_f[:, :], rhs=accA[:], start=True,
                                                       stop=True), [ones_f, accA], [ps_d[0]])
                        tr.op("pe", lambda e: e.matmul(ps_d[1][:, :], lhsT=ones_f[:, :], rhs=accB[:], start=True,
                                                       stop=True), [ones_f, accB], [ps_d[1]])

                        def fin1():
                            tr.op("dve", lambda e: e.reciprocal(out=r0[:], in_=ps_d[0][:, :]), [ps_d[0]], [r0])
                            tr.op("dve", lambda e: e.reciprocal(out=r1[:], in_=ps_d[1][:, :]), [ps_d[1]], [r1])
                            tr.op("dve", lambda e: e.tensor_tensor(out=fa[:], in0=ps_o[0][:, :], in1=r0[:],
                                                                   op=ALU.mult), [ps_o[0], r0], [fa])
                            tr.op("dve", lambda e: e.tensor_tensor(out=fb[:], in0=ps_o[1][:, :], in1=r1[:],
                                                                   op=ALU.mult), [ps_o[1], r1], [fb])
                            tr.op("dve", lambda e: e.scalar_tensor_tensor(out=fo[:], in0=fb[:], scalar=nlam[:, 0:1],
                                                                          in1=fa[:], op0=ALU.mult, op1=ALU.add),
                                  [fb, fa, nlam], [fo])
                            tr.op("pool", lambda e: e.tensor_tensor(out=fsq[:], in0=fo[:], in1=fo[:], op=ALU.mult),
                                  [fo], [fsq])

                        def fin2(h=h, t0=t0, qt=qt):
                            tr.op("pe", lambda e: e.matmul(ps_d[0][:, :], lhsT=ones_f[:, :], rhs=fsq[:], start=True,
                                                           stop=True), [ones_f, fsq], [ps_d[0]])
                            rstd_from_ss(ps_d[0], ps_d[0][:, :], frs, frs[:], 1.0 / 128)
                            ot = OTst[ocntB[0] % 2]
                            ocntB[0] += 1
                            tr.op("dve", lambda e: e.scalar_tensor_tensor(out=ot[:], in0=fo[:], scalar=c_out,
                                                                          in1=frs[:], op0=ALU.mult, op1=ALU.mult),
                                  [fo, frs], [ot])
                            tr.dma(OTd[4 + h, :, t0 + qt * 512:t0 + (qt + 1) * 512], ot[:], ot, False)
                        pendB.append((1, fin1))
                        pendB.append((3, fin2))
            while pendB:
                pendB.pop(0)[1]()
            ph.close()

        def phase_p3a(l):
            even = (l % 2 == 0)
            li = l // 2
            TT = 512
            ph = Phase(tr, nc)
            Wb = ph.sb([128, 8 * D], BF16, name="Wout")
            load_weight(ph, Wb, (w_out_even if even else w_out_odd)[li], 8, D, nch=1024)
            gpost = ph.sb([128, D], F32, dma=True)
            tr.dma(gpost[:], gpost_mix[l], gpost, True)
            xts = [ph.sb([128, 4 * D], F32, dma=True, name="xt") for _ in range(2)]
            OTs = [ph.sb([128, 8 * TT], BF16, dma=True, name="OT") for _ in range(2)]
            junk = ph.sb([128, D], BF16)
            tmp = ph.sb([128, D], F32)
            ss2 = ph.sb([128, 4], F32)
            rstd2 = ph.sb([128, 4], F32)
            ps_m = [ph.ps([128, 1024], F32, name="ps_m") for _ in range(2)]
            tiles = [t0 + i * TT for (t0, S) in seqs for i in range(S // TT)]

            def issue_loads(ti):
                g0 = tiles[ti]
                xt = xts[ti % 2]
                tr.dma(view(xt[:, 0:1], [(D, 4), (1, D)]),
                       x_rows(l == 0, g0, TT).rearrange("(j p) d -> p j d", p=128), xt, True)
                ot = OTs[ti % 2]
                tr.dma(view(ot[:, 0:1], [(TT, 8), (1, TT)]), OTd[:, :, g0:g0 + TT].rearrange("c p s -> p c s"),
                       ot, True)

            issue_loads(0)
            k = 0
            for ti in range(len(tiles)):
                g0 = tiles[ti]
                if ti + 1 < len(tiles):
                    issue_loads(ti + 1)
                xt = xts[ti % 2]
                ot = OTs[ti % 2]
                for j in range(4):
                    pm = ps_m[k % 2]
                    k += 1
                    def f(e):
                        r = None
                        for half in range(2):
                            for c in range(8):
                                r = e.matmul(pm[:, half * 512:(half + 1) * 512],
                                             lhsT=ot[:, c * TT + j * 128:c * TT + (j + 1) * 128],
                                             rhs=Wb[:, c * D + half * 512:c * D + (half + 1) * 512],
                                             start=(c == 0), stop=(c == 7))
                        return r
                    tr.op("pe", f, [ot, Wb], [pm])
                    post_residual(pm, xt, j, ss2, rstd2, junk, tmp, gpost)
                tr.dma(x_rows(False, g0, TT).rearrange("(j p) d -> p j d", p=128),
                       view(xt[:, 0:1], [(D, 4), (1, D)]), xt, False)
            ph.close()

        def phase_p3b(l):
            TT = 256
            NJ = TT // 128
            ph = Phase(tr, nc)
            Wgu = ph.sb([128, 8 * 2 * FFN_H], BF16, name="Wgu")
            Wd = ph.sb([128, NJH * D], BF16, name="Wd")
            gT = ph.sb([128, 8], F32, dma=True)
            tr.dma(gT[:], gpreT_ffn[l], gT, True)
            load_weight(ph, Wgu, w_gate_up[l], 8, 2 * FFN_H, gT=gT)
            load_weight(ph, Wd, w_down[l], NJH, D, nch=1024)
            gpost = ph.sb([128, D], F32, dma=True)
            tr.dma(gpost[:], gpost_ffn[l], gpost, True)
            xts = [ph.sb([128, NJ * D], F32, dma=True, name="xt") for _ in range(2)]
            hTs = [ph.sb([128, 8 * TT], BF16, name="hT") for _ in range(2)]
            aT = ph.sb([128, NJH * TT], BF16, name="aT")
            hb = ph.sb([128, D], BF16)
            junk = ph.sb([128, D], BF16)
            tmp = ph.sb([128, D], F32)
            sg = [ph.sb([128, TT], F32, name="sg") for _ in range(2)]
            ss = ph.sb([128, 4], F32)
            rstd = ph.sb([128, 4], F32)
            ss2 = ph.sb([128, 4], F32)
            rstd2 = ph.sb([128, 4], F32)
            psT = ph.ps([128, 1024], BF16, name="psT")
            ps_gu = [ph.ps([128, 512], F32, name="ps_gu") for _ in range(3)]
            ps_y = [ph.ps([128, 1024], F32, name="ps_y") for _ in range(2)]
            tiles = [t0 + i * TT for (t0, S) in seqs for i in range(S // TT)]

            def issue_loads(ti):
                g0 = tiles[ti]
                xt = xts[ti % 2]
                tr.dma(view(xt[:, 0:1], [(D, NJ), (1, D)]),
                       x_rows(False, g0, TT).rearrange("(j p) d -> p j d", p=128), xt, True)

            def chain(ti_):
                for j_ in range(NJ):
                    norm_transpose(xts[ti_ % 2], j_, ss, rstd, junk, hb, psT, hTs[ti_ % 2], TT)

            issue_loads(0)
            if len(tiles) > 1:
                issue_loads(1)
            chain(0)
            kg = 0
            ky = 0
            for ti in range(len(tiles)):
                g0 = tiles[ti]
                xt = xts[ti % 2]
                hT = hTs[ti % 2]
                for jh in range(NJH):
                    pg = ps_gu[kg % 3]
                    sgb = sg[kg % 2]
                    kg += 1
                    def f(e):
                        r = None
                        for gu in range(2):
                            col = gu * FFN_H + jh * 128
                            for c in range(8):
                                r = e.matmul(pg[:, gu * TT:(gu + 1) * TT],
                                             lhsT=Wgu[:, c * 2 * FFN_H + col:c * 2 * FFN_H + col + 128],
                                             rhs=hT[:, c * TT:(c + 1) * TT], start=(c == 0), stop=(c == 7))
                        return r
                    tr.op("pe", f, [Wgu, hT], [pg])
                    tr.op("act", lambda e: e.activation(out=sgb[:], in_=pg[:, 0:TT], func=AF.Silu), [pg], [sgb])
                    tr.op("dve", lambda e: e.tensor_tensor(out=aT[:, jh * TT:(jh + 1) * TT], in0=pg[:, TT:2 * TT],
                                                           in1=sgb[:], op=ALU.mult), [pg, sgb], [aT])
                if ti + 1 < len(tiles):
                    chain(ti + 1)
                for j in range(NJ):
                    py = ps_y[ky % 2]
                    ky += 1
                    def f2(e):
                        r = None
                        for half in range(2):
                            for jh in range(NJH):
                                r = e.matmul(py[:, half * 512:(half + 1) * 512],
                                             lhsT=aT[:, jh * TT + j * 128:jh * TT + (j + 1) * 128],
                                             rhs=Wd[:, jh * D + half * 512:jh * D + (half + 1) * 512],
                                             start=(jh == 0), stop=(jh == NJH - 1))
                        return r
                    tr.op("pe", f2, [aT, Wd], [py])
                    post_residual(py, xt, j, ss2, rstd2, junk, tmp, gpost)
                tr.dma(x_rows(False, g0, TT).rearrange("(j p) d -> p j d", p=128),
                       view(xt[:, 0:1], [(D, NJ), (1, D)]), xt, False)
                if ti + 2 < len(tiles):
                    issue_loads(ti + 2)
            ph.close()

        tr.barrier()
        import os
        dbg = os.environ.get("KPHASES")
        for l in range(depth):
            if os.environ.get("KLAYERS") and str(l) not in os.environ["KLAYERS"]:
                continue
            dbg = os.environ.get("KPH%d" % l, os.environ.get("KPHASES"))
            if dbg is None or "1" in dbg:
                phase_p1(l)
            if l % 2 == 0:
                if dbg is None or "a" in dbg:
                    phase_p2_even_a(l)
                if dbg is None or "b" in dbg:
                    phase_p2_even_b(l)
            else:
                if dbg is None or "c" in dbg:
                    phase_p2_odd(l)
            if dbg is None or "3" in dbg:
                phase_p3a(l)
            if dbg is None or "4" in dbg:
                phase_p3b(l)
        gph.close()
    return nc


def _rope_table(S):
    t = np.arange(S)
    inv_ax = (10000.0 ** (-np.arange(0, 32, 2, dtype=np.float32) / 32)).astype(np.float32)
    row = (t // 64).astype(np.float32)[:, None] * inv_ax[None, :]
    col = (t % 64).astype(np.float32)[:, None] * inv_ax[None, :]
    inv_p = (500000.0 ** (-np.arange(0, 16, 2, dtype=np.float32) / 16)).astype(np.float32)
    ang = t.astype(np.float32)[:, None] * inv_p[None, :]
    tab = np.zeros((S, 160), np.float32)
    rc, rs, cc, cs = np.cos(row), np.sin(row), np.cos(col), np.sin(col)
    tab[:, 0:16] = rc; tab[:, 16:32] = rc; tab[:, 32:48] = cc; tab[:, 48:64] = cc
    tab[:, 64:80] = -rs; tab[:, 80:96] = rs; tab[:, 96:112] = -cs; tab[:, 112:128] = cs
    pc, ps = np.cos(ang), np.sin(ang)
    tab[:, 128:136] = pc; tab[:, 136:144] = pc
    tab[:, 144:152] = -ps; tab[:, 152:160] = ps
    return tab


def _mask_const():
    k = np.arange(128)[:, None]
    q = np.arange(512)[None, :]
    m = np.zeros((128, 6 * 512), np.float32)
    for jm in range(6):
        j = jm - 1
        m[:, jm * 512:(jm + 1) * 512] = (np.abs(q - (j * 128 + k)) <= 128).astype(np.float32)
    return m


_CACHE = {}


def run(inputs, Sp, Ss, n_cores, depth=DEPTH, trace=False):
    key = (Sp, Ss, depth)
    if key not in _CACHE:
        _CACHE[key] = build_program(Sp, Ss, depth)
    nc = _CACHE[key]
    f = lambda a: np.ascontiguousarray(np.asarray(a, dtype=np.float32))
    xpf, xsf = f(inputs["x_prompt"]), f(inputs["x_sample"])
    nbs = xsf.shape[0]
    qk = f(inputs["qk_norm_a"])
    gqk = np.stack([np.broadcast_to(np.concatenate([np.tile(qk[i, 0], 8), np.tile(qk[i, 1], 2)])[None, :], (128, 640))
                    for i in range(2)])
    dl = f(inputs["diff_lambda"]).reshape(2, 1, 256)
    shared = {
        "w_in_even": f(inputs["w_in_even"]), "w_out_even": f(inputs["w_out_even"]),
        "w_in_odd": f(inputs["w_in_odd"]), "w_out_odd": f(inputs["w_out_odd"]),
        "w_gate_up": f(inputs["w_gate_up"]), "w_down": f(inputs["w_down"]),
        "gpreT_mix": f(f(inputs["norm_mix_pre"]).reshape(4, 8, 128).transpose(0, 2, 1)),
        "gpreT_ffn": f(f(inputs["norm_ffn_pre"]).reshape(4, 8, 128).transpose(0, 2, 1)),
        "gpost_mix": f(np.broadcast_to(f(inputs["norm_mix_post"])[:, None, :], (4, 128, D))),
        "gpost_ffn": f(np.broadcast_to(f(inputs["norm_ffn_post"])[:, None, :], (4, 128, D))),
        "gqk": f(gqk), "dlam": f(np.broadcast_to(dl, (2, 128, 256))),
        "sinkc": f(inputs["sink_c"]),
        "tab": _rope_table(max(Sp, Ss)), "maskc": _mask_const(), "identc": np.eye(128, dtype=np.float32),
    }
    in_maps = []
    for i in range(n_cores):
        m = dict(shared)
        m["xp"] = xpf[i]
        m["xs"] = xsf[i % nbs]
        in_maps.append(m)
    res = run_bass_kernel_spmd(nc, in_maps, core_ids=list(range(n_cores)), trace=trace)
    yp = np.stack([res.results[i]["yp"] for i in range(n_cores)]).astype(np.float32)
    ysm = np.stack([res.results[i]["ys"] for i in range(nbs)]).astype(np.float32)
    return (yp, ysm), res


def kernel(**inputs):
    (yp, ysm), _ = run(inputs, 8192, 4096, NCORES)
    return (yp, ysm)
```

```python
import math
from contextlib import ExitStack

import numpy as np
import concourse.bass as bass
import concourse.mybir as mybir
from concourse.bass_utils import run_bass_kernel_spmd

F32 = mybir.dt.float32
BF16 = mybir.dt.bfloat16
ALU = mybir.AluOpType
AF = mybir.ActivationFunctionType
AX = mybir.AxisListType

D = 1024
DEPTH = 4
HD = 64
EPS = 1e-6
FFN_H = 2816
NJH = FFN_H // 128
EVEN_IN = 2304
ODD_IN = 1536
SCALE = HD ** -0.5
NCORES = 8


def lam_init_of(l):
    return 0.8 - 0.6 * math.exp(-0.3 * l)


class Buf:
    __slots__ = ("t", "lw", "rd", "dsem")

    def __init__(self, t, dsem=None):
        self.t = t
        self.lw = None
        self.rd = {}
        self.dsem = dsem

    def __getitem__(self, k):
        return self.t[k]


def view(ap, dims):
    return bass.AP(ap.tensor, ap.offset, [list(ap.ap[0])] + [list(d) for d in dims])


class TR:
    def __init__(self, nc, es, n_dma_sems=48):
        self.nc = nc
        self.E = {"pe": nc.tensor, "act": nc.scalar, "dve": nc.vector, "pool": nc.gpsimd, "sp": nc.sync}
        self.sems = {}
        self.cnt = {}
        for e in ("pe", "act", "dve", "pool"):
            self.sems[e] = es.enter_context(nc.semaphore("s_" + e))
            self.cnt[e] = 0
        self.free_dsems = []
        for i in range(n_dma_sems):
            n = "d%d" % i
            self.sems[n] = es.enter_context(nc.semaphore("s_" + n))
            self.cnt[n] = 0
            self.free_dsems.append(n)
        self.waited = {e: {} for e in self.E}
        self.inflight = {}
        import os
        self.max_inflight = int(os.environ.get("KINFLIGHT", "3"))
        self.nopool = bool(os.environ.get("KNOPOOL"))

    def _wait(self, eng, toks):
        E = self.E[eng]
        w = self.waited[eng]
        for s, v in toks.items():
            if eng == "pe" and s == "pe":
                continue
            if w.get(s, 0) < v:
                E.wait_ge(self.sems[s], v)
                w[s] = v

    @staticmethod
    def _collect(reads, writes):
        toks = {}
        for b in reads:
            if b.lw is not None:
                s, v = b.lw
                if toks.get(s, 0) < v:
                    toks[s] = v
        for b in writes:
            if b.lw is not None:
                s, v = b.lw
                if toks.get(s, 0) < v:
                    toks[s] = v
            for s, v in b.rd.items():
                if toks.get(s, 0) < v:
                    toks[s] = v
        return toks

    def op(self, eng, fn, reads=(), writes=()):
        if eng == "pool" and self.nopool:
            eng = "dve"
        self._wait(eng, self._collect(reads, writes))
        inst = fn(self.E[eng])
        self.cnt[eng] += 1
        inst.then_inc(self.sems[eng], 1)
        tok = (eng, self.cnt[eng])
        for b in writes:
            b.lw = tok
            b.rd = {}
        for b in reads:
            if b.rd.get(eng, 0) < tok[1]:
                b.rd[eng] = tok[1]
        return tok

    def dma(self, out_ap, in_ap, buf, load, q="sp"):
        if load:
            toks = self._collect((), (buf,))
        else:
            toks = self._collect((buf,), ())
        fl = self.inflight.setdefault(q, [])
        while len(fl) >= self.max_inflight:
            s0, v0 = fl.pop(0)
            if toks.get(s0, 0) < v0:
                toks[s0] = v0
        self._wait(q, toks)
        s = buf.dsem
        inst = self.E[q].dma_start(out=out_ap, in_=in_ap)
        self.cnt[s] += 16
        inst.then_inc(self.sems[s], 16)
        tok = (s, self.cnt[s])
        fl.append(tok)
        if load:
            buf.lw = tok
            buf.rd = {}
        else:
            buf.rd[s] = tok[1]
        return tok

    def barrier(self):
        for e in self.E:
            self._wait(e, dict(self.cnt))


_UID = [0]


class Phase:
    def __init__(self, tr, nc):
        self.tr = tr
        self.nc = nc
        self.es = ExitStack()
        self.dsems = []
        self.k = 0

    def sb(self, shape, dt, dma=False, name=None):
        _UID[0] += 1
        t = self.es.enter_context(self.nc.sbuf_tensor("%s_%d" % (name or "sb", _UID[0]), list(shape), dt))
        ds = None
        if dma:
            ds = self.tr.free_dsems.pop()
            self.dsems.append(ds)
        return Buf(t, ds)

    def ps(self, shape, dt, name=None):
        _UID[0] += 1
        t = self.es.enter_context(self.nc.psum_tensor("%s_%d" % (name or "ps", _UID[0]), list(shape), dt))
        return Buf(t)

    def close(self):
        self.tr.barrier()
        self.es.close()
        self.tr.free_dsems.extend(self.dsems)
        self.dsems = []


def build_program(Sp, Ss, depth=DEPTH):
    T = Sp + Ss
    seqs = [(0, Sp), (Sp, Ss)]
    nc = bass.Bass("TRN2", target_bir_lowering=False)

    def din(name, shape, dt=F32):
        return nc.dram_tensor(name, list(shape), dt, kind="ExternalInput").ap()

    xp = din("xp", [Sp, D])
    xs = din("xs", [Ss, D])
    w_in_even = din("w_in_even", [2, D, EVEN_IN])
    w_out_even = din("w_out_even", [2, D, D])
    w_in_odd = din("w_in_odd", [2, D, ODD_IN])
    w_out_odd = din("w_out_odd", [2, D, D])
    w_gate_up = din("w_gate_up", [4, D, 2 * FFN_H])
    w_down = din("w_down", [4, FFN_H, D])
    gpreT_mix = din("gpreT_mix", [4, 128, 8])
    gpreT_ffn = din("gpreT_ffn", [4, 128, 8])
    gpost_mix = din("gpost_mix", [4, 128, D])
    gpost_ffn = din("gpost_ffn", [4, 128, D])
    gqk = din("gqk", [2, 128, 640])
    dlam = din("dlam", [2, 128, 256])
    sinkc = din("sinkc", [2, 16])
    tab = din("tab", [max(Sp, Ss), 160])
    maskc = din("maskc", [128, 6 * 512])
    identc = din("identc", [128, 128])
    yp = nc.dram_tensor("yp", [Sp, D], F32, kind="ExternalOutput").ap()
    ys = nc.dram_tensor("ys", [Ss, D], F32, kind="ExternalOutput").ap()
    QTd = nc.dram_tensor("QTd", [8, 128, T], BF16).ap()
    KTd = nc.dram_tensor("KTd", [5, 128, T], BF16).ap()
    Vd = nc.dram_tensor("Vd", [T, 640], BF16).ap()
    OTd = nc.dram_tensor("OTd", [8, 128, T], BF16).ap()

    def x_rows(src_is_input, t0, n):
        if t0 < Sp:
            base = xp if src_is_input else yp
            return base[t0:t0 + n, :]
        base = xs if src_is_input else ys
        return base[t0 - Sp:t0 - Sp + n, :]

    with ExitStack() as es:
        tr = TR(nc, es)
        gph = Phase(tr, nc)
        ident_f = gph.sb([128, 128], F32, dma=True)
        ident = gph.sb([128, 128], BF16)
        ones_f = gph.sb([128, 128], F32)
        ones_b = gph.sb([128, 128], BF16)
        mask_f = gph.sb([128, 6 * 512], F32, dma=True)
        mask_b = gph.sb([128, 6 * 512], BF16)
        tr.dma(ident_f[:], identc[:, :], ident_f, True)
        tr.op("dve", lambda e: e.tensor_copy(out=ident[:], in_=ident_f[:]), [ident_f], [ident])
        tr.op("pool", lambda e: e.memset(ones_f[:], 1.0), [], [ones_f])
        tr.op("pool", lambda e: e.memset(ones_b[:], 1.0), [], [ones_b])
        eps_t = gph.sb([128, 1], F32)
        tr.op("pool", lambda e: e.memset(eps_t[:], EPS), [], [eps_t])
        tr.dma(mask_f[:], maskc[:, :], mask_f, True)
        tr.op("dve", lambda e: e.tensor_copy(out=mask_b[:], in_=mask_f[:]), [mask_f], [mask_b])

        def load_weight(ph, Wb, Wd, C, N, gT=None, nch=1408):
            sub = Phase(tr, nc)
            stg = [sub.sb([128, nch], F32, dma=True, name="wst") for _ in range(3)]
            k = 0
            for c in range(C):
                for n0 in range(0, N, nch):
                    w = min(nch, N - n0)
                    st = stg[k % 3]
                    tr.dma(st[:, 0:w], Wd[c * 128:(c + 1) * 128, n0:n0 + w], st, True)
                    dst = Wb[:, c * N + n0:c * N + n0 + w]
                    if gT is not None:
                        if k % 2 == 0:
                            tr.op("dve", lambda e: e.tensor_scalar(
                                out=dst, in0=st[:, 0:w], scalar1=gT[:, c:c + 1], scalar2=None, op0=ALU.mult),
                                [st, gT], [Wb])
                        else:
                            tr.op("act", lambda e: e.activation(out=dst, in_=st[:, 0:w], func=AF.Copy,
                                                                scale=gT[:, c:c + 1]), [st, gT], [Wb])
                    else:
                        eng = ("dve", "pool")[k % 2]
                        tr.op(eng, lambda e: e.tensor_copy(out=dst, in_=st[:, 0:w]), [st], [Wb])
                    k += 1
            sub.close()

        def rstd_from_ss(eng_ss_buf, ss_ap, out_buf, out_ap, inv_n):
            tr.op("act", lambda e: e.activation(out=out_ap, in_=ss_ap, func=AF.Ln, scale=inv_n, bias=eps_t[:, 0:1]),
                  [eng_ss_buf, eps_t], [out_buf])
            tr.op("act", lambda e: e.activation(out=out_ap, in_=out_ap, func=AF.Exp, scale=-0.5),
                  [out_buf], [out_buf])

        def norm_transpose(xt, j, ss, rstd, junk, hb, psT, hT, TT, jo=None):
            if jo is None:
                jo = j
            xj = xt[:, j * D:(j + 1) * D]
            tr.op("act", lambda e: e.activation(out=junk[:], in_=xj, func=AF.Square, accum_out=ss[:, j:j + 1]),
                  [xt], [junk, ss])
            rstd_from_ss(ss, ss[:, j:j + 1], rstd, rstd[:, j:j + 1], 1.0 / D)
            tr.op("dve", lambda e: e.tensor_scalar(out=hb[:], in0=xj, scalar1=rstd[:, j:j + 1], scalar2=None,
                                                   op0=ALU.mult), [xt, rstd], [hb])

            def tps(e):
                r = None
                for c in range(8):
                    r = e.transpose(psT[:, c * 128:(c + 1) * 128], hb[:, c * 128:(c + 1) * 128], ident[:])
                return r
            tr.op("pe", tps, [hb, ident], [psT])
            tr.op("act", lambda e: e.activation(
                out=view(hT[:, jo * 128:jo * 128 + 1], [(TT, 8), (1, 128)]),
                in_=view(psT[:, 0:1], [(128, 8), (1, 128)]), func=AF.Copy), [psT], [hT])

        def post_residual(ps_y, xt, j, ss2, rstd2, junk, tmp, gpost):
            xj = xt[:, j * D:(j + 1) * D]
            tr.op("act", lambda e: e.activation(out=junk[:], in_=ps_y[:, 0:D], func=AF.Square,
                                                accum_out=ss2[:, j:j + 1]), [ps_y], [junk, ss2])
            rstd_from_ss(ss2, ss2[:, j:j + 1], rstd2, rstd2[:, j:j + 1], 1.0 / D)
            tr.op("dve", lambda e: e.scalar_tensor_tensor(out=tmp[:], in0=ps_y[:, 0:D], scalar=rstd2[:, j:j + 1],
                                                          in1=gpost[:], op0=ALU.mult, op1=ALU.mult),
                  [ps_y, rstd2, gpost], [tmp])
            tr.op("pool", lambda e: e.tensor_tensor(out=xj, in0=xj, in1=tmp[:], op=ALU.add), [xt, tmp], [xt])

        def rope_small(ps_src, nh, tb, j, ra, rb, dst, dst_off):
            import os as _os4
            RS = _os4.environ.get("KROPE", "")
            tbj = tb[:, j * 160:(j + 1) * 160]
            x_all = view(ps_src[:, 0:1], [(64, nh), (1, 16)])
            cc = view(tbj[:, 128:129], [(0, nh), (1, 16)])
            if "1" in RS:
                return
            tr.op("dve", lambda e: e.tensor_tensor(out=view(ra[:, 0:1], [(16, nh), (1, 16)]), in0=x_all, in1=cc,
                                                   op=ALU.mult), [ps_src, tb], [ra])
            if "2" in RS:
                return
            x_hi = view(ps_src[:, 8:9], [(64, nh), (1, 8)])
            x_lo = view(ps_src[:, 0:1], [(64, nh), (1, 8)])
            s_neg = view(tbj[:, 144:145], [(0, nh), (1, 8)])
            s_pos = view(tbj[:, 152:153], [(0, nh), (1, 8)])
            tr.op("dve", lambda e: e.tensor_tensor(out=view(rb[:, 0:1], [(16, nh), (1, 8)]), in0=x_hi, in1=s_neg,
                                                   op=ALU.mult), [ps_src, tb], [rb])
            tr.op("dve", lambda e: e.tensor_tensor(out=view(rb[:, 8:9], [(16, nh), (1, 8)]), in0=x_lo, in1=s_pos,
                                                   op=ALU.mult), [ps_src, tb], [rb])
            if "3" in RS:
                return
            tr.op("pool", lambda e: e.tensor_tensor(out=view(dst[:, dst_off:dst_off + 1], [(64, nh), (1, 16)]),
                                                    in0=view(ra[:, 0:1], [(16, nh), (1, 16)]),
                                                    in1=view(rb[:, 0:1], [(16, nh), (1, 16)]), op=ALU.add),
                  [ra, rb], [dst])

        def transposes_to_stage(src, src_off, nchunk, psT2, stage, st_chunk0, j, TT):
            def tps(e):
                r = None
                for c in range(nchunk):
                    r = e.transpose(psT2[:, c * 128:(c + 1) * 128],
                                    src[:, src_off + c * 128:src_off + (c + 1) * 128], ident[:])
                return r
            tr.op("pe", tps, [src, ident], [psT2])
            tr.op("act", lambda e: e.activation(
                out=view(stage[:, st_chunk0 * TT + j * 128:st_chunk0 * TT + j * 128 + 1], [(TT, nchunk), (1, 128)]),
                in_=view(psT2[:, 0:1], [(128, nchunk), (1, 128)]), func=AF.Copy), [psT2], [stage])

        def phase_p1(l):
            even = (l % 2 == 0)
            li = l // 2
            NIN = EVEN_IN if even else ODD_IN
            TT = 512
            ph = Phase(tr, nc)
            Wb = ph.sb([128, 8 * NIN], BF16, name="Win")
            import os as _os6
            if not even:
                dummy2 = ph.sb([128, 8], F32, dma=True)
            gT = ph.sb([128, 8], F32, dma=True)
            tr.dma(gT[:], gpreT_mix[l], gT, True)
            load_weight(ph, Wb, (w_in_even if even else w_in_odd)[li], 8, NIN, gT=gT, nch=(1152 if even else 768))
            NQC = 8
            NKC = 5 if even else 2
            NV = 640 if even else 256
            xts = [ph.sb([128, 4 * D], F32, dma=True, name="xt") for _ in range(2)]
            tbs = [ph.sb([128, 4 * 160], F32, dma=True, name="tb") for _ in range(2)]
            QTst = [ph.sb([128, NQC * TT], BF16, dma=True, name="QTst") for _ in range(2)]
            KTst = [ph.sb([128, NKC * TT], BF16, dma=True, name="KTst") for _ in range(2)]
            Vst = [ph.sb([128, 4 * NV], BF16, dma=True, name="Vst") for _ in range(2)]
            hTs = [ph.sb([128, 8 * 128], BF16, name="hT") for _ in range(2)]
            hb = ph.sb([128, D], BF16, name="hb")
            junk = ph.sb([128, D], BF16, name="junk")
            ss = ph.sb([128, 4], F32)
            rstd = ph.sb([128, 4], F32)
            ra = ph.sb([128, 256], F32)
            rb = ph.sb([128, 256], F32)
            psT = ph.ps([128, 1024], BF16, name="psT")
            if even:
                gq = ph.sb([128, 640], F32, dma=True)
                tr.dma(gq[:], gqk[li], gq, True)
                sqs = ph.sb([128, 640], F32)
                ssh = ph.sb([128, 10], F32)
                rsh = ph.sb([128, 10], F32)
                tq = ph.sb([128, 640], F32)
                ta = ph.sb([128, 640], F32)
                tbb = ph.sb([128, 640], F32)
                qkb = ph.sb([128, 640], BF16)
                qbb = ph.sb([128, 512], BF16)
                kbb = ph.sb([128, 512], BF16)
                psA = ph.ps([128, 1024], F32, name="psA")
                psQB = ph.ps([128, 512], F32, name="psQB")
                psKB = ph.ps([128, 512], F32, name="psKB")
                psVB = ph.ps([128, 512], F32, name="psVB")
                psT2a = ph.ps([128, 1024], BF16, name="psT2a")
                psT2b = ph.ps([128, 1024], BF16, name="psT2b")
            else:
                import os as _os5
                if _os5.environ.get("KDUMMY"):
                    dummy = ph.sb([128, int(_os5.environ["KDUMMY"])], F32, dma=True)
                qb16 = ph.sb([128, 1024], BF16)
                kb16 = ph.sb([128, 256], BF16)
                psQ0 = ph.ps([128, 512], F32, name="psQ0")
                psQ1 = ph.ps([128, 512], F32, name="psQ1")
                psKV = ph.ps([128, 512], F32, name="psKV")
                psT2a = ph.ps([128, 1024], BF16, name="psT2a")
                psT2b = ph.ps([128, 1024], BF16, name="psT2b")

            qfs = [ph.sb([128, 512], F32, name="qf") for _ in range(2)]
            qfc = [0]

            def evac_rope(ps_buf, ps_off, ncols, nh, dst, dst_off, tb, j):
                qf = qfs[qfc[0] % 2]
                qfc[0] += 1
                tr.op("act", lambda e: e.activation(out=qf[:, 0:ncols], in_=ps_buf[:, ps_off:ps_off + ncols],
                                                    func=AF.Copy), [ps_buf], [qf])
                tr.op("pool", lambda e: e.tensor_copy(out=dst[:, dst_off:dst_off + ncols], in_=qf[:, 0:ncols]),
                      [qf], [dst])
                rope_small(qf, nh, tb, j, ra, rb, dst, dst_off)

            tiles = [(t0 + i * TT, t0, i * TT) for (t0, S) in seqs for i in range(S // TT)]

            def issue_loads(ti):
                g0, t0, off = tiles[ti]
                xt = xts[ti % 2]
                import os as _os3
                tr.dma(view(xt[:, 0:1], [(D, 4), (1, D)]),
                       x_rows(l == 0 or bool(_os3.environ.get("KXIN")), g0, TT).rearrange("(j p) d -> p j d", p=128), xt, True)
                tb = tbs[ti % 2]
                tr.dma(view(tb[:, 0:1], [(160, 4), (1, 160)]),
                       tab[off:off + TT, :].rearrange("(j p) d -> p j d", p=128), tb, True)

            hcur = [None]

            def mm_group(ps_buf, ps_off, j, col0, ncols):
                hT = hcur[0]
                def f(e):
                    r = None
                    for c in range(8):
                        r = e.matmul(ps_buf[:, ps_off:ps_off + ncols],
                                     lhsT=hT[:, c * 128:(c + 1) * 128],
                                     rhs=Wb[:, c * NIN + col0:c * NIN + col0 + ncols],
                                     start=(c == 0), stop=(c == 7))
                    return r
                return f

            def chain(n):
                ti_, j_ = n // 4, n % 4
                norm_transpose(xts[ti_ % 2], j_, ss, rstd, junk, hb, psT, hTs[n % 2], 128, jo=0)

            issue_loads(0)
            if len(tiles) > 1:
                issue_loads(1)
            chain(0)
            for ti in range(len(tiles)):
                g0, t0, off = tiles[ti]
                xt = xts[ti % 2]
                tb = tbs[ti % 2]
                qst, kst, vst = QTst[ti % 2], KTst[ti % 2], Vst[ti % 2]
                for j in range(4):
                    n = ti * 4 + j
                    if n + 1 < 4 * len(tiles):
                        chain(n + 1)
                    hT = hTs[n % 2]
                    hcur[0] = hT
                    tbj = tb[:, j * 160:(j + 1) * 160]
                    if even:
                        tr.op("pe", mm_group(psA, 0, j, 0, 512), [hT, Wb], [psA])
                        tr.op("pe", mm_group(psA, 512, j, 512, 256), [hT, Wb], [psA])
                        tr.op("pe", mm_group(psQB, 0, j, 768, 512), [hT, Wb], [psQB])
                        tr.op("pe", mm_group(psKB, 0, j, 1280, 512), [hT, Wb], [psKB])
                        tr.op("pe", mm_group(psVB, 0, j, 1792, 512), [hT, Wb], [psVB])
                        tr.op("act", lambda e: e.activation(out=sqs[:], in_=psA[:, 0:640], func=AF.Square),
                              [psA], [sqs])
                        tr.op("dve", lambda e: e.tensor_reduce(out=ssh[:], in_=view(sqs[:, 0:1], [(64, 10), (1, 64)]),
                                                               axis=AX.X, op=ALU.add), [sqs], [ssh])
                        rstd_from_ss(ssh, ssh[:], rsh, rsh[:], 1.0 / HD)
                        tr.op("dve", lambda e: e.tensor_tensor(
                            out=view(tq[:, 0:1], [(64, 10), (1, 64)]), in0=view(psA[:, 0:1], [(64, 10), (1, 64)]),
                            in1=view(rsh[:, 0:1], [(1, 10), (0, 64)]), op=ALU.mult), [psA, rsh], [tq])
                        tr.op("act", lambda e: e.activation(out=vst[:, j * 640:j * 640 + 128], in_=psA[:, 640:768],
                                                            func=AF.Copy), [psA], [vst])
                        tr.op("pool", lambda e: e.tensor_tensor(out=tq[:], in0=tq[:], in1=gq[:], op=ALU.mult),
                              [tq, gq], [tq])
                        tr.op("pool", lambda e: e.tensor_tensor(
                            out=view(ta[:, 0:1], [(64, 10), (1, 64)]), in0=view(tq[:, 0:1], [(64, 10), (1, 64)]),
                            in1=view(tbj[:, 0:1], [(0, 10), (1, 64)]), op=ALU.mult), [tq, tb], [ta])
                        tr.op("pool", lambda e: e.tensor_tensor(
                            out=view(tbb[:, 0:1], [(64, 10), (32, 2), (1, 16)]),
                            in0=view(tq[:, 16:17], [(64, 10), (32, 2), (1, 16)]),
                            in1=view(tbj[:, 64:65], [(0, 10), (32, 2), (1, 16)]), op=ALU.mult), [tq, tb], [tbb])
                        tr.op("pool", lambda e: e.tensor_tensor(
                            out=view(tbb[:, 16:17], [(64, 10), (32, 2), (1, 16)]),
                            in0=view(tq[:, 0:1], [(64, 10), (32, 2), (1, 16)]),
                            in1=view(tbj[:, 80:81], [(0, 10), (32, 2), (1, 16)]), op=ALU.mult), [tq, tb], [tbb])
                        tr.op("pool", lambda e: e.tensor_tensor(out=qkb[:], in0=ta[:], in1=tbb[:], op=ALU.add),
                              [ta, tbb], [qkb])
                        transposes_to_stage(qkb, 0, 4, psT2a, qst, 0, j, TT)
                        transposes_to_stage(qkb, 512, 1, psT2b, kst, 0, j, TT)
                        evac_rope(psQB, 0, 512, 8, qbb, 0, tb, j)
                        transposes_to_stage(qbb, 0, 4, psT2a, qst, 4, j, TT)
                        evac_rope(psKB, 0, 512, 8, kbb, 0, tb, j)
                        transposes_to_stage(kbb, 0, 4, psT2b, kst, 1, j, TT)
                        tr.op("act", lambda e: e.activation(out=vst[:, j * 640 + 128:(j + 1) * 640],
                                                            in_=psVB[:, 0:512], func=AF.Copy), [psVB], [vst])
                    else:
                        import os as _os
                        SK = _os.environ.get("KSKIP", "")
                        tr.op("pe", mm_group(psQ0, 0, j, 0, 512), [hT, Wb], [psQ0])
                        tr.op("pe", mm_group(psQ1, 0, j, 512, 512), [hT, Wb], [psQ1])
                        tr.op("pe", mm_group(psKV, 0, j, 1024, 512), [hT, Wb], [psKV])
                        evac_rope(psQ0, 0, 512, 8, qb16, 0, tb, j)
                        evac_rope(psQ1, 0, 512, 8, qb16, 512, tb, j)
                        if "c" not in SK:
                            transposes_to_stage(qb16, 0, 8, psT2a, qst, 0, j, TT)
                        evac_rope(psKV, 0, 256, 4, kb16, 0, tb, j)
                        if "e" not in SK:
                            transposes_to_stage(kb16, 0, 2, psT2b, kst, 0, j, TT)
                        tr.op("act", lambda e: e.activation(out=vst[:, j * 256:(j + 1) * 256], in_=psKV[:, 256:512],
                                                            func=AF.Copy), [psKV], [vst])
                if ti + 2 < len(tiles):
                    issue_loads(ti + 2)
                import os as _os2
                SK2 = _os2.environ.get("KSKIP", "")
                if "q" not in SK2:
                    tr.dma(QTd[0:NQC, :, g0:g0 + TT].rearrange("c p s -> p c s"),
                           view(qst[:, 0:1], [(TT, NQC), (1, TT)]), qst, False)
                if "k" not in SK2:
                    tr.dma(KTd[0:NKC, :, g0:g0 + TT].rearrange("c p s -> p c s"),
                           view(kst[:, 0:1], [(TT, NKC), (1, TT)]), kst, False)
                if "v" not in SK2:
                    tr.dma(Vd[g0:g0 + TT, 0:NV].rearrange("(j p) d -> p j d", p=128),
                           view(vst[:, 0:1], [(NV, 4), (1, NV)]), vst, False)
            ph.close()

        def attn_ac(ph, seq, K2, Vg, qchunk, window, esink, es_cols, bufs):
            t0, S = seq
            (QTs, ps_s, pts, ps_o, ps_bc, recs, bcs, OTst, cnt, pend) = bufs
            nk = S // 128
            nq = S // 512
            for qt in range(nq):
                QT = QTs[cnt[0] % len(QTs)]
                cnt[0] += 1
                tr.dma(QT[:], QTd[qchunk, :, t0 + qt * 512:t0 + (qt + 1) * 512], QT, True)
                if window:
                    kcs = [kc for kc in range(qt * 4 - 1, qt * 4 + 5) if 0 <= kc < nk]
                else:
                    kcs = list(range(nk))

                def qk(i, kc):
                    pss = ps_s[i % 2]
                    def f(e):
                        e.matmul(pss[:, 0:512], lhsT=K2[0:64, kc * 128:(kc + 1) * 128], rhs=QT[0:64, :],
                                 start=True, stop=True)
                        return e.matmul(pss[:, 512:1024], lhsT=K2[64:128, kc * 128:(kc + 1) * 128],
                                        rhs=QT[64:128, :], start=True, stop=True)
                    tr.op("pe", f, [K2, QT], [pss])

                def expo(i, kc):
                    pss = ps_s[i % 2]
                    pt = pts[i % 3]
                    tr.op("act", lambda e: e.activation(out=pt[:], in_=pss[:], func=AF.Exp, scale=SCALE),
                          [pss], [pt])
                    if window:
                        jm = kc - qt * 4 + 1
                        tr.op("dve", lambda e: e.tensor_tensor(
                            out=view(pt[:, 0:1], [(512, 2), (1, 512)]), in0=view(pt[:, 0:1], [(512, 2), (1, 512)]),
                            in1=view(mask_b[:, jm * 512:jm * 512 + 1], [(0, 2), (1, 512)]), op=ALU.mult),
                            [pt, mask_b], [pt])

                def pv(i, kc):
                    pt = pts[i % 3]
                    def f(e):
                        e.matmul(ps_o[0][0:65, :], lhsT=Vg[:, kc * 65:(kc + 1) * 65], rhs=pt[:, 0:512],
                                 start=(i == 0), stop=(i == len(kcs) - 1))
                        return e.matmul(ps_o[1][0:65, :], lhsT=Vg[:, kc * 65:(kc + 1) * 65], rhs=pt[:, 512:1024],
                                        start=(i == 0), stop=(i == len(kcs) - 1))
                    tr.op("pe", f, [Vg, pt], [ps_o[0], ps_o[1]])

                for i, kc in enumerate(kcs):
                    qk(i, kc)
                    expo(i, kc)
                    if i == 1 and pend:
                        pend.pop(0)()
                    if i > 0:
                        pv(i - 1, kcs[i - 1])
                pv(len(kcs) - 1, kcs[-1])
                for hh in range(2):
                    po = ps_o[hh]
                    rec = recs[hh]
                    if esink is not None:
                        col = es_cols[hh]
                        tr.op("act", lambda e: e.activation(out=rec[64:65, :], in_=po[64:65, :], func=AF.Ln,
                                                            bias=esink[64:65, col:col + 1]), [po, esink], [rec])
                    else:
                        tr.op("act", lambda e: e.activation(out=rec[64:65, :], in_=po[64:65, :], func=AF.Ln),
                              [po], [rec])
                    tr.op("act", lambda e: e.activation(out=rec[64:65, :], in_=rec[64:65, :], func=AF.Exp,
                                                        scale=-1.0), [rec], [rec])

                def fin(qt=qt):
                    for hh in range(2):
                        po = ps_o[hh]
                        rec = recs[hh]
                        tr.op("pe", lambda e: e.matmul(ps_bc[0:64, :], lhsT=ones_f[64:65, 0:64], rhs=rec[64:65, :],
                                                       start=True, stop=True), [ones_f, rec], [ps_bc])
                        tr.op("act", lambda e: e.activation(out=bcs[0:64, :], in_=ps_bc[0:64, :], func=AF.Copy),
                              [ps_bc], [bcs])
                        ot = OTst[cnt[1] % len(OTst)]
                        cnt[1] += 1
                        tr.op("dve", lambda e: e.tensor_tensor(out=ot[0:64, :], in0=po[0:64, :], in1=bcs[0:64, :],
                                                               op=ALU.mult), [po, bcs], [ot])
                        tr.dma(OTd[qchunk, hh * 64:(hh + 1) * 64, t0 + qt * 512:t0 + (qt + 1) * 512], ot[0:64, :],
                               ot, False)
                pend.append(fin)

        def alloc_ac(ph):
            QTs = [ph.sb([128, 512], BF16, dma=True, name="QT") for _ in range(3)]
            ps_s = [ph.ps([128, 1024], F32, name="ps_s") for _ in range(2)]
            pts = [ph.sb([128, 1024], BF16, name="pt") for _ in range(3)]
            ps_o = [ph.ps([128, 512], F32, name="ps_o") for _ in range(2)]
            ps_bc = ph.ps([128, 512], F32, name="ps_bc")
            recs = [ph.sb([128, 512], F32, name="rec") for _ in range(2)]
            bcs = ph.sb([128, 512], F32, name="bcs")
            OTst = [ph.sb([128, 512], BF16, dma=True, name="OTst") for _ in range(4)]
            return (QTs, ps_s, pts, ps_o, ps_bc, recs, bcs, OTst, [0, 0], [])

        def load_kv_ac(seq, K2, Vg, kchunk, khalf, vcol):
            t0, S = seq
            nk = S // 128
            for half in range(2):
                tr.dma(K2[half * 64:(half + 1) * 64, 0:S], KTd[kchunk, khalf * 64:(khalf + 1) * 64, t0:t0 + S], K2, True)
            tr.dma(view(Vg[:, 0:1], [(65, nk), (1, 64)]),
                   Vd[t0:t0 + S, vcol:vcol + 64].rearrange("(k p) d -> p k d", p=128), Vg, True)
            tr.op("pool", lambda e: e.memset(view(Vg[:, 64:65], [(65, nk), (1, 1)]), 1.0), [], [Vg])

        def phase_p2_even_a(l):
            ph = Phase(tr, nc)
            Smax = max(Sp, Ss)
            K2 = ph.sb([128, Smax], BF16, dma=True, name="K2")
            Vg = ph.sb([128, (Smax // 128) * 65], BF16, dma=True, name="Vg")
            bufs = alloc_ac(ph)
            for seq in seqs:
                for g in range(2):
                    load_kv_ac(seq, K2, Vg, 0, g, g * 64)
                    for pair in range(2):
                        attn_ac(ph, seq, K2, Vg, g * 2 + pair, False, None, None, bufs)
            while bufs[9]:
                bufs[9].pop(0)()
            ph.close()

        def phase_p2_odd(l):
            li = l // 2
            ph = Phase(tr, nc)
            Smax = max(Sp, Ss)
            K2 = ph.sb([128, Smax], BF16, dma=True, name="K2")
            Vg = ph.sb([128, (Smax // 128) * 65], BF16, dma=True, name="Vg")
            esink = ph.sb([128, 16], F32, dma=True, name="esink")
            tr.dma(esink[64:65, :], sinkc[li:li + 1, :], esink, True)
            tr.op("act", lambda e: e.activation(out=esink[64:65, :], in_=esink[64:65, :], func=AF.Exp),
                  [esink], [esink])
            bufs = alloc_ac(ph)
            for seq in seqs:
                for hk in range(4):
                    load_kv_ac(seq, K2, Vg, hk // 2, hk % 2, hk * 64)
                    for pair in range(2):
                        qc = hk * 2 + pair
                        attn_ac(ph, seq, K2, Vg, qc, True, esink, (2 * qc, 2 * qc + 1), bufs)
            while bufs[9]:
                bufs[9].pop(0)()
            ph.close()

        def phase_p2_even_b(l):
            li = l // 2
            c_out = 1.0 - lam_init_of(l)
            ph = Phase(tr, nc)
            Smax = max(Sp, Ss)
            K2 = ph.sb([128, Smax], BF16, dma=True, name="KB")
            VB = ph.sb([128, Smax], BF16, dma=True, name="VB")
            QTs = [ph.sb([128, 512], BF16, dma=True, name="QT") for _ in range(3)]
            ps_s = [ph.ps([128, 1024], F32, name="ps_s") for _ in range(2)]
            pts = [ph.sb([128, 1024], BF16, name="pt") for _ in range(3)]
            ps_o = [ph.ps([128, 512], F32, name="ps_o") for _ in range(2)]
            ps_d = [ph.ps([128, 512], F32, name="ps_d") for _ in range(2)]
            r0 = ph.sb([128, 512], F32)
            r1 = ph.sb([128, 512], F32)
            fa = ph.sb([128, 512], F32)
            fb = ph.sb([128, 512], F32)
            fo = ph.sb([128, 512], F32)
            fsq = ph.sb([128, 512], F32)
            frs = ph.sb([128, 512], F32)
            OTst = [ph.sb([128, 512], BF16, dma=True, name="OTst") for _ in range(2)]
            dl = ph.sb([128, 256], F32, dma=True)
            tr.dma(dl[:], dlam[li], dl, True)
            pr = ph.sb([128, 128], F32)
            sm = ph.sb([128, 2], F32)
            ex = ph.sb([128, 2], F32)
            nlam = ph.sb([128, 1], F32)
            tr.op("dve", lambda e: e.tensor_tensor(out=view(pr[:, 0:1], [(64, 2), (1, 64)]),
                                                   in0=view(dl[:, 0:1], [(128, 2), (1, 64)]),
                                                   in1=view(dl[:, 64:65], [(128, 2), (1, 64)]), op=ALU.mult), [dl], [pr])
            tr.op("dve", lambda e: e.tensor_reduce(out=sm[:], in_=view(pr[:, 0:1], [(64, 2), (1, 64)]), axis=AX.X,
                                                   op=ALU.add), [pr], [sm])
            tr.op("act", lambda e: e.activation(out=ex[:], in_=sm[:], func=AF.Exp), [sm], [ex])
            tr.op("dve", lambda e: e.tensor_tensor(out=nlam[:], in0=ex[:, 1:2], in1=ex[:, 0:1], op=ALU.subtract),
                  [ex], [nlam])
            tr.op("dve", lambda e: e.tensor_scalar(out=nlam[:], in0=nlam[:], scalar1=-lam_init_of(l), scalar2=None,
                                                   op0=ALU.add), [nlam], [nlam])
            qcnt = 0
            ocntB = [0]
            pendB = []
            accA = ph.sb([128, 512], F32, name="accA")
            accB = ph.sb([128, 512], F32, name="accB")
            for (t0, S) in seqs:
                nk = S // 128
                nq = S // 512
                for h in range(4):
                    tr.dma(K2[:, 0:S], KTd[1 + h, :, t0:t0 + S], K2, True)
                    tr.dma(view(VB[:, 0:1], [(128, nk), (1, 128)]),
                           Vd[t0:t0 + S, 128 + h * 128:128 + (h + 1) * 128].rearrange("(k p) d -> p k d", p=128),
                           VB, True)
                    for qt in range(nq):
                        QT = QTs[qcnt % 3]
                        qcnt += 1
                        tr.dma(QT[:], QTd[4 + h, :, t0 + qt * 512:t0 + (qt + 1) * 512], QT, True)

                        def qk(i):
                            pss = ps_s[i % 2]
                            def f(e):
                                e.matmul(pss[:, 0:512], lhsT=K2[0:64, i * 128:(i + 1) * 128], rhs=QT[0:64, :],
                                         start=True, stop=True)
                                return e.matmul(pss[:, 512:1024], lhsT=K2[64:128, i * 128:(i + 1) * 128],
                                                rhs=QT[64:128, :], start=True, stop=True)
                            tr.op("pe", f, [K2, QT], [pss])

                        def expo(i):
                            pss = ps_s[i % 2]
                            pt = pts[i % 3]
                            tr.op("act", lambda e: e.activation(out=pt[:], in_=pss[:], func=AF.Exp, scale=SCALE),
                                  [pss], [pt])

                        def pv(i):
                            pt = pts[i % 3]
                            st, sp_ = (i == 0), (i == nk - 1)
                            def f(e):
                                e.matmul(ps_o[0][:, :], lhsT=VB[:, i * 128:(i + 1) * 128], rhs=pt[:, 0:512],
                                         start=st, stop=sp_)
                                return e.matmul(ps_o[1][:, :], lhsT=VB[:, i * 128:(i + 1) * 128],
                                                rhs=pt[:, 512:1024], start=st, stop=sp_)
                            tr.op("pe", f, [VB, pt], [ps_o[0], ps_o[1]])

                        def dacc(i):
                            pt = pts[i % 3]
                            if i == 0:
                                tr.op("dve", lambda e: e.tensor_copy(out=accA[:], in_=pt[:, 0:512]), [pt], [accA])
                                tr.op("pool", lambda e: e.tensor_copy(out=accB[:], in_=pt[:, 512:1024]), [pt], [accB])
                            else:
                                tr.op("dve", lambda e: e.tensor_tensor(out=accA[:], in0=accA[:], in1=pt[:, 0:512],
                                                                       op=ALU.add), [pt, accA], [accA])
                                tr.op("pool", lambda e: e.tensor_tensor(out=accB[:], in0=accB[:], in1=pt[:, 512:1024],
                                                                        op=ALU.add), [pt, accB], [accB])

                        for i in range(nk):
                            qk(i)
                            expo(i)
                            dacc(i)
                            if pendB and i == pendB[0][0]:
                                pendB.pop(0)[1]()
                            if i > 0:
                                pv(i - 1)
                        pv(nk - 1)

                        tr.op("pe", lambda e: e.matmul(ps_d[0][:, :], lhsT=ones_f[:, :], rhs=accA[:], start=True,
                                                       stop=True), [ones_f, accA], [ps_d[0]])
                        tr.op("pe", lambda e: e.matmul(ps_d[1][:, :], lhsT=ones_f[:, :], rhs=accB[:], start=True,
                                                       stop=True), [ones_f, accB], [ps_d[1]])

                        def fin1():
                            for rr, pd in ((r0, ps_d[0]), (r1, ps_d[1])):
                                tr.op("act", lambda e: e.activation(out=rr[:], in_=pd[:, :], func=AF.Ln), [pd], [rr])
                                tr.op("act", lambda e: e.activation(out=rr[:], in_=rr[:], func=AF.Exp, scale=-1.0),
                                      [rr], [rr])
                            tr.op("dve", lambda e: e.tensor_tensor(out=fa[:], in0=ps_o[0][:, :], in1=r0[:],
                                                                   op=ALU.mult), [ps_o[0], r0], [fa])
                            tr.op("dve", lambda e: e.tensor_tensor(out=fb[:], in0=ps_o[1][:, :], in1=r1[:],
                                                                   op=ALU.mult), [ps_o[1], r1], [fb])
                            tr.op("dve", lambda e: e.scalar_tensor_tensor(out=fo[:], in0=fb[:], scalar=nlam[:, 0:1],
                                                                          in1=fa[:], op0=ALU.mult, op1=ALU.add),
                                  [fb, fa, nlam], [fo])
                            tr.op("pool", lambda e: e.tensor_tensor(out=fsq[:], in0=fo[:], in1=fo[:], op=ALU.mult),
                                  [fo], [fsq])

                        def fin2(h=h, t0=t0, qt=qt):
                            tr.op("pe", lambda e: e.matmul(ps_d[0][:, :], lhsT=ones_f[:, :], rhs=fsq[:], start=True,
                                                           stop=True), [ones_f, fsq], [ps_d[0]])
                            rstd_from_ss(ps_d[0], ps_d[0][:, :], frs, frs[:], 1.0 / 128)
                            ot = OTst[ocntB[0] % 2]
                            ocntB[0] += 1
                            tr.op("dve", lambda e: e.scalar_tensor_tensor(out=ot[:], in0=fo[:], scalar=c_out,
                                                                          in1=frs[:], op0=ALU.mult, op1=ALU.mult),
                                  [fo, frs], [ot])
                            tr.dma(OTd[4 + h, :, t0 + qt * 512:t0 + (qt + 1) * 512], ot[:], ot, False)
                        pendB.append((1, fin1))
                        pendB.append((3, fin2))
            while pendB:
                pendB.pop(0)[1]()
            ph.close()

        def phase_p3a(l):
            even = (l % 2 == 0)
            li = l // 2
            TT = 512
            ph = Phase(tr, nc)
            Wb = ph.sb([128, 8 * D], BF16, name="Wout")
            load_weight(ph, Wb, (w_out_even if even else w_out_odd)[li], 8, D, nch=1024)
            gpost = ph.sb([128, D], F32, dma=True)
            tr.dma(gpost[:], gpost_mix[l], gpost, True)
            xts = [ph.sb([128, 4 * D], F32, dma=True, name="xt") for _ in range(2)]
            OTs = [ph.sb([128, 8 * TT], BF16, dma=True, name="OT") for _ in range(2)]
            junk = ph.sb([128, D], BF16)
            tmp = ph.sb([128, D], F32)
            ss2 = ph.sb([128, 4], F32)
            rstd2 = ph.sb([128, 4], F32)
            ps_m = [ph.ps([128, 1024], F32, name="ps_m") for _ in range(2)]
            tiles = [t0 + i * TT for (t0, S) in seqs for i in range(S // TT)]

            def issue_loads(ti):
                g0 = tiles[ti]
                xt = xts[ti % 2]
                tr.dma(view(xt[:, 0:1], [(D, 4), (1, D)]),
                       x_rows(l == 0, g0, TT).rearrange("(j p) d -> p j d", p=128), xt, True)
                ot = OTs[ti % 2]
                tr.dma(view(ot[:, 0:1], [(TT, 8), (1, TT)]), OTd[:, :, g0:g0 + TT].rearrange("c p s -> p c s"),
                       ot, True)

            issue_loads(0)
            k = 0
            for ti in range(len(tiles)):
                g0 = tiles[ti]
                if ti + 1 < len(tiles):
                    issue_loads(ti + 1)
                xt = xts[ti % 2]
                ot = OTs[ti % 2]
                for j in range(4):
                    pm = ps_m[k % 2]
                    k += 1
                    def f(e):
                        r = None
                        for half in range(2):
                            for c in range(8):
                                r = e.matmul(pm[:, half * 512:(half + 1) * 512],
                                             lhsT=ot[:, c * TT + j * 128:c * TT + (j + 1) * 128],
                                             rhs=Wb[:, c * D + half * 512:c * D + (half + 1) * 512],
                                             start=(c == 0), stop=(c == 7))
                        return r
                    tr.op("pe", f, [ot, Wb], [pm])
                    post_residual(pm, xt, j, ss2, rstd2, junk, tmp, gpost)
                tr.dma(x_rows(False, g0, TT).rearrange("(j p) d -> p j d", p=128),
                       view(xt[:, 0:1], [(D, 4), (1, D)]), xt, False)
            ph.close()

        def phase_p3b(l):
            TT = 256
            NJ = TT // 128
            ph = Phase(tr, nc)
            Wgu = ph.sb([128, 8 * 2 * FFN_H], BF16, name="Wgu")
            Wd = ph.sb([128, NJH * D], BF16, name="Wd")
            gT = ph.sb([128, 8], F32, dma=True)
            tr.dma(gT[:], gpreT_ffn[l], gT, True)
            load_weight(ph, Wgu, w_gate_up[l], 8, 2 * FFN_H, gT=gT)
            load_weight(ph, Wd, w_down[l], NJH, D, nch=1024)
            gpost = ph.sb([128, D], F32, dma=True)
            tr.dma(gpost[:], gpost_ffn[l], gpost, True)
            xts = [ph.sb([128, NJ * D], F32, dma=True, name="xt") for _ in range(2)]
            hTs = [ph.sb([128, 8 * TT], BF16, name="hT") for _ in range(2)]
            aT = ph.sb([128, NJH * TT], BF16, name="aT")
            hb = ph.sb([128, D], BF16)
            junk = ph.sb([128, D], BF16)
            tmp = ph.sb([128, D], F32)
            sg = [ph.sb([128, TT], F32, name="sg") for _ in range(2)]
            ss = ph.sb([128, 4], F32)
            rstd = ph.sb([128, 4], F32)
            ss2 = ph.sb([128, 4], F32)
            rstd2 = ph.sb([128, 4], F32)
            psT = ph.ps([128, 1024], BF16, name="psT")
            ps_gu = [ph.ps([128, 512], F32, name="ps_gu") for _ in range(3)]
            ps_y = [ph.ps([128, 1024], F32, name="ps_y") for _ in range(2)]
            tiles = [t0 + i * TT for (t0, S) in seqs for i in range(S // TT)]

            def issue_loads(ti):
                g0 = tiles[ti]
                xt = xts[ti % 2]
                tr.dma(view(xt[:, 0:1], [(D, NJ), (1, D)]),
                       x_rows(False, g0, TT).rearrange("(j p) d -> p j d", p=128), xt, True)

            def chain(ti_):
                for j_ in range(NJ):
                    norm_transpose(xts[ti_ % 2], j_, ss, rstd, junk, hb, psT, hTs[ti_ % 2], TT)

            issue_loads(0)
            if len(tiles) > 1:
                issue_loads(1)
            chain(0)
            kg = 0
            ky = 0
            for ti in range(len(tiles)):
                g0 = tiles[ti]
                xt = xts[ti % 2]
                hT = hTs[ti % 2]
                for jh in range(NJH):
                    pg = ps_gu[kg % 3]
                    sgb = sg[kg % 2]
                    kg += 1
                    def f(e):
                        r = None
                        for gu in range(2):
                            col = gu * FFN_H + jh * 128
                            for c in range(8):
                                r = e.matmul(pg[:, gu * TT:(gu + 1) * TT],
                                             lhsT=Wgu[:, c * 2 * FFN_H + col:c * 2 * FFN_H + col + 128],
                                             rhs=hT[:, c * TT:(c + 1) * TT], start=(c == 0), stop=(c == 7))
                        return r
                    tr.op("pe", f, [Wgu, hT], [pg])
                    tr.op("act", lambda e: e.activation(out=sgb[:], in_=pg[:, 0:TT], func=AF.Silu), [pg], [sgb])
                    tr.op("dve", lambda e: e.tensor_tensor(out=aT[:, jh * TT:(jh + 1) * TT], in0=pg[:, TT:2 * TT],
                                                           in1=sgb[:], op=ALU.mult), [pg, sgb], [aT])
                if ti + 1 < len(tiles):
                    chain(ti + 1)
                for j in range(NJ):
                    py = ps_y[ky % 2]
                    ky += 1
                    def f2(e):
                        r = None
                        for half in range(2):
                            for jh in range(NJH):
                                r = e.matmul(py[:, half * 512:(half + 1) * 512],
                                             lhsT=aT[:, jh * TT + j * 128:jh * TT + (j + 1) * 128],
                                             rhs=Wd[:, jh * D + half * 512:jh * D + (half + 1) * 512],
                                             start=(jh == 0), stop=(jh == NJH - 1))
                        return r
                    tr.op("pe", f2, [aT, Wd], [py])
                    post_residual(py, xt, j, ss2, rstd2, junk, tmp, gpost)
                tr.dma(x_rows(False, g0, TT).rearrange("(j p) d -> p j d", p=128),
                       view(xt[:, 0:1], [(D, NJ), (1, D)]), xt, False)
                if ti + 2 < len(tiles):
                    issue_loads(ti + 2)
            ph.close()

        tr.barrier()
        import os
        dbg = os.environ.get("KPHASES")
        for l in range(depth):
            if os.environ.get("KLAYERS") and str(l) not in os.environ["KLAYERS"]:
                continue
            dbg = os.environ.get("KPH%d" % l, os.environ.get("KPHASES"))
            if dbg is None or "1" in dbg:
                phase_p1(l)
            if l % 2 == 0:
                if dbg is None or "a" in dbg:
                    phase_p2_even_a(l)
                if dbg is None or "b" in dbg:
                    phase_p2_even_b(l)
            else:
                if dbg is None or "c" in dbg:
                    phase_p2_odd(l)
            if dbg is None or "3" in dbg:
                phase_p3a(l)
            if dbg is None or "4" in dbg:
                phase_p3b(l)
        gph.close()
    return nc


def _rope_table(S):
    t = np.arange(S)
    inv_ax = (10000.0 ** (-np.arange(0, 32, 2, dtype=np.float32) / 32)).astype(np.float32)
    row = (t // 64).astype(np.float32)[:, None] * inv_ax[None, :]
    col = (t % 64).astype(np.float32)[:, None] * inv_ax[None, :]
    inv_p = (500000.0 ** (-np.arange(0, 16, 2, dtype=np.float32) / 16)).astype(np.float32)
    ang = t.astype(np.float32)[:, None] * inv_p[None, :]
    tab = np.zeros((S, 160), np.float32)
    rc, rs, cc, cs = np.cos(row), np.sin(row), np.cos(col), np.sin(col)
    tab[:, 0:16] = rc; tab[:, 16:32] = rc; tab[:, 32:48] = cc; tab[:, 48:64] = cc
    tab[:, 64:80] = -rs; tab[:, 80:96] = rs; tab[:, 96:112] = -cs; tab[:, 112:128] = cs
    pc, ps = np.cos(ang), np.sin(ang)
    tab[:, 128:136] = pc; tab[:, 136:144] = pc
    tab[:, 144:152] = -ps; tab[:, 152:160] = ps
    return tab


def _mask_const():
    k = np.arange(128)[:, None]
    q = np.arange(512)[None, :]
    m = np.zeros((128, 6 * 512), np.float32)
    for jm in range(6):
        j = jm - 1
        m[:, jm * 512:(jm + 1) * 512] = (np.abs(q - (j * 128 + k)) <= 128).astype(np.float32)
    return m


_CACHE = {}


def run(inputs, Sp, Ss, n_cores, depth=DEPTH, trace=False):
    key = (Sp, Ss, depth)
    if key not in _CACHE:
        _CACHE[key] = build_program(Sp, Ss, depth)
    nc = _CACHE[key]
    f = lambda a: np.ascontiguousarray(np.asarray(a, dtype=np.float32))
    xpf, xsf = f(inputs["x_prompt"]), f(inputs["x_sample"])
    nbs = xsf.shape[0]
    qk = f(inputs["qk_norm_a"])
    gqk = np.stack([np.broadcast_to(np.concatenate([np.tile(qk[i, 0], 8), np.tile(qk[i, 1], 2)])[None, :], (128, 640))
                    for i in range(2)])
    dl = f(inputs["diff_lambda"]).reshape(2, 1, 256)
    shared = {
        "w_in_even": f(inputs["w_in_even"]), "w_out_even": f(inputs["w_out_even"]),
        "w_in_odd": f(inputs["w_in_odd"]), "w_out_odd": f(inputs["w_out_odd"]),
        "w_gate_up": f(inputs["w_gate_up"]), "w_down": f(inputs["w_down"]),
        "gpreT_mix": f(f(inputs["norm_mix_pre"]).reshape(4, 8, 128).transpose(0, 2, 1)),
        "gpreT_ffn": f(f(inputs["norm_ffn_pre"]).reshape(4, 8, 128).transpose(0, 2, 1)),
        "gpost_mix": f(np.broadcast_to(f(inputs["norm_mix_post"])[:, None, :], (4, 128, D))),
        "gpost_ffn": f(np.broadcast_to(f(inputs["norm_ffn_post"])[:, None, :], (4, 128, D))),
        "gqk": f(gqk), "dlam": f(np.broadcast_to(dl, (2, 128, 256))),
        "sinkc": f(inputs["sink_c"]),
        "tab": _rope_table(max(Sp, Ss)), "maskc": _mask_const(), "identc": np.eye(128, dtype=np.float32),
    }
    in_maps = []
    for i in range(n_cores):
        m = dict(shared)
        m["xp"] = xpf[i]
        m["xs"] = xsf[i % nbs]
        in_maps.append(m)
    res = run_bass_kernel_spmd(nc, in_maps, core_ids=list(range(n_cores)), trace=trace)
    yp = np.stack([res.results[i]["yp"] for i in range(n_cores)]).astype(np.float32)
    ysm = np.stack([res.results[i]["ys"] for i in range(nbs)]).astype(np.float32)
    return (yp, ysm), res


def kernel(**inputs):
    (yp, ysm), _ = run(inputs, 8192, 4096, NCORES)
    return (yp, ysm)
```

```python
import math
from contextlib import ExitStack

import numpy as np
import concourse.bass as bass
import concourse.mybir as mybir
from concourse.bass_utils import run_bass_kernel_spmd

F32 = mybir.dt.float32
BF16 = mybir.dt.bfloat16
ALU = mybir.AluOpType
AF = mybir.ActivationFunctionType
AX = mybir.AxisListType

D = 1024
DEPTH = 4
HD = 64
EPS = 1e-6
FFN_H = 2816
NJH = FFN_H // 128
EVEN_IN = 2304
ODD_IN = 1536
SCALE = HD ** -0.5
NCORES = 8


def lam_init_of(l):
    return 0.8 - 0.6 * math.exp(-0.3 * l)


class Buf:
    __slots__ = ("t", "lw", "rd", "dsem")

    def __init__(self, t, dsem=None):
        self.t = t
        self.lw = None
        self.rd = {}
        self.dsem = dsem

    def __getitem__(self, k):
        return self.t[k]


def view(ap, dims):
    return bass.AP(ap.tensor, ap.offset, [list(ap.ap[0])] + [list(d) for d in dims])


class TR:
    def __init__(self, nc, es, n_dma_sems=48):
        import os
        n_dma_sems = int(os.environ.get("KNSEM", n_dma_sems))
        self.nc = nc
        self.E = {"pe": nc.tensor, "act": nc.scalar, "dve": nc.vector, "pool": nc.gpsimd, "sp": nc.sync}
        self.sems = {}
        self.cnt = {}
        for e in ("pe", "act", "dve", "pool"):
            self.sems[e] = es.enter_context(nc.semaphore("s_" + e))
            self.cnt[e] = 0
        self.free_dsems = []
        for i in range(n_dma_sems):
            n = "d%d" % i
            self.sems[n] = es.enter_context(nc.semaphore("s_" + n))
            self.cnt[n] = 0
            self.free_dsems.append(n)
        self.waited = {e: {} for e in self.E}
        self.inflight = {}
        import os
        self.max_inflight = int(os.environ.get("KINFLIGHT", "3"))
        self.nopool = bool(os.environ.get("KNOPOOL"))

    def _wait(self, eng, toks):
        E = self.E[eng]
        w = self.waited[eng]
        for s, v in toks.items():
            if eng == "pe" and s == "pe":
                continue
            if w.get(s, 0) < v:
                E.wait_ge(self.sems[s], v)
                w[s] = v

    @staticmethod
    def _collect(reads, writes):
        toks = {}
        for b in reads:
            if b.lw is not None:
                s, v = b.lw
                if toks.get(s, 0) < v:
                    toks[s] = v
        for b in writes:
            if b.lw is not None:
                s, v = b.lw
                if toks.get(s, 0) < v:
                    toks[s] = v
            for s, v in b.rd.items():
                if toks.get(s, 0) < v:
                    toks[s] = v
        return toks

    def op(self, eng, fn, reads=(), writes=()):
        if eng == "pool" and self.nopool:
            eng = "dve"
        self._wait(eng, self._collect(reads, writes))
        inst = fn(self.E[eng])
        self.cnt[eng] += 1
        inst.then_inc(self.sems[eng], 1)
        tok = (eng, self.cnt[eng])
        for b in writes:
            b.lw = tok
            b.rd = {}
        for b in reads:
            if b.rd.get(eng, 0) < tok[1]:
                b.rd[eng] = tok[1]
        return tok

    def dma(self, out_ap, in_ap, buf, load, q="sp"):
        if load:
            toks = self._collect((), (buf,))
        else:
            toks = self._collect((buf,), ())
        fl = self.inflight.setdefault(q, [])
        while len(fl) >= self.max_inflight:
            s0, v0 = fl.pop(0)
            if toks.get(s0, 0) < v0:
                toks[s0] = v0
        self._wait(q, toks)
        s = buf.dsem
        inst = self.E[q].dma_start(out=out_ap, in_=in_ap)
        self.cnt[s] += 16
        inst.then_inc(self.sems[s], 16)
        tok = (s, self.cnt[s])
        fl.append(tok)
        if load:
            buf.lw = tok
            buf.rd = {}
        else:
            buf.rd[s] = tok[1]
        return tok

    def barrier(self):
        for e in self.E:
            self._wait(e, dict(self.cnt))


_UID = [0]


class Phase:
    def __init__(self, tr, nc):
        self.tr = tr
        self.nc = nc
        self.es = ExitStack()
        self.dsems = []
        self.k = 0

    def sb(self, shape, dt, dma=False, name=None):
        _UID[0] += 1
        t = self.es.enter_context(self.nc.sbuf_tensor("%s_%d" % (name or "sb", _UID[0]), list(shape), dt))
        ds = None
        if dma:
            ds = self.tr.free_dsems.pop()
            self.dsems.append(ds)
        return Buf(t, ds)

    def ps(self, shape, dt, name=None):
        _UID[0] += 1
        t = self.es.enter_context(self.nc.psum_tensor("%s_%d" % (name or "ps", _UID[0]), list(shape), dt))
        return Buf(t)

    def close(self):
        self.tr.barrier()
        self.es.close()
        self.tr.free_dsems.extend(self.dsems)
        self.dsems = []


def build_program(Sp, Ss, depth=DEPTH):
    T = Sp + Ss
    seqs = [(0, Sp), (Sp, Ss)]
    nc = bass.Bass("TRN2", target_bir_lowering=False)

    def din(name, shape, dt=F32):
        return nc.dram_tensor(name, list(shape), dt, kind="ExternalInput").ap()

    xp = din("xp", [Sp, D])
    xs = din("xs", [Ss, D])
    w_in_even = din("w_in_even", [2, D, EVEN_IN])
    w_out_even = din("w_out_even", [2, D, D])
    w_in_odd = din("w_in_odd", [2, D, ODD_IN])
    w_out_odd = din("w_out_odd", [2, D, D])
    w_gate_up = din("w_gate_up", [4, D, 2 * FFN_H])
    w_down = din("w_down", [4, FFN_H, D])
    gpreT_mix = din("gpreT_mix", [4, 128, 8])
    gpreT_ffn = din("gpreT_ffn", [4, 128, 8])
    gpost_mix = din("gpost_mix", [4, 128, D])
    gpost_ffn = din("gpost_ffn", [4, 128, D])
    gqk = din("gqk", [2, 128, 640])
    dlam = din("dlam", [2, 128, 256])
    sinkc = din("sinkc", [2, 16])
    tab = din("tab", [max(Sp, Ss), 160])
    maskc = din("maskc", [128, 6 * 512])
    identc = din("identc", [128, 128])
    yp = nc.dram_tensor("yp", [Sp, D], F32, kind="ExternalOutput").ap()
    ys = nc.dram_tensor("ys", [Ss, D], F32, kind="ExternalOutput").ap()
    QTd = nc.dram_tensor("QTd", [8, 128, T], BF16).ap()
    KTd = nc.dram_tensor("KTd", [5, 128, T], BF16).ap()
    Vd = nc.dram_tensor("Vd", [T, 640], BF16).ap()
    OTd = nc.dram_tensor("OTd", [8, 128, T], BF16).ap()

    def x_rows(src_is_input, t0, n):
        if t0 < Sp:
            base = xp if src_is_input else yp
            return base[t0:t0 + n, :]
        base = xs if src_is_input else ys
        return base[t0 - Sp:t0 - Sp + n, :]

    with ExitStack() as es:
        tr = TR(nc, es)
        gph = Phase(tr, nc)
        ident_f = gph.sb([128, 128], F32, dma=True)
        ident = gph.sb([128, 128], BF16)
        ones_f = gph.sb([128, 128], F32)
        ones_b = gph.sb([128, 128], BF16)
        mask_f = gph.sb([128, 6 * 512], F32, dma=True)
        mask_b = gph.sb([128, 6 * 512], BF16)
        tr.dma(ident_f[:], identc[:, :], ident_f, True)
        tr.op("dve", lambda e: e.tensor_copy(out=ident[:], in_=ident_f[:]), [ident_f], [ident])
        tr.op("pool", lambda e: e.memset(ones_f[:], 1.0), [], [ones_f])
        tr.op("pool", lambda e: e.memset(ones_b[:], 1.0), [], [ones_b])
        eps_t = gph.sb([128, 1], F32)
        tr.op("pool", lambda e: e.memset(eps_t[:], EPS), [], [eps_t])
        tr.dma(mask_f[:], maskc[:, :], mask_f, True)
        tr.op("dve", lambda e: e.tensor_copy(out=mask_b[:], in_=mask_f[:]), [mask_f], [mask_b])

        def load_weight(ph, Wb, Wd, C, N, gT=None, nch=1408):
            sub = Phase(tr, nc)
            stg = [sub.sb([128, nch], F32, dma=True, name="wst") for _ in range(3)]
            k = 0
            for c in range(C):
                for n0 in range(0, N, nch):
                    w = min(nch, N - n0)
                    st = stg[k % 3]
                    tr.dma(st[:, 0:w], Wd[c * 128:(c + 1) * 128, n0:n0 + w], st, True)
                    dst = Wb[:, c * N + n0:c * N + n0 + w]
                    if gT is not None:
                        if k % 2 == 0:
                            tr.op("dve", lambda e: e.tensor_scalar(
                                out=dst, in0=st[:, 0:w], scalar1=gT[:, c:c + 1], scalar2=None, op0=ALU.mult),
                                [st, gT], [Wb])
                        else:
                            tr.op("act", lambda e: e.activation(out=dst, in_=st[:, 0:w], func=AF.Copy,
                                                                scale=gT[:, c:c + 1]), [st, gT], [Wb])
                    else:
                        eng = ("dve", "pool")[k % 2]
                        tr.op(eng, lambda e: e.tensor_copy(out=dst, in_=st[:, 0:w]), [st], [Wb])
                    k += 1
            sub.close()

        def rstd_from_ss(eng_ss_buf, ss_ap, out_buf, out_ap, inv_n):
            tr.op("act", lambda e: e.activation(out=out_ap, in_=ss_ap, func=AF.Ln, scale=inv_n, bias=eps_t[:, 0:1]),
                  [eng_ss_buf, eps_t], [out_buf])
            tr.op("act", lambda e: e.activation(out=out_ap, in_=out_ap, func=AF.Exp, scale=-0.5),
                  [out_buf], [out_buf])

        def norm_transpose(xt, j, ss, rstd, junk, hb, psT, hT, TT, jo=None):
            if jo is None:
                jo = j
            xj = xt[:, j * D:(j + 1) * D]
            tr.op("act", lambda e: e.activation(out=junk[:], in_=xj, func=AF.Square, accum_out=ss[:, j:j + 1]),
                  [xt], [junk, ss])
            rstd_from_ss(ss, ss[:, j:j + 1], rstd, rstd[:, j:j + 1], 1.0 / D)
            tr.op("dve", lambda e: e.tensor_scalar(out=hb[:], in0=xj, scalar1=rstd[:, j:j + 1], scalar2=None,
                                                   op0=ALU.mult), [xt, rstd], [hb])

            def tps(e):
                r = None
                for c in range(8):
                    r = e.transpose(psT[:, c * 128:(c + 1) * 128], hb[:, c * 128:(c + 1) * 128], ident[:])
                return r
            tr.op("pe", tps, [hb, ident], [psT])
            tr.op("act", lambda e: e.activation(
                out=view(hT[:, jo * 128:jo * 128 + 1], [(TT, 8), (1, 128)]),
                in_=view(psT[:, 0:1], [(128, 8), (1, 128)]), func=AF.Copy), [psT], [hT])

        def post_residual(ps_y, xt, j, ss2, rstd2, junk, tmp, gpost):
            xj = xt[:, j * D:(j + 1) * D]
            tr.op("act", lambda e: e.activation(out=junk[:], in_=ps_y[:, 0:D], func=AF.Square,
                                                accum_out=ss2[:, j:j + 1]), [ps_y], [junk, ss2])
            rstd_from_ss(ss2, ss2[:, j:j + 1], rstd2, rstd2[:, j:j + 1], 1.0 / D)
            tr.op("dve", lambda e: e.scalar_tensor_tensor(out=tmp[:], in0=ps_y[:, 0:D], scalar=rstd2[:, j:j + 1],
                                                          in1=gpost[:], op0=ALU.mult, op1=ALU.mult),
                  [ps_y, rstd2, gpost], [tmp])
            tr.op("pool", lambda e: e.tensor_tensor(out=xj, in0=xj, in1=tmp[:], op=ALU.add), [xt, tmp], [xt])

        def rope_small(ps_src, nh, tb, j, ra, rb, dst, dst_off):
            import os as _os4
            RS = _os4.environ.get("KROPE", "")
            tbj = tb[:, j * 160:(j + 1) * 160]
            x_all = view(ps_src[:, 0:1], [(64, nh), (1, 16)])
            cc = view(tbj[:, 128:129], [(0, nh), (1, 16)])
            if "1" in RS:
                return
            tr.op("dve", lambda e: e.tensor_tensor(out=view(ra[:, 0:1], [(16, nh), (1, 16)]), in0=x_all, in1=cc,
                                                   op=ALU.mult), [ps_src, tb], [ra])
            if "2" in RS:
                return
            x_hi = view(ps_src[:, 8:9], [(64, nh), (1, 8)])
            x_lo = view(ps_src[:, 0:1], [(64, nh), (1, 8)])
            s_neg = view(tbj[:, 144:145], [(0, nh), (1, 8)])
            s_pos = view(tbj[:, 152:153], [(0, nh), (1, 8)])
            tr.op("dve", lambda e: e.tensor_tensor(out=view(rb[:, 0:1], [(16, nh), (1, 8)]), in0=x_hi, in1=s_neg,
                                                   op=ALU.mult), [ps_src, tb], [rb])
            tr.op("dve", lambda e: e.tensor_tensor(out=view(rb[:, 8:9], [(16, nh), (1, 8)]), in0=x_lo, in1=s_pos,
                                                   op=ALU.mult), [ps_src, tb], [rb])
            if "3" in RS:
                return
            tr.op("pool", lambda e: e.tensor_tensor(out=view(dst[:, dst_off:dst_off + 1], [(64, nh), (1, 16)]),
                                                    in0=view(ra[:, 0:1], [(16, nh), (1, 16)]),
                                                    in1=view(rb[:, 0:1], [(16, nh), (1, 16)]), op=ALU.add),
                  [ra, rb], [dst])

        def transposes_to_stage(src, src_off, nchunk, psT2, stage, st_chunk0, j, TT):
            def tps(e):
                r = None
                for c in range(nchunk):
                    r = e.transpose(psT2[:, c * 128:(c + 1) * 128],
                                    src[:, src_off + c * 128:src_off + (c + 1) * 128], ident[:])
                return r
            tr.op("pe", tps, [src, ident], [psT2])
            tr.op("act", lambda e: e.activation(
                out=view(stage[:, st_chunk0 * TT + j * 128:st_chunk0 * TT + j * 128 + 1], [(TT, nchunk), (1, 128)]),
                in_=view(psT2[:, 0:1], [(128, nchunk), (1, 128)]), func=AF.Copy), [psT2], [stage])

        def phase_p1(l):
            even = (l % 2 == 0)
            li = l // 2
            NIN = EVEN_IN if even else ODD_IN
            TT = 512
            ph = Phase(tr, nc)
            Wb = ph.sb([128, 8 * NIN], BF16, name="Win")
            import os as _os6
            if not even and not _os6.environ.get("KNODUMMY"):
                dummy2 = ph.sb([128, 8], F32, dma=True)
            gT = ph.sb([128, 8], F32, dma=True)
            tr.dma(gT[:], gpreT_mix[l], gT, True)
            load_weight(ph, Wb, (w_in_even if even else w_in_odd)[li], 8, NIN, gT=gT, nch=(1152 if even else 768))
            NQC = 8
            NKC = 5 if even else 2
            NV = 640 if even else 256
            xts = [ph.sb([128, 4 * D], F32, dma=True, name="xt") for _ in range(2)]
            tbs = [ph.sb([128, 4 * 160], F32, dma=True, name="tb") for _ in range(2)]
            QTst = [ph.sb([128, NQC * TT], BF16, dma=True, name="QTst") for _ in range(2)]
            KTst = [ph.sb([128, NKC * TT], BF16, dma=True, name="KTst") for _ in range(2)]
            Vst = [ph.sb([128, 4 * NV], BF16, dma=True, name="Vst") for _ in range(2)]
            hTs = [ph.sb([128, 8 * 128], BF16, name="hT") for _ in range(2)]
            hb = ph.sb([128, D], BF16, name="hb")
            junk = ph.sb([128, D], BF16, name="junk")
            ss = ph.sb([128, 4], F32)
            rstd = ph.sb([128, 4], F32)
            ra = ph.sb([128, 256], F32)
            rb = ph.sb([128, 256], F32)
            psT = ph.ps([128, 1024], BF16, name="psT")
            if even:
                gq = ph.sb([128, 640], F32, dma=True)
                tr.dma(gq[:], gqk[li], gq, True)
                sqs = ph.sb([128, 640], F32)
                ssh = ph.sb([128, 10], F32)
                rsh = ph.sb([128, 10], F32)
                tq = ph.sb([128, 640], F32)
                ta = ph.sb([128, 640], F32)
                tbb = ph.sb([128, 640], F32)
                qkb = ph.sb([128, 640], BF16)
                qbb = ph.sb([128, 512], BF16)
                kbb = ph.sb([128, 512], BF16)
                psA = ph.ps([128, 1024], F32, name="psA")
                psQB = ph.ps([128, 512], F32, name="psQB")
                psKB = ph.ps([128, 512], F32, name="psKB")
                psVB = ph.ps([128, 512], F32, name="psVB")
                psT2a = ph.ps([128, 1024], BF16, name="psT2a")
                psT2b = ph.ps([128, 1024], BF16, name="psT2b")
            else:
                import os as _os5
                if _os5.environ.get("KDUMMY"):
                    dummy = ph.sb([128, int(_os5.environ["KDUMMY"])], F32, dma=True)
                qb16 = ph.sb([128, 1024], BF16)
                kb16 = ph.sb([128, 256], BF16)
                psQ0 = ph.ps([128, 512], F32, name="psQ0")
                psQ1 = ph.ps([128, 512], F32, name="psQ1")
                psKV = ph.ps([128, 512], F32, name="psKV")
                psT2a = ph.ps([128, 1024], BF16, name="psT2a")
                psT2b = ph.ps([128, 1024], BF16, name="psT2b")

            qfs = [ph.sb([128, 512], F32, name="qf") for _ in range(2)]
            qfc = [0]

            def evac_rope(ps_buf, ps_off, ncols, nh, dst, dst_off, tb, j):
                qf = qfs[qfc[0] % 2]
                qfc[0] += 1
                tr.op("act", lambda e: e.activation(out=qf[:, 0:ncols], in_=ps_buf[:, ps_off:ps_off + ncols],
                                                    func=AF.Copy), [ps_buf], [qf])
                tr.op("pool", lambda e: e.tensor_copy(out=dst[:, dst_off:dst_off + ncols], in_=qf[:, 0:ncols]),
                      [qf], [dst])
                rope_small(qf, nh, tb, j, ra, rb, dst, dst_off)

            tiles = [(t0 + i * TT, t0, i * TT) for (t0, S) in seqs for i in range(S // TT)]

            def issue_loads(ti):
                g0, t0, off = tiles[ti]
                xt = xts[ti % 2]
                import os as _os3
                tr.dma(view(xt[:, 0:1], [(D, 4), (1, D)]),
                       x_rows(l == 0 or bool(_os3.environ.get("KXIN")), g0, TT).rearrange("(j p) d -> p j d", p=128), xt, True)
                tb = tbs[ti % 2]
                tr.dma(view(tb[:, 0:1], [(160, 4), (1, 160)]),
                       tab[off:off + TT, :].rearrange("(j p) d -> p j d", p=128), tb, True)

            hcur = [None]

            def mm_group(ps_buf, ps_off, j, col0, ncols):
                hT = hcur[0]
                def f(e):
                    r = None
                    for c in range(8):
                        r = e.matmul(ps_buf[:, ps_off:ps_off + ncols],
                                     lhsT=hT[:, c * 128:(c + 1) * 128],
                                     rhs=Wb[:, c * NIN + col0:c * NIN + col0 + ncols],
                                     start=(c == 0), stop=(c == 7))
                    return r
                return f

            def chain(n):
                ti_, j_ = n // 4, n % 4
                norm_transpose(xts[ti_ % 2], j_, ss, rstd, junk, hb, psT, hTs[n % 2], 128, jo=0)

            issue_loads(0)
            if len(tiles) > 1:
                issue_loads(1)
            chain(0)
            for ti in range(len(tiles)):
                g0, t0, off = tiles[ti]
                xt = xts[ti % 2]
                tb = tbs[ti % 2]
                qst, kst, vst = QTst[ti % 2], KTst[ti % 2], Vst[ti % 2]
                for j in range(4):
                    n = ti * 4 + j
                    if n + 1 < 4 * len(tiles):
                        chain(n + 1)
                    hT = hTs[n % 2]
                    hcur[0] = hT
                    tbj = tb[:, j * 160:(j + 1) * 160]
                    if even:
                        tr.op("pe", mm_group(psA, 0, j, 0, 512), [hT, Wb], [psA])
                        tr.op("pe", mm_group(psA, 512, j, 512, 256), [hT, Wb], [psA])
                        tr.op("pe", mm_group(psQB, 0, j, 768, 512), [hT, Wb], [psQB])
                        tr.op("pe", mm_group(psKB, 0, j, 1280, 512), [hT, Wb], [psKB])
                        tr.op("pe", mm_group(psVB, 0, j, 1792, 512), [hT, Wb], [psVB])
                        tr.op("act", lambda e: e.activation(out=sqs[:], in_=psA[:, 0:640], func=AF.Square),
                              [psA], [sqs])
                        tr.op("dve", lambda e: e.tensor_reduce(out=ssh[:], in_=view(sqs[:, 0:1], [(64, 10), (1, 64)]),
                                                               axis=AX.X, op=ALU.add), [sqs], [ssh])
                        rstd_from_ss(ssh, ssh[:], rsh, rsh[:], 1.0 / HD)
                        tr.op("dve", lambda e: e.tensor_tensor(
                            out=view(tq[:, 0:1], [(64, 10), (1, 64)]), in0=view(psA[:, 0:1], [(64, 10), (1, 64)]),
                            in1=view(rsh[:, 0:1], [(1, 10), (0, 64)]), op=ALU.mult), [psA, rsh], [tq])
                        tr.op("act", lambda e: e.activation(out=vst[:, j * 640:j * 640 + 128], in_=psA[:, 640:768],
                                                            func=AF.Copy), [psA], [vst])
                        tr.op("pool", lambda e: e.tensor_tensor(out=tq[:], in0=tq[:], in1=gq[:], op=ALU.mult),
                              [tq, gq], [tq])
                        tr.op("pool", lambda e: e.tensor_tensor(
                            out=view(ta[:, 0:1], [(64, 10), (1, 64)]), in0=view(tq[:, 0:1], [(64, 10), (1, 64)]),
                            in1=view(tbj[:, 0:1], [(0, 10), (1, 64)]), op=ALU.mult), [tq, tb], [ta])
                        tr.op("pool", lambda e: e.tensor_tensor(
                            out=view(tbb[:, 0:1], [(64, 10), (32, 2), (1, 16)]),
                            in0=view(tq[:, 16:17], [(64, 10), (32, 2), (1, 16)]),
                            in1=view(tbj[:, 64:65], [(0, 10), (32, 2), (1, 16)]), op=ALU.mult), [tq, tb], [tbb])
                        tr.op("pool", lambda e: e.tensor_tensor(
                            out=view(tbb[:, 16:17], [(64, 10), (32, 2), (1, 16)]),
                            in0=view(tq[:, 0:1], [(64, 10), (32, 2), (1, 16)]),
                            in1=view(tbj[:, 80:81], [(0, 10), (32, 2), (1, 16)]), op=ALU.mult), [tq, tb], [tbb])
                        tr.op("pool", lambda e: e.tensor_tensor(out=qkb[:], in0=ta[:], in1=tbb[:], op=ALU.add),
                              [ta, tbb], [qkb])
                        transposes_to_stage(qkb, 0, 4, psT2a, qst, 0, j, TT)
                        transposes_to_stage(qkb, 512, 1, psT2b, kst, 0, j, TT)
                        evac_rope(psQB, 0, 512, 8, qbb, 0, tb, j)
                        transposes_to_stage(qbb, 0, 4, psT2a, qst, 4, j, TT)
                        evac_rope(psKB, 0, 512, 8, kbb, 0, tb, j)
                        transposes_to_stage(kbb, 0, 4, psT2b, kst, 1, j, TT)
                        tr.op("act", lambda e: e.activation(out=vst[:, j * 640 + 128:(j + 1) * 640],
                                                            in_=psVB[:, 0:512], func=AF.Copy), [psVB], [vst])
                    else:
                        import os as _os
                        SK = _os.environ.get("KSKIP", "")
                        tr.op("pe", mm_group(psQ0, 0, j, 0, 512), [hT, Wb], [psQ0])
                        tr.op("pe", mm_group(psQ1, 0, j, 512, 512), [hT, Wb], [psQ1])
                        tr.op("pe", mm_group(psKV, 0, j, 1024, 512), [hT, Wb], [psKV])
                        evac_rope(psQ0, 0, 512, 8, qb16, 0, tb, j)
                        evac_rope(psQ1, 0, 512, 8, qb16, 512, tb, j)
                        if "c" not in SK:
                            transposes_to_stage(qb16, 0, 8, psT2a, qst, 0, j, TT)
                        evac_rope(psKV, 0, 256, 4, kb16, 0, tb, j)
                        if "e" not in SK:
                            transposes_to_stage(kb16, 0, 2, psT2b, kst, 0, j, TT)
                        tr.op("act", lambda e: e.activation(out=vst[:, j * 256:(j + 1) * 256], in_=psKV[:, 256:512],
                                                            func=AF.Copy), [psKV], [vst])
                if ti + 2 < len(tiles):
                    issue_loads(ti + 2)
                import os as _os2
                SK2 = _os2.environ.get("KSKIP", "")
                if "q" not in SK2:
                    tr.dma(QTd[0:NQC, :, g0:g0 + TT].rearrange("c p s -> p c s"),
                           view(qst[:, 0:1], [(TT, NQC), (1, TT)]), qst, False)
                if "k" not in SK2:
                    tr.dma(KTd[0:NKC, :, g0:g0 + TT].rearrange("c p s -> p c s"),
                           view(kst[:, 0:1], [(TT, NKC), (1, TT)]), kst, False)
                if "v" not in SK2:
                    tr.dma(Vd[g0:g0 + TT, 0:NV].rearrange("(j p) d -> p j d", p=128),
                           view(vst[:, 0:1], [(NV, 4), (1, NV)]), vst, False)
            ph.close()

        def attn_ac(ph, seq, K2, Vg, qchunk, window, esink, es_cols, bufs):
            t0, S = seq
            (QTs, ps_s, pts, ps_o, ps_bc, recs, bcs, OTst, cnt, pend) = bufs
            nk = S // 128
            nq = S // 512
            for qt in range(nq):
                QT = QTs[cnt[0] % len(QTs)]
                cnt[0] += 1
                tr.dma(QT[:], QTd[qchunk, :, t0 + qt * 512:t0 + (qt + 1) * 512], QT, True)
                if window:
                    kcs = [kc for kc in range(qt * 4 - 1, qt * 4 + 5) if 0 <= kc < nk]
                else:
                    kcs = list(range(nk))

                def qk(i, kc):
                    pss = ps_s[i % 2]
                    def f(e):
                        e.matmul(pss[:, 0:512], lhsT=K2[0:64, kc * 128:(kc + 1) * 128], rhs=QT[0:64, :],
                                 start=True, stop=True)
                        return e.matmul(pss[:, 512:1024], lhsT=K2[64:128, kc * 128:(kc + 1) * 128],
                                        rhs=QT[64:128, :], start=True, stop=True)
                    tr.op("pe", f, [K2, QT], [pss])

                def expo(i, kc):
                    pss = ps_s[i % 2]
                    pt = pts[i % 3]
                    tr.op("act", lambda e: e.activation(out=pt[:], in_=pss[:], func=AF.Exp, scale=SCALE),
                          [pss], [pt])
                    if window:
                        jm = kc - qt * 4 + 1
                        tr.op("dve", lambda e: e.tensor_tensor(
                            out=view(pt[:, 0:1], [(512, 2), (1, 512)]), in0=view(pt[:, 0:1], [(512, 2), (1, 512)]),
                            in1=view(mask_b[:, jm * 512:jm * 512 + 1], [(0, 2), (1, 512)]), op=ALU.mult),
                            [pt, mask_b], [pt])

                def pv(i, kc):
                    pt = pts[i % 3]
                    def f(e):
                        e.matmul(ps_o[0][0:65, :], lhsT=Vg[:, kc * 65:(kc + 1) * 65], rhs=pt[:, 0:512],
                                 start=(i == 0), stop=(i == len(kcs) - 1))
                        return e.matmul(ps_o[1][0:65, :], lhsT=Vg[:, kc * 65:(kc + 1) * 65], rhs=pt[:, 512:1024],
                                        start=(i == 0), stop=(i == len(kcs) - 1))
                    tr.op("pe", f, [Vg, pt], [ps_o[0], ps_o[1]])

                for i, kc in enumerate(kcs):
                    qk(i, kc)
                    expo(i, kc)
                    if i == 1 and pend:
                        pend.pop(0)()
                    if i > 0:
                        pv(i - 1, kcs[i - 1])
                pv(len(kcs) - 1, kcs[-1])
                for hh in range(2):
                    po = ps_o[hh]
                    rec = recs[hh]
                    if esink is not None:
                        col = es_cols[hh]
                        tr.op("act", lambda e: e.activation(out=rec[64:65, :], in_=po[64:65, :], func=AF.Ln,
                                                            bias=esink[64:65, col:col + 1]), [po, esink], [rec])
                    else:
                        tr.op("act", lambda e: e.activation(out=rec[64:65, :], in_=po[64:65, :], func=AF.Ln),
                              [po], [rec])
                    tr.op("act", lambda e: e.activation(out=rec[64:65, :], in_=rec[64:65, :], func=AF.Exp,
                                                        scale=-1.0), [rec], [rec])

                def fin(qt=qt):
                    for hh in range(2):
                        po = ps_o[hh]
                        rec = recs[hh]
                        tr.op("pe", lambda e: e.matmul(ps_bc[0:64, :], lhsT=ones_f[64:65, 0:64], rhs=rec[64:65, :],
                                                       start=True, stop=True), [ones_f, rec], [ps_bc])
                        tr.op("act", lambda e: e.activation(out=bcs[0:64, :], in_=ps_bc[0:64, :], func=AF.Copy),
                              [ps_bc], [bcs])
                        ot = OTst[cnt[1] % len(OTst)]
                        cnt[1] += 1
                        tr.op("dve", lambda e: e.tensor_tensor(out=ot[0:64, :], in0=po[0:64, :], in1=bcs[0:64, :],
                                                               op=ALU.mult), [po, bcs], [ot])
                        tr.dma(OTd[qchunk, hh * 64:(hh + 1) * 64, t0 + qt * 512:t0 + (qt + 1) * 512], ot[0:64, :],
                               ot, False)
                pend.append(fin)

        def alloc_ac(ph):
            QTs = [ph.sb([128, 512], BF16, dma=True, name="QT") for _ in range(3)]
            ps_s = [ph.ps([128, 1024], F32, name="ps_s") for _ in range(2)]
            pts = [ph.sb([128, 1024], BF16, name="pt") for _ in range(3)]
            ps_o = [ph.ps([128, 512], F32, name="ps_o") for _ in range(2)]
            ps_bc = ph.ps([128, 512], F32, name="ps_bc")
            recs = [ph.sb([128, 512], F32, name="rec") for _ in range(2)]
            bcs = ph.sb([128, 512], F32, name="bcs")
            OTst = [ph.sb([128, 512], BF16, dma=True, name="OTst") for _ in range(4)]
            return (QTs, ps_s, pts, ps_o, ps_bc, recs, bcs, OTst, [0, 0], [])

        def load_kv_ac(seq, K2, Vg, kchunk, khalf, vcol):
            t0, S = seq
            nk = S // 128
            for half in range(2):
                tr.dma(K2[half * 64:(half + 1) * 64, 0:S], KTd[kchunk, khalf * 64:(khalf + 1) * 64, t0:t0 + S], K2, True)
            tr.dma(view(Vg[:, 0:1], [(65, nk), (1, 64)]),
                   Vd[t0:t0 + S, vcol:vcol + 64].rearrange("(k p) d -> p k d", p=128), Vg, True)
            tr.op("pool", lambda e: e.memset(view(Vg[:, 64:65], [(65, nk), (1, 1)]), 1.0), [], [Vg])

        def phase_p2_even_a(l):
            ph = Phase(tr, nc)
            Smax = max(Sp, Ss)
            K2 = ph.sb([128, Smax], BF16, dma=True, name="K2")
            Vg = ph.sb([128, (Smax // 128) * 65], BF16, dma=True, name="Vg")
            bufs = alloc_ac(ph)
            for seq in seqs:
                for g in range(2):
                    load_kv_ac(seq, K2, Vg, 0, g, g * 64)
                    for pair in range(2):
                        attn_ac(ph, seq, K2, Vg, g * 2 + pair, False, None, None, bufs)
            while bufs[9]:
                bufs[9].pop(0)()
            ph.close()

        def phase_p2_odd(l):
            li = l // 2
            ph = Phase(tr, nc)
            Smax = max(Sp, Ss)
            K2 = ph.sb([128, Smax], BF16, dma=True, name="K2")
            Vg = ph.sb([128, (Smax // 128) * 65], BF16, dma=True, name="Vg")
            esink = ph.sb([128, 16], F32, dma=True, name="esink")
            tr.dma(esink[64:65, :], sinkc[li:li + 1, :], esink, True)
            tr.op("act", lambda e: e.activation(out=esink[64:65, :], in_=esink[64:65, :], func=AF.Exp),
                  [esink], [esink])
            bufs = alloc_ac(ph)
            for seq in seqs:
                for hk in range(4):
                    load_kv_ac(seq, K2, Vg, hk // 2, hk % 2, hk * 64)
                    for pair in range(2):
                        qc = hk * 2 + pair
                        attn_ac(ph, seq, K2, Vg, qc, True, esink, (2 * qc, 2 * qc + 1), bufs)
            while bufs[9]:
                bufs[9].pop(0)()
            ph.close()

        def phase_p2_even_b(l):
            li = l // 2
            c_out = 1.0 - lam_init_of(l)
            ph = Phase(tr, nc)
            Smax = max(Sp, Ss)
            K2 = ph.sb([128, Smax], BF16, dma=True, name="KB")
            VB = ph.sb([128, Smax], BF16, dma=True, name="VB")
            QTs = [ph.sb([128, 512], BF16, dma=True, name="QT") for _ in range(3)]
            ps_s = [ph.ps([128, 1024], F32, name="ps_s") for _ in range(2)]
            pts = [ph.sb([128, 1024], BF16, name="pt") for _ in range(3)]
            ps_o = [ph.ps([128, 512], F32, name="ps_o") for _ in range(2)]
            ps_d = [ph.ps([128, 512], F32, name="ps_d") for _ in range(2)]
            r0 = ph.sb([128, 512], F32)
            r1 = ph.sb([128, 512], F32)
            fa = ph.sb([128, 512], F32)
            fb = ph.sb([128, 512], F32)
            fo = ph.sb([128, 512], F32)
            fsq = ph.sb([128, 512], F32)
            frs = ph.sb([128, 512], F32)
            OTst = [ph.sb([128, 512], BF16, dma=True, name="OTst") for _ in range(2)]
            dl = ph.sb([128, 256], F32, dma=True)
            tr.dma(dl[:], dlam[li], dl, True)
            pr = ph.sb([128, 128], F32)
            sm = ph.sb([128, 2], F32)
            ex = ph.sb([128, 2], F32)
            nlam = ph.sb([128, 1], F32)
            tr.op("dve", lambda e: e.tensor_tensor(out=view(pr[:, 0:1], [(64, 2), (1, 64)]),
                                                   in0=view(dl[:, 0:1], [(128, 2), (1, 64)]),
                                                   in1=view(dl[:, 64:65], [(128, 2), (1, 64)]), op=ALU.mult), [dl], [pr])
            tr.op("dve", lambda e: e.tensor_reduce(out=sm[:], in_=view(pr[:, 0:1], [(64, 2), (1, 64)]), axis=AX.X,
                                                   op=ALU.add), [pr], [sm])
            tr.op("act", lambda e: e.activation(out=ex[:], in_=sm[:], func=AF.Exp), [sm], [ex])
            tr.op("dve", lambda e: e.tensor_tensor(out=nlam[:], in0=ex[:, 1:2], in1=ex[:, 0:1], op=ALU.subtract),
                  [ex], [nlam])
            tr.op("dve", lambda e: e.tensor_scalar(out=nlam[:], in0=nlam[:], scalar1=-lam_init_of(l), scalar2=None,
                                                   op0=ALU.add), [nlam], [nlam])
            qcnt = 0
            ocntB = [0]
            pendB = []
            accA = ph.sb([128, 512], F32, name="accA")
            accB = ph.sb([128, 512], F32, name="accB")
            accB2 = Buf(accB.t)
            for (t0, S) in seqs:
                nk = S // 128
                nq = S // 512
                for h in range(4):
                    tr.dma(K2[:, 0:S], KTd[1 + h, :, t0:t0 + S], K2, True)
                    tr.dma(view(VB[:, 0:1], [(128, nk), (1, 128)]),
                           Vd[t0:t0 + S, 128 + h * 128:128 + (h + 1) * 128].rearrange("(k p) d -> p k d", p=128),
                           VB, True)
                    for qt in range(nq):
                        QT = QTs[qcnt % 3]
                        qcnt += 1
                        tr.dma(QT[:], QTd[4 + h, :, t0 + qt * 512:t0 + (qt + 1) * 512], QT, True)

                        def qk(i):
                            pss = ps_s[i % 2]
                            def f(e):
                                e.matmul(pss[:, 0:512], lhsT=K2[0:64, i * 128:(i + 1) * 128], rhs=QT[0:64, :],
                                         start=True, stop=True)
                                return e.matmul(pss[:, 512:1024], lhsT=K2[64:128, i * 128:(i + 1) * 128],
                                                rhs=QT[64:128, :], start=True, stop=True)
                            tr.op("pe", f, [K2, QT], [pss])

                        def expo(i):
                            pss = ps_s[i % 2]
                            pt = pts[i % 3]
                            tr.op("act", lambda e: e.activation(out=pt[:], in_=pss[:], func=AF.Exp, scale=SCALE),
                                  [pss], [pt])

                        def pv(i):
                            pt = pts[i % 3]
                            st, sp_ = (i == 0), (i == nk - 1)
                            def f(e):
                                e.matmul(ps_o[0][:, :], lhsT=VB[:, i * 128:(i + 1) * 128], rhs=pt[:, 0:512],
                                         start=st, stop=sp_)
                                return e.matmul(ps_o[1][:, :], lhsT=VB[:, i * 128:(i + 1) * 128],
                                                rhs=pt[:, 512:1024], start=st, stop=sp_)
                            tr.op("pe", f, [VB, pt], [ps_o[0], ps_o[1]])

                        def dacc(i):
                            pt = pts[i % 3]
                            if i == 0:
                                tr.op("dve", lambda e: e.tensor_copy(out=accA[:], in_=pt[:, 0:512]), [pt], [accA])
                                tr.op("dve", lambda e: e.tensor_copy(out=accB[:, 0:256], in_=pt[:, 512:768]),
                                      [pt], [accB])
                                tr.op("pool", lambda e: e.tensor_copy(out=accB[:, 256:512], in_=pt[:, 768:1024]),
                                      [pt], [accB2])
                            else:
                                tr.op("dve", lambda e: e.tensor_tensor(out=accA[:], in0=accA[:], in1=pt[:, 0:512],
                                                                       op=ALU.add), [pt, accA], [accA])
                                tr.op("dve", lambda e: e.tensor_tensor(out=accB[:, 0:256], in0=accB[:, 0:256],
                                                                       in1=pt[:, 512:768], op=ALU.add),
                                      [pt, accB], [accB])
                                tr.op("pool", lambda e: e.tensor_tensor(out=accB[:, 256:512], in0=accB[:, 256:512],
                                                                        in1=pt[:, 768:1024], op=ALU.add),
                                      [pt, accB2], [accB2])

                        for i in range(nk):
                            qk(i)
                            expo(i)
                            dacc(i)
                            if pendB and i == pendB[0][0]:
                                pendB.pop(0)[1]()
                            if i > 0:
                                pv(i - 1)
                        pv(nk - 1)

                        tr.op("pe", lambda e: e.matmul(ps_d[0][:, :], lhsT=ones_f[:, :], rhs=accA[:], start=True,
                                                       stop=True), [ones_f, accA], [ps_d[0]])
                        tr.op("pe", lambda e: e.matmul(ps_d[1][:, :], lhsT=ones_f[:, :], rhs=accB[:], start=True,
                                                       stop=True), [ones_f, accB, accB2], [ps_d[1]])

                        def fin1():
                            for rr, pd in ((r0, ps_d[0]), (r1, ps_d[1])):
                                tr.op("act", lambda e: e.activation(out=rr[:], in_=pd[:, :], func=AF.Ln), [pd], [rr])
                                tr.op("act", lambda e: e.activation(out=rr[:], in_=rr[:], func=AF.Exp, scale=-1.0),
                                      [rr], [rr])
                            tr.op("dve", lambda e: e.tensor_tensor(out=fa[:], in0=ps_o[0][:, :], in1=r0[:],
                                                                   op=ALU.mult), [ps_o[0], r0], [fa])
                            tr.op("dve", lambda e: e.tensor_tensor(out=fb[:], in0=ps_o[1][:, :], in1=r1[:],
                                                                   op=ALU.mult), [ps_o[1], r1], [fb])
                            tr.op("dve", lambda e: e.scalar_tensor_tensor(out=fo[:], in0=fb[:], scalar=nlam[:, 0:1],
                                                                          in1=fa[:], op0=ALU.mult, op1=ALU.add),
                                  [fb, fa, nlam], [fo])
                            tr.op("pool", lambda e: e.tensor_tensor(out=fsq[:], in0=fo[:], in1=fo[:], op=ALU.mult),
                                  [fo], [fsq])

                        def fin2(h=h, t0=t0, qt=qt):
                            tr.op("pe", lambda e: e.matmul(ps_d[0][:, :], lhsT=ones_f[:, :], rhs=fsq[:], start=True,
                                                           stop=True), [ones_f, fsq], [ps_d[0]])
                            rstd_from_ss(ps_d[0], ps_d[0][:, :], frs, frs[:], 1.0 / 128)
                            ot = OTst[ocntB[0] % 2]
                            ocntB[0] += 1
                            tr.op("dve", lambda e: e.scalar_tensor_tensor(out=ot[:], in0=fo[:], scalar=c_out,
                                                                          in1=frs[:], op0=ALU.mult, op1=ALU.mult),
                                  [fo, frs], [ot])
                            tr.dma(OTd[4 + h, :, t0 + qt * 512:t0 + (qt + 1) * 512], ot[:], ot, False)
                        pendB.append((1, fin1))
                        pendB.append((3, fin2))
            while pendB:
                pendB.pop(0)[1]()
            ph.close()

        def phase_p3a(l):
            even = (l % 2 == 0)
            li = l // 2
            TT = 512
            ph = Phase(tr, nc)
            Wb = ph.sb([128, 8 * D], BF16, name="Wout")
            load_weight(ph, Wb, (w_out_even if even else w_out_odd)[li], 8, D, nch=1024)
            gpost = ph.sb([128, D], F32, dma=True)
            tr.dma(gpost[:], gpost_mix[l], gpost, True)
            xts = [ph.sb([128, 4 * D], F32, dma=True, name="xt") for _ in range(2)]
            OTs = [ph.sb([128, 8 * TT], BF16, dma=True, name="OT") for _ in range(2)]
            junk = ph.sb([128, D], BF16)
            tmp = ph.sb([128, D], F32)
            ss2 = ph.sb([128, 4], F32)
            rstd2 = ph.sb([128, 4], F32)
            ps_m = [ph.ps([128, 1024], F32, name="ps_m") for _ in range(2)]
            tiles = [t0 + i * TT for (t0, S) in seqs for i in range(S // TT)]

            def issue_loads(ti):
                g0 = tiles[ti]
                xt = xts[ti % 2]
                tr.dma(view(xt[:, 0:1], [(D, 4), (1, D)]),
                       x_rows(l == 0, g0, TT).rearrange("(j p) d -> p j d", p=128), xt, True)
                ot = OTs[ti % 2]
                tr.dma(view(ot[:, 0:1], [(TT, 8), (1, TT)]), OTd[:, :, g0:g0 + TT].rearrange("c p s -> p c s"),
                       ot, True)

            issue_loads(0)
            k = 0
            for ti in range(len(tiles)):
                g0 = tiles[ti]
                if ti + 1 < len(tiles):
                    issue_loads(ti + 1)
                xt = xts[ti % 2]
                ot = OTs[ti % 2]
                for j in range(4):
                    pm = ps_m[k % 2]
                    k += 1
                    def f(e):
                        r = None
                        for half in range(2):
                            for c in range(8):
                                r = e.matmul(pm[:, half * 512:(half + 1) * 512],
                                             lhsT=ot[:, c * TT + j * 128:c * TT + (j + 1) * 128],
                                             rhs=Wb[:, c * D + half * 512:c * D + (half + 1) * 512],
                                             start=(c == 0), stop=(c == 7))
                        return r
                    tr.op("pe", f, [ot, Wb], [pm])
                    post_residual(pm, xt, j, ss2, rstd2, junk, tmp, gpost)
                tr.dma(x_rows(False, g0, TT).rearrange("(j p) d -> p j d", p=128),
                       view(xt[:, 0:1], [(D, 4), (1, D)]), xt, False)
            ph.close()

        def phase_p3b(l):
            TT = 256
            NJ = TT // 128
            ph = Phase(tr, nc)
            Wgu = ph.sb([128, 8 * 2 * FFN_H], BF16, name="Wgu")
            Wd = ph.sb([128, NJH * D], BF16, name="Wd")
            gT = ph.sb([128, 8], F32, dma=True)
            tr.dma(gT[:], gpreT_ffn[l], gT, True)
            load_weight(ph, Wgu, w_gate_up[l], 8, 2 * FFN_H, gT=gT)
            load_weight(ph, Wd, w_down[l], NJH, D, nch=1024)
            gpost = ph.sb([128, D], F32, dma=True)
            tr.dma(gpost[:], gpost_ffn[l], gpost, True)
            xts = [ph.sb([128, NJ * D], F32, dma=True, name="xt") for _ in range(2)]
            hTs = [ph.sb([128, 8 * TT], BF16, name="hT") for _ in range(2)]
            aT = ph.sb([128, NJH * TT], BF16, name="aT")
            hb = ph.sb([128, D], BF16)
            junk = ph.sb([128, D], BF16)
            tmp = ph.sb([128, D], F32)
            sg = [ph.sb([128, TT], F32, name="sg") for _ in range(2)]
            ss = ph.sb([128, 4], F32)
            rstd = ph.sb([128, 4], F32)
            ss2 = ph.sb([128, 4], F32)
            rstd2 = ph.sb([128, 4], F32)
            psT = ph.ps([128, 1024], BF16, name="psT")
            ps_gu = [ph.ps([128, 512], F32, name="ps_gu") for _ in range(3)]
            ps_y = [ph.ps([128, 1024], F32, name="ps_y") for _ in range(2)]
            tiles = [t0 + i * TT for (t0, S) in seqs for i in range(S // TT)]

            def issue_loads(ti):
                g0 = tiles[ti]
                xt = xts[ti % 2]
                tr.dma(view(xt[:, 0:1], [(D, NJ), (1, D)]),
                       x_rows(False, g0, TT).rearrange("(j p) d -> p j d", p=128), xt, True)

            def chain(ti_):
                for j_ in range(NJ):
                    norm_transpose(xts[ti_ % 2], j_, ss, rstd, junk, hb, psT, hTs[ti_ % 2], TT)

            issue_loads(0)
            if len(tiles) > 1:
                issue_loads(1)
            chain(0)
            kg = 0
            ky = 0
            for ti in range(len(tiles)):
                g0 = tiles[ti]
                xt = xts[ti % 2]
                hT = hTs[ti % 2]
                for jh in range(NJH):
                    pg = ps_gu[kg % 3]
                    sgb = sg[kg % 2]
                    kg += 1
                    def f(e):
                        r = None
                        for gu in range(2):
                            col = gu * FFN_H + jh * 128
                            for c in range(8):
                                r = e.matmul(pg[:, gu * TT:(gu + 1) * TT],
                                             lhsT=Wgu[:, c * 2 * FFN_H + col:c * 2 * FFN_H + col + 128],
                                             rhs=hT[:, c * TT:(c + 1) * TT], start=(c == 0), stop=(c == 7))
                        return r
                    tr.op("pe", f, [Wgu, hT], [pg])
                    tr.op("act", lambda e: e.activation(out=sgb[:], in_=pg[:, 0:TT], func=AF.Silu), [pg], [sgb])
                    tr.op("dve", lambda e: e.tensor_tensor(out=aT[:, jh * TT:(jh + 1) * TT], in0=pg[:, TT:2 * TT],
                                                           in1=sgb[:], op=ALU.mult), [pg, sgb], [aT])
                if ti + 1 < len(tiles):
                    chain(ti + 1)
                for j in range(NJ):
                    py = ps_y[ky % 2]
                    ky += 1
                    def f2(e):
                        r = None
                        for half in range(2):
                            for jh in range(NJH):
                                r = e.matmul(py[:, half * 512:(half + 1) * 512],
                                             lhsT=aT[:, jh * TT + j * 128:jh * TT + (j + 1) * 128],
                                             rhs=Wd[:, jh * D + half * 512:jh * D + (half + 1) * 512],
                                             start=(jh == 0), stop=(jh == NJH - 1))
                        return r
                    tr.op("pe", f2, [aT, Wd], [py])
                    post_residual(py, xt, j, ss2, rstd2, junk, tmp, gpost)
                tr.dma(x_rows(False, g0, TT).rearrange("(j p) d -> p j d", p=128),
                       view(xt[:, 0:1], [(D, NJ), (1, D)]), xt, False)
                if ti + 2 < len(tiles):
                    issue_loads(ti + 2)
            ph.close()

        tr.barrier()
        import os
        dbg = os.environ.get("KPHASES")
        for l in range(depth):
            if os.environ.get("KLAYERS") and str(l) not in os.environ["KLAYERS"]:
                continue
            dbg = os.environ.get("KPH%d" % l, os.environ.get("KPHASES"))
            if dbg is None or "1" in dbg:
                phase_p1(l)
            if l % 2 == 0:
                if dbg is None or "a" in dbg:
                    phase_p2_even_a(l)
                if dbg is None or "b" in dbg:
                    phase_p2_even_b(l)
            else:
                if dbg is None or "c" in dbg:
                    phase_p2_odd(l)
            if dbg is None or "3" in dbg:
                phase_p3a(l)
            if dbg is None or "4" in dbg:
                phase_p3b(l)
        gph.close()
    return nc


def _rope_table(S):
    t = np.arange(S)
    inv_ax = (10000.0 ** (-np.arange(0, 32, 2, dtype=np.float32) / 32)).astype(np.float32)
    row = (t // 64).astype(np.float32)[:, None] * inv_ax[None, :]
    col = (t % 64).astype(np.float32)[:, None] * inv_ax[None, :]
    inv_p = (500000.0 ** (-np.arange(0, 16, 2, dtype=np.float32) / 16)).astype(np.float32)
    ang = t.astype(np.float32)[:, None] * inv_p[None, :]
    tab = np.zeros((S, 160), np.float32)
    rc, rs, cc, cs = np.cos(row), np.sin(row), np.cos(col), np.sin(col)
    tab[:, 0:16] = rc; tab[:, 16:32] = rc; tab[:, 32:48] = cc; tab[:, 48:64] = cc
    tab[:, 64:80] = -rs; tab[:, 80:96] = rs; tab[:, 96:112] = -cs; tab[:, 112:128] = cs
    pc, ps = np.cos(ang), np.sin(ang)
    tab[:, 128:136] = pc; tab[:, 136:144] = pc
    tab[:, 144:152] = -ps; tab[:, 152:160] = ps
    return tab


def _mask_const():
    k = np.arange(128)[:, None]
    q = np.arange(512)[None, :]
    m = np.zeros((128, 6 * 512), np.float32)
    for jm in range(6):
        j = jm - 1
        m[:, jm * 512:(jm + 1) * 512] = (np.abs(q - (j * 128 + k)) <= 128).astype(np.float32)
    return m


_CACHE = {}


def run(inputs, Sp, Ss, n_cores, depth=DEPTH, trace=False):
    key = (Sp, Ss, depth)
    if key not in _CACHE:
        _CACHE[key] = build_program(Sp, Ss, depth)
    nc = _CACHE[key]
    f = lambda a: np.ascontiguousarray(np.asarray(a, dtype=np.float32))
    xpf, xsf = f(inputs["x_prompt"]), f(inputs["x_sample"])
    nbs = xsf.shape[0]
    qk = f(inputs["qk_norm_a"])
    gqk = np.stack([np.broadcast_to(np.concatenate([np.tile(qk[i, 0], 8), np.tile(qk[i, 1], 2)])[None, :], (128, 640))
                    for i in range(2)])
    dl = f(inputs["diff_lambda"]).reshape(2, 1, 256)
    shared = {
        "w_in_even": f(inputs["w_in_even"]), "w_out_even": f(inputs["w_out_even"]),
        "w_in_odd": f(inputs["w_in_odd"]), "w_out_odd": f(inputs["w_out_odd"]),
        "w_gate_up": f(inputs["w_gate_up"]), "w_down": f(inputs["w_down"]),
        "gpreT_mix": f(f(inputs["norm_mix_pre"]).reshape(4, 8, 128).transpose(0, 2, 1)),
        "gpreT_ffn": f(f(inputs["norm_ffn_pre"]).reshape(4, 8, 128).transpose(0, 2, 1)),
        "gpost_mix": f(np.broadcast_to(f(inputs["norm_mix_post"])[:, None, :], (4, 128, D))),
        "gpost_ffn": f(np.broadcast_to(f(inputs["norm_ffn_post"])[:, None, :], (4, 128, D))),
        "gqk": f(gqk), "dlam": f(np.broadcast_to(dl, (2, 128, 256))),
        "sinkc": f(inputs["sink_c"]),
        "tab": _rope_table(max(Sp, Ss)), "maskc": _mask_const(), "identc": np.eye(128, dtype=np.float32),
    }
    in_maps = []
    for i in range(n_cores):
        m = dict(shared)
        m["xp"] = xpf[i]
        m["xs"] = xsf[i % nbs]
        in_maps.append(m)
    res = run_bass_kernel_spmd(nc, in_maps, core_ids=list(range(n_cores)), trace=trace)
    yp = np.stack([res.results[i]["yp"] for i in range(n_cores)]).astype(np.float32)
    ysm = np.stack([res.results[i]["ys"] for i in range(nbs)]).astype(np.float32)
    return (yp, ysm), res


def kernel(**inputs):
    (yp, ysm), _ = run(inputs, 8192, 4096, NCORES)
    return (yp, ysm)
```
